# Optimizing a Trainium2 kernel written in Bass

```python
import jax, jax.numpy as jnp
from jax import lax
import numpy as np

D_MODEL = 2048
BATCH = 1
SEQ = 16384
DEPTH = 2

N_MIXERS = 2
N_MLSTM_LAYERS = (DEPTH + 1) // 2
N_MLA_LAYERS = DEPTH // 2
D_FF = 5632
RMS_EPS = 1e-6

MLSTM_HEADS = 4
MLSTM_DQK = D_MODEL // (2 * MLSTM_HEADS)
MLSTM_DV = D_MODEL // MLSTM_HEADS
MLSTM_CHUNK = 64
MLSTM_SPLITS = [MLSTM_HEADS * MLSTM_DQK,
                2 * MLSTM_HEADS * MLSTM_DQK,
                2 * MLSTM_HEADS * MLSTM_DQK + MLSTM_HEADS * MLSTM_DV,
                2 * MLSTM_HEADS * MLSTM_DQK + 2 * MLSTM_HEADS * MLSTM_DV,
                2 * MLSTM_HEADS * MLSTM_DQK + 2 * MLSTM_HEADS * MLSTM_DV + MLSTM_HEADS]
MLSTM_IN = MLSTM_SPLITS[-1] + MLSTM_HEADS

MLA_HEADS = 16
MLA_Q_RANK = 512
MLA_KV_RANK = 512
MLA_NOPE = 128
MLA_ROPE = 64
MLA_V = 128
MLA_QK = MLA_NOPE + MLA_ROPE
MLA_IN = MLA_Q_RANK + MLA_KV_RANK + MLA_ROPE
ROPE_THETA = 10000.0
Q_BLOCK = 128

kernel_name = 'hybrid_mlstm_mla_macaron'


def rms_norm(x, g):
    xf = x.astype(jnp.float32)
    y = xf * lax.rsqrt(jnp.mean(xf * xf, axis=-1, keepdims=True) + RMS_EPS)
    return (y * g.astype(jnp.float32)).astype(x.dtype)


def swiglu_ffn(x, w_gate_up, w_down):
    gate, up = jnp.split(x @ w_gate_up, 2, axis=-1)
    return (jax.nn.silu(gate) * up) @ w_down


def mlstm_mixer(x, w_in, gate_bias, head_norm, w_out):
    B, S, _ = x.shape
    H, dqk, dv, L = MLSTM_HEADS, MLSTM_DQK, MLSTM_DV, MLSTM_CHUNK
    nc = S // L
    f32 = jnp.float32
    q, k, v, o, ig, fg = jnp.split(x @ w_in, MLSTM_SPLITS, axis=-1)

    def to_chunks(t, d):
        return t.reshape(B, nc, L, H, d).transpose(1, 0, 3, 2, 4).astype(f32)

    def gate_chunks(t):
        return t.astype(f32).reshape(B, nc, L, H).transpose(1, 0, 3, 2)

    qc = to_chunks(q, dqk) * (dqk ** -0.5)
    kc = to_chunks(k, dqk)
    vc = to_chunks(v, dv)
    igc = gate_chunks(ig + gate_bias[0])
    lfc = jax.nn.log_sigmoid(gate_chunks(fg + gate_bias[1]))
    causal = jnp.tril(jnp.ones((L, L), dtype=bool))

    def step(carry, inp):
        C, n, m = carry
        qb, kb, vb, ib, lfb = inp
        b = jnp.cumsum(lfb, axis=-1)
        dmat = jnp.where(causal, b[..., :, None] - b[..., None, :] + ib[..., None, :], -jnp.inf)
        inter = b + m[..., None]
        m_t = jnp.maximum(inter, jnp.max(dmat, axis=-1))
        w = jnp.einsum('bhtd,bhsd->bhts', qb, kb) * jnp.exp(dmat - m_t[..., None])
        a = jnp.exp(inter - m_t)
        num = a[..., None] * jnp.einsum('bhtd,bhvd->bhtv', qb, C) + jnp.einsum('bhts,bhsv->bhtv', w, vb)
        den = a * jnp.einsum('bhtd,bhd->bht', qb, n) + jnp.sum(w, axis=-1)
        h = num / jnp.maximum(jnp.abs(den), jnp.exp(-m_t))[..., None]
        b_last = b[..., -1]
        g = b_last[..., None] - b + ib
        m_new = jnp.maximum(b_last + m, jnp.max(g, axis=-1))
        decay = jnp.exp(b_last + m - m_new)
        wk = jnp.exp(g - m_new[..., None])
        C = decay[..., None, None] * C + jnp.einsum('bhs,bhsv,bhsd->bhvd', wk, vb, kb)
        n = decay[..., None] * n + jnp.einsum('bhs,bhsd->bhd', wk, kb)
        return (C, n, m_new), h

    init = (jnp.zeros((B, H, dv, dqk), f32), jnp.zeros((B, H, dqk), f32), jnp.zeros((B, H), f32))
    _, hc = lax.scan(step, init, (qc, kc, vc, igc, lfc))
    h = hc.transpose(1, 0, 3, 2, 4).reshape(B, S, H, dv)
    h = rms_norm(h, head_norm.reshape(H, dv)).reshape(B, S, H * dv).astype(x.dtype)
    return (jax.nn.sigmoid(o) * h) @ w_out


def apply_rope(t, cos, sin):
    t1, t2 = jnp.split(t.astype(jnp.float32), 2, axis=-1)
    return jnp.concatenate([t1 * cos - t2 * sin, t2 * cos + t1 * sin], axis=-1).astype(t.dtype)


def mla_mixer(x, positions, w_in, q_norm, kv_norm, w_uq, w_ukv, qk_norm, w_out):
    B, S, _ = x.shape
    H = MLA_HEADS
    c_q, c_kv, k_rope = jnp.split(x @ w_in, [MLA_Q_RANK, MLA_Q_RANK + MLA_KV_RANK], axis=-1)
    q = (rms_norm(c_q, q_norm) @ w_uq).reshape(B, S, H, MLA_QK)
    kv = (rms_norm(c_kv, kv_norm) @ w_ukv).reshape(B, S, H, MLA_NOPE + MLA_V)
    q_nope, q_rope = jnp.split(q, [MLA_NOPE], axis=-1)
    k_nope, v = jnp.split(kv, [MLA_NOPE], axis=-1)
    q_nope = rms_norm(q_nope, qk_norm[0, :MLA_NOPE])
    q_rope = rms_norm(q_rope, qk_norm[0, MLA_NOPE:])
    k_nope = rms_norm(k_nope, qk_norm[1, :MLA_NOPE])
    k_rope = rms_norm(k_rope, qk_norm[1, MLA_NOPE:])
    freqs = ROPE_THETA ** (-jnp.arange(0, MLA_ROPE, 2, dtype=jnp.float32) / MLA_ROPE)
    ang = positions.astype(jnp.float32)[..., None] * freqs
    cos, sin = jnp.cos(ang), jnp.sin(ang)
    q_rope = apply_rope(q_rope, cos[:, :, None, :], sin[:, :, None, :])
    k_rope = apply_rope(k_rope, cos, sin)
    scale = MLA_QK ** -0.5
    nb = S // Q_BLOCK
    qn_blocks = (q_nope * scale).reshape(B, nb, Q_BLOCK, H, MLA_NOPE).transpose(1, 0, 2, 3, 4)
    qr_blocks = (q_rope * scale).reshape(B, nb, Q_BLOCK, H, MLA_ROPE).transpose(1, 0, 2, 3, 4)
    key_pos = jnp.arange(S)

    def attend(args):
        qn, qr, start = args
        s = (jnp.einsum('bqhd,bkhd->bhqk', qn, k_nope)
             + jnp.einsum('bqhr,bkr->bhqk', qr, k_rope)).astype(jnp.float32)
        qpos = start + jnp.arange(Q_BLOCK)
        s = jnp.where(key_pos[None, :] <= qpos[:, None], s, -jnp.inf)
        p = jax.nn.softmax(s, axis=-1).astype(v.dtype)
        return jnp.einsum('bhqk,bkhd->bqhd', p, v)

    o = lax.map(attend, (qn_blocks, qr_blocks, jnp.arange(nb) * Q_BLOCK))
    o = o.transpose(1, 0, 2, 3, 4).reshape(B, S, H * MLA_V)
    return o @ w_out


def setup_inputs(seed: int = 0) -> dict:
    key = jax.random.key(seed)
    ks = jax.random.split(key, 24)
    f32 = jnp.float32

    def w(k, shape, fan_in):
        return jax.random.normal(k, shape, f32) * (fan_in ** -0.5)

    def gain(k, shape):
        return 1.0 + 0.02 * jax.random.normal(k, shape, f32)

    NM, NA = N_MLSTM_LAYERS, N_MLA_LAYERS
    gate_bias = jnp.stack([0.1 * jax.random.normal(ks[12], (NM, MLSTM_HEADS), f32),
                           3.0 + 0.5 * jax.random.normal(ks[13], (NM, MLSTM_HEADS), f32)], axis=1)
    return {
        'x': jax.random.normal(ks[0], (BATCH, SEQ, D_MODEL), f32),
        'positions': jnp.broadcast_to(jnp.arange(SEQ, dtype=jnp.int32), (BATCH, SEQ)),
        'ffn1_norm': gain(ks[1], (DEPTH, D_MODEL)),
        'ffn1_w_gate_up': w(ks[2], (DEPTH, D_MODEL, 2 * D_FF), D_MODEL),
        'ffn1_w_down': w(ks[3], (DEPTH, D_FF, D_MODEL), D_FF),
        'mix_norm': gain(ks[4], (DEPTH, D_MODEL)),
        'ffn2_norm': gain(ks[5], (DEPTH, D_MODEL)),
        'ffn2_w_gate_up': w(ks[6], (DEPTH, D_MODEL, 2 * D_FF), D_MODEL),
        'ffn2_w_down': w(ks[7], (DEPTH, D_FF, D_MODEL), D_FF),
        'mlstm_w_in': w(ks[8], (NM, D_MODEL, MLSTM_IN), D_MODEL),
        'mlstm_gate_bias': gate_bias,
        'mlstm_head_norm': gain(ks[9], (NM, MLSTM_HEADS * MLSTM_DV)),
        'mlstm_w_out': w(ks[10], (NM, MLSTM_HEADS * MLSTM_DV, D_MODEL), MLSTM_HEADS * MLSTM_DV),
        'mla_w_in': w(ks[14], (NA, D_MODEL, MLA_IN), D_MODEL),
        'mla_q_norm': gain(ks[15], (NA, MLA_Q_RANK)),
        'mla_kv_norm': gain(ks[16], (NA, MLA_KV_RANK)),
        'mla_w_uq': w(ks[17], (NA, MLA_Q_RANK, MLA_HEADS * MLA_QK), MLA_Q_RANK),
        'mla_w_ukv': w(ks[18], (NA, MLA_KV_RANK, MLA_HEADS * (MLA_NOPE + MLA_V)), MLA_KV_RANK),
        'mla_qk_norm': gain(ks[19], (NA, 2, MLA_QK)),
        'mla_w_out': w(ks[20], (NA, MLA_HEADS * MLA_V, D_MODEL), MLA_HEADS * MLA_V),
    }


def reference(x, positions, ffn1_norm, ffn1_w_gate_up, ffn1_w_down, mix_norm,
              ffn2_norm, ffn2_w_gate_up, ffn2_w_down,
              mlstm_w_in, mlstm_gate_bias, mlstm_head_norm, mlstm_w_out,
              mla_w_in, mla_q_norm, mla_kv_norm, mla_w_uq, mla_w_ukv, mla_qk_norm, mla_w_out):
    for i in range(DEPTH):
        x = x + 0.5 * swiglu_ffn(rms_norm(x, ffn1_norm[i]), ffn1_w_gate_up[i], ffn1_w_down[i])
        h = rms_norm(x, mix_norm[i])
        j = i // N_MIXERS
        if i % N_MIXERS == 0:
            y = mlstm_mixer(h, mlstm_w_in[j], mlstm_gate_bias[j], mlstm_head_norm[j], mlstm_w_out[j])
        else:
            y = mla_mixer(h, positions, mla_w_in[j], mla_q_norm[j], mla_kv_norm[j],
                          mla_w_uq[j], mla_w_ukv[j], mla_qk_norm[j], mla_w_out[j])
        x = x + y
        x = x + 0.5 * swiglu_ffn(rms_norm(x, ffn2_norm[i]), ffn2_w_gate_up[i], ffn2_w_down[i])
    return x
```

```python
import contextlib
import numpy as np
DBG = 99
import concourse.bass as bass
import concourse.mybir as mybir
from concourse.bass_utils import run_bass_kernel_spmd

F32 = mybir.dt.float32
BF16 = mybir.dt.bfloat16
I32 = mybir.dt.int32
AF = mybir.ActivationFunctionType
ALU = mybir.AluOpType
AX = mybir.AxisListType

NCORES = 8
D = 2048
S = 16384
T = S // NCORES
KD = D // 128
DFF = 5632
NH = DFF // 128
TT = 1024
EPS = 1e-6


class Buf:
    __slots__ = ("w", "r", "name")

    def __init__(self, name=""):
        self.w = None
        self.r = {}
        self.name = name


class Slot:
    def __init__(self, sem):
        self.sem = sem
        self.expected = 0


class _Rec:
    def __init__(self):
        self.calls = []

    def __getattr__(self, name):
        def f(*a, **k):
            self.calls.append((name, a, k))
            return self
        return f


class Prog:
    ENGS = ("pe", "act", "dve", "pool", "sp")

    def __init__(self, nc, es):
        self.nc = nc
        self.es = es
        self.ops = {e: [] for e in self.ENGS}
        self.count = {e: 0 for e in self.ENGS}
        self.waited = {e: {} for e in self.ENGS}
        self.es_base = es
        self.esem = {e: es.enter_context(nc.semaphore("s_" + e)) for e in self.ENGS}
        self.nsem = len(self.ENGS)
        self.slots = set()
        self.scopes = []
        self.free_slots = []
        self.scope_slots = []

    def sem(self, name):
        self.nsem += 1
        self.uid = getattr(self, "uid", 0) + 1
        return self.es_base.enter_context(self.nc.semaphore(f"{name}_{self.uid}"))

    def slot(self, name):
        if self.free_slots:
            sl = self.free_slots.pop()
        else:
            sl = Slot(self.sem(name))
        if self.scope_slots:
            self.scope_slots[-1].append(sl)
        return sl

    def sbuf(self, name, shape, dt):
        self.uid = getattr(self, "uid", 0) + 1
        return self.es.enter_context(self.nc.sbuf_tensor(f"sb{self.uid}_{name}", list(shape), dt))

    def psum(self, name, shape, dt):
        return self.es.enter_context(self.nc.psum_tensor(name, list(shape), dt))

    def _need(self, eng, waits, ticket):
        if ticket is None:
            return
        sem, val = ticket
        if eng == "pe" and sem is self.esem["pe"]:
            return
        w = self.waited[eng]
        if w.get(id(sem), 0) < val:
            w[id(sem)] = val
            waits.append((sem, val))

    def _deps(self, eng, reads, writes):
        waits = []
        for b in reads:
            self._need(eng, waits, b.w)
        for b in writes:
            self._need(eng, waits, b.w)
            for sem_id, (sem, val) in b.r.items():
                self._need(eng, waits, (sem, val))
        return waits

    def _mark(self, tk, reads, writes):
        sem, val = tk
        for b in reads:
            cur = b.r.get(id(sem))
            if cur is None or cur[1] < val:
                b.r[id(sem)] = (sem, val)
        for b in writes:
            b.w = tk
            b.r = {}

    def op(self, eng, fn, reads=(), writes=()):
        waits = self._deps(eng, reads, writes)
        self.count[eng] += 1
        tk = (self.esem[eng], self.count[eng])
        self._mark(tk, reads, writes)
        rec = _Rec()
        fn(rec)
        self.ops[eng].append((waits, rec.calls, (self.esem[eng], 1)))
        return tk

    def dma(self, eng, out, in_, slot, reads=(), writes=(), **kw):
        waits = self._deps(eng, reads, writes)
        slot.expected += 16
        tk = (slot.sem, slot.expected)
        self._mark(tk, reads, writes)
        self.ops[eng].append((waits, [("dma_start", (), dict(out=out, in_=in_, **kw))], (slot.sem, 16)))
        self.slots.add(slot)
        return tk

    def collective(self, kind, ins, outs, slot, reads=(), writes=(), inc=1):
        waits = self._deps("pool", reads, writes)
        slot.expected += inc
        tk = (slot.sem, slot.expected)
        self._mark(tk, reads, writes)
        call = ("collective_compute", (kind, ALU.bypass), dict(replica_groups=[list(range(NCORES))], ins=list(ins),
                                                               outs=list(outs)))
        self.ops["pool"].append((waits, [call], (slot.sem, inc)))
        self.slots.add(slot)
        return tk

    def wait(self, eng, tickets):
        waits = []
        for t in tickets:
            self._need(eng, waits, t)
        if waits:
            self.ops[eng].append((waits, None, None))

    def barrier(self):
        tks = [(self.esem[e], self.count[e]) for e in self.ENGS if self.count[e] > 0]
        tks += [(sl.sem, sl.expected) for sl in self.slots if sl.expected > 0]
        for e in self.ENGS:
            self.wait(e, tks)

    def push(self):
        self.scope_slots.append([])
        self.scopes.append(self.es)
        self.es = contextlib.ExitStack()
        self.es.__enter__()

    def pop(self):
        self.barrier()
        self.es.__exit__(None, None, None)
        self.es = self.scopes.pop()
        self.free_slots.extend(self.scope_slots.pop())

    def emit(self):
        nc = self.nc

        def run(e, ops):
            for waits, fn, inc in ops:
                for sem, val in waits:
                    e.wait_ge(sem, val)
                if fn is not None:
                    ins = None
                    for name, a, k in fn:
                        ins = getattr(e, name)(*a, **k)
                    if inc is not None:
                        ins.then_inc(inc[0], inc[1])

        with nc.Block() as block:
            @block.tensor
            def _(e):
                run(e, self.ops["pe"])

            @block.scalar
            def _(e):
                run(e, self.ops["act"])

            @block.vector
            def _(e):
                run(e, self.ops["dve"])

            @block.gpsimd
            def _(e):
                run(e, self.ops["pool"])

            @block.sync
            def _(e):
                run(e, self.ops["sp"])


class TB:
    def __init__(self, t, name, slot=None):
        self.t = t
        self.b = Buf(name)
        self.slot = slot


class Res:
    pass


def alloc_common(P):
    R = Res()
    R.ps = [TB(P.psum(f"ps{i}", [128, 512], F32), f"ps{i}") for i in range(8)]
    R.ones = TB(P.sbuf("ones_bf", [128, 128], BF16), "ones")
    P.op("dve", lambda e: e.memset(R.ones.t[:], 1.0), writes=[R.ones.b])
    R.epsb = TB(P.sbuf("epsb", [128, 1], F32), "epsb")
    P.op("dve", lambda e: e.memset(R.epsb.t[:], EPS), writes=[R.epsb.b])
    R.onec = TB(P.sbuf("onec", [128, 1], F32), "onec")
    P.op("dve", lambda e: e.memset(R.onec.t[:], 1.0), writes=[R.onec.b])
    R.rot = Rot()
    return R


def alloc_ffn(P, R):
    alloc_normproj(P, R)
    alloc_ffn_rest(P, R)


def alloc_normproj(P, R, nw=2):
    R.xc = [TB(P.sbuf(f"xc{i}", [128, TT], F32), f"xc{i}", P.slot(f"sl_xc{i}")) for i in range(3)]
    R.sq = [TB(P.sbuf(f"sq{i}", [128, TT], BF16), f"sq{i}") for i in range(2)]
    R.rt = TB(P.sbuf("rtmp", [128, TT], F32), "rtmp")
    R.rstd = TB(P.sbuf("rstd", [128, TT], F32), "rstd")
    R.xn_t = P.sbuf("xn", [128, KD, TT], BF16)
    R.xn = [Buf(f"xn{k}") for k in range(KD)]
    R.wgu = [TB(P.sbuf(f"wgu{i}", [128, KD, 256], BF16), f"wgu{i}", P.slot(f"sl_wgu{i}")) for i in range(nw)]
    R.silu = [TB(P.sbuf(f"silu{i}", [128, TT], F32), f"silu{i}") for i in range(2)]


def alloc_resid(P, R):
    R.xr = [TB(P.sbuf(f"xr{i}", [128, 512], F32), f"xr{i}", P.slot(f"sl_xr{i}")) for i in range(3)]
    R.osb = [TB(P.sbuf(f"osb{i}", [128, 512], F32), f"osb{i}", P.slot(f"sl_o{i}")) for i in range(3)]


def alloc_ffn_rest(P, R):
    R.g = TB(P.sbuf("ffn_g", [128, KD], F32), "ffn_g", P.slot("sl_g"))
    alloc_resid(P, R)
    R.act_t = P.sbuf("act", [128, NH, TT], BF16)
    R.act = [Buf(f"act{c}") for c in range(NH)]
    R.wd = [TB(P.sbuf(f"wd{i}", [128, NH // 2, 256], BF16), f"wd{i}", P.slot(f"sl_wd{i}")) for i in range(2)]


def emit_norm(P, R, xin, t0, g, nxt):
    ones = R.ones
    ss = [R.ps[0], R.ps[1]]
    for k in range(KD):
        xb = nxt("xc", R.xc)
        P.dma("sp", xb.t[:], xin[k * 128:(k + 1) * 128, t0:t0 + TT], xb.slot, writes=[xb.b])
        sq = nxt("sq", R.sq)
        P.op("act", lambda e: e.activation(out=sq.t[:], in_=xb.t[:], func=AF.Square), reads=[xb.b], writes=[sq.b])

        def f(e):
            for h in range(2):
                e.matmul(ss[h].t[:], lhsT=ones.t[:], rhs=sq.t[:, h * 512:(h + 1) * 512],
                         start=(k == 0), stop=(k == KD - 1))
        P.op("pe", f, reads=[sq.b, ones.b], writes=[ss[0].b, ss[1].b])
    for h in range(2):
        P.op("act", lambda e: e.activation(out=R.rt.t[:, h * 512:(h + 1) * 512], in_=ss[h].t[:],
                                           func=AF.Sqrt, scale=1.0 / D, bias=R.epsb.t[:]),
             reads=[ss[h].b, R.epsb.b], writes=[R.rt.b])
    P.op("dve", lambda e: e.reciprocal(out=R.rstd.t[:], in_=R.rt.t[:]), reads=[R.rt.b], writes=[R.rstd.b])
    for k in range(KD):
        xb = nxt("xc", R.xc)
        P.dma("sp", xb.t[:], xin[k * 128:(k + 1) * 128, t0:t0 + TT], xb.slot, writes=[xb.b])
        P.op("dve", lambda e: e.scalar_tensor_tensor(
            out=R.xn_t[:, k, :], in0=xb.t[:], scalar=g.t[:, k:k + 1], in1=R.rstd.t[:],
            op0=ALU.mult, op1=ALU.mult), reads=[xb.b, g.b, R.rstd.b], writes=[R.xn[k]])


class Rot:
    def __init__(self):
        self.cnt = {}

    def __call__(self, name, lst):
        i = self.cnt.get(name, 0)
        self.cnt[name] = i + 1
        return lst[i % len(lst)]


def emit_ffn(P, R, xin, xout, g_dram, wgu, wd, ntok):
    ones = R.ones
    out_tk = []
    P.dma("sp", R.g.t[:], g_dram, R.g.slot, writes=[R.g.b])
    wd_v = wd.rearrange("(c p) n -> p c n", p=128)
    nxt = R.rot

    for tt in range(ntok // TT):
        t0 = tt * TT
        emit_norm(P, R, xin, t0, R.g, nxt)
        for c in range(NH):
            w = nxt("wgu", R.wgu)
            P.dma("pool", w.t[:], wgu[c], w.slot, writes=[w.b])
            base = (c % 2) * 4
            gp = [R.ps[base + 0], R.ps[base + 1]]
            up = [R.ps[base + 2], R.ps[base + 3]]
            for which, banks in ((0, gp), (1, up)):
                def f(e, w=w, banks=banks, which=which):
                    for k in range(KD):
                        for h in range(2):
                            ins = e.matmul(banks[h].t[:], lhsT=w.t[:, k, which * 128:(which + 1) * 128],
                                           rhs=R.xn_t[:, k, h * 512:(h + 1) * 512],
                                           start=(k == 0), stop=(k == KD - 1))
                    return ins
                P.op("pe", f, reads=[w.b] + R.xn, writes=[banks[0].b, banks[1].b])
            sl = nxt("silu", R.silu)
            for h in range(2):
                P.op("act", (lambda sl, h, gp: lambda e: e.activation(
                    out=sl.t[:, h * 512:(h + 1) * 512], in_=gp[h].t[:], func=AF.Silu))(sl, h, gp),
                    reads=[gp[h].b], writes=[sl.b])
            for h in range(2):
                P.op("dve", (lambda sl, h, up, c: lambda e: e.tensor_tensor(
                    out=R.act_t[:, c, h * 512:(h + 1) * 512], in0=sl.t[:, h * 512:(h + 1) * 512],
                    in1=up[h].t[:], op=ALU.mult))(sl, h, up, c),
                    reads=[sl.b, up[h].b], writes=[R.act[c]])
        HC = NH // 2
        for dg in range(D // 256):
            base = (dg % 2) * 4
            for hc in range(2):
                w = nxt("wd", R.wd)
                P.dma("pool", w.t[:], wd_v[:, hc * HC:(hc + 1) * HC, dg * 256:(dg + 1) * 256], w.slot, writes=[w.b])

                def f(e, w=w, hc=hc, base=base):
                    for cc in range(HC):
                        c = hc * HC + cc
                        for dd in range(2):
                            for h in range(2):
                                ins = e.matmul(R.ps[base + dd * 2 + h].t[:], lhsT=w.t[:, cc, dd * 128:(dd + 1) * 128],
                                               rhs=R.act_t[:, c, h * 512:(h + 1) * 512],
                                               start=(c == 0), stop=(c == NH - 1))
                    return ins
                P.op("pe", f, reads=[w.b] + R.act[hc * HC:(hc + 1) * HC],
                     writes=[R.ps[base + i].b for i in range(4)])
            for dd in range(2):
                for h in range(2):
                    row = (dg * 2 + dd) * 128
                    xr = nxt("xr", R.xr)
                    P.dma("sp", xr.t[:], xin[row:row + 128, t0 + h * 512:t0 + (h + 1) * 512], xr.slot, writes=[xr.b])
                    ob = nxt("osb", R.osb)
                    pb = R.ps[base + dd * 2 + h]
                    P.op("dve", (lambda ob, pb, xr: lambda e: e.scalar_tensor_tensor(
                        out=ob.t[:], in0=pb.t[:], scalar=0.5, in1=xr.t[:], op0=ALU.mult, op1=ALU.add))(ob, pb, xr),
                        reads=[pb.b, xr.b], writes=[ob.b])
                    tk = P.dma("sp", xout[row:row + 128, t0 + h * 512:t0 + (h + 1) * 512], ob.t[:], ob.slot,
                               reads=[ob.b])
                    out_tk.append(tk)
    return out_tk


def build_ffn_prog(ntok=T):
    nc = bass.Bass("TRN2", target_bir_lowering=False)
    xin = nc.dram_tensor("xin", [D, ntok], F32, kind="ExternalInput").ap()
    g = nc.dram_tensor("g", [128, KD], F32, kind="ExternalInput").ap()
    wgu = nc.dram_tensor("wgu", [NH, 128, KD, 256], F32, kind="ExternalInput").ap()
    wd = nc.dram_tensor("wd", [DFF, D], F32, kind="ExternalInput").ap()
    xout = nc.dram_tensor("xout", [D, ntok], F32, kind="ExternalOutput").ap()
    with contextlib.ExitStack() as es:
        P = Prog(nc, es)
        R = alloc_common(P)
        alloc_ffn(P, R)
        tks = emit_ffn(P, R, xin, xout, g, wgu, wd, ntok)
        P.wait("sp", tks)
        P.emit()
    return nc


def lay_gain(g):
    return np.ascontiguousarray(g.reshape(KD, 128).T)


def lay_wgu(w):
    gate = w[:, :DFF].reshape(KD, 128, NH, 128)
    up = w[:, DFF:].reshape(KD, 128, NH, 128)
    cat = np.concatenate([gate, up], axis=3)
    return np.ascontiguousarray(cat.transpose(2, 1, 0, 3))


def emit_proj_fm(P, R, rhs_t, rhs_bufs, nk, wtiles, ntile, ntok, banks, evac, tok0=0):
    nxt = R.rot
    for ti in range(ntile):
        w = nxt("wgu", R.wgu)
        P.dma("pool", w.t[:, 0:nk, :], wtiles[ti], w.slot, writes=[w.b])
        for j in range(2):
            oc = ti * 2 + j
            for tb in range(ntok // 512):
                bank = nxt("pbank", banks)

                def f(e):
                    for k in range(nk):
                        e.matmul(bank.t[:], lhsT=w.t[:, k, j * 128:(j + 1) * 128],
                                 rhs=rhs_t[:, k, tok0 + tb * 512:tok0 + (tb + 1) * 512],
                                 start=(k == 0), stop=(k == nk - 1))
                P.op("pe", f, reads=[w.b] + list(rhs_bufs), writes=[bank.b])
                evac(oc, tb, bank)


MH = 4
DQK = 256
DV = 512
NT128 = T // 128


def alloc_mlstm_proj(P, R):
    R.g2 = TB(P.sbuf("g2", [128, KD], F32), "g2", P.slot("sl_g2"))
    R.hng = TB(P.sbuf("hng", [128, KD], F32), "hng", P.slot("sl_hng"))
    R.gb = TB(P.sbuf("gb", [4, 2], F32), "gb", P.slot("sl_gb"))
    R.ngb = TB(P.sbuf("ngb", [4, 2], F32), "ngb")
    R.wg = TB(P.sbuf("wg", [128, KD, 8], BF16), "wg", P.slot("sl_wg"))
    R.wtm = [TB(P.sbuf(f"wtm{i}", [128, KD, 512], BF16), f"wtm{i}", P.slot(f"sl_wtm{i}")) for i in range(2)]
    R.stg = [TB(P.sbuf(f"stg{i}", [128, 512], BF16), f"stg{i}", P.slot(f"sl_stg{i}")) for i in range(4)]
    R.rows = {n: TB(P.sbuf("row_" + n, [4, T], F32), "row_" + n, P.slot("sl_row_" + n)) for n in ("ig", "sp")}
    R.onesrow = TB(P.sbuf("onesrow", [4, T], F32), "onesrow")
    P.op("dve", lambda e: e.memset(R.onesrow.t[:], 1.0), writes=[R.onesrow.b])


def emit_mlstm_proj(P, R, x1, g2_d, wfm, wtm, wg_d, gb_d, hng_d, S_):
    nxt = R.rot
    P.dma("sp", R.g2.t[:], g2_d, R.g2.slot, writes=[R.g2.b])
    P.dma("sp", R.hng.t[:], hng_d, R.hng.slot, writes=[R.hng.b])
    P.dma("sp", R.gb.t[:], gb_d, R.gb.slot, writes=[R.gb.b])
    P.dma("pool", R.wg.t[:], wg_d, R.wg.slot, writes=[R.wg.b])
    P.op("dve", lambda e: e.tensor_scalar(out=R.ngb.t[:], in0=R.gb.t[:], scalar1=-1.0, scalar2=None, op0=ALU.mult),
         reads=[R.gb.b], writes=[R.ngb.b])
    tks = []
    for tt in range(T // TT):
        t0 = tt * TT
        emit_norm(P, R, x1, t0, R.g2, nxt)

        def evac(oc, tb, bank):
            st = nxt("stg", R.stg)
            cols = slice(t0 + tb * 512, t0 + (tb + 1) * 512)
            if oc < 8:
                P.op("act", lambda e: e.activation(out=st.t[:], in_=bank.t[:], func=AF.Copy, scale=DQK ** -0.5),
                     reads=[bank.b], writes=[st.b])
                dst = S_["qT"][oc * 128:(oc + 1) * 128, cols]
            elif oc < 16:
                P.op("act", lambda e: e.activation(out=st.t[:], in_=bank.t[:], func=AF.Copy),
                     reads=[bank.b], writes=[st.b])
                dst = S_["kT"][(oc - 8) * 128:(oc - 7) * 128, cols]
            else:
                j = oc - 16
                sl = nxt("silu", R.silu)
                P.op("act", lambda e: e.activation(out=sl.t[:, 0:512], in_=bank.t[:], func=AF.Sigmoid),
                     reads=[bank.b], writes=[sl.b])
                P.op("dve", lambda e: e.tensor_scalar(out=st.t[:], in0=sl.t[:, 0:512], scalar1=R.hng.t[:, j:j + 1],
                                                      scalar2=None, op0=ALU.mult),
                     reads=[sl.b, R.hng.b], writes=[st.b])
                dst = S_["sg"][j * 128:(j + 1) * 128, cols]
            tks.append(P.dma("sp", dst, st.t[:], st.slot, reads=[st.b]))
        emit_proj_fm(P, R, R.xn_t, R.xn, KD, wfm, 16, TT, R.ps[2:8], evac)

        for tb in range(TT // 512):
            cols = slice(t0 + tb * 512, t0 + (tb + 1) * 512)
            for which in range(2):
                bank = nxt("pbank", R.ps[2:8])

                def f(e):
                    for k in range(KD):
                        e.matmul(bank.t[0:4, :], lhsT=R.wg.t[:, k, which * 4:(which + 1) * 4],
                                 rhs=R.xn_t[:, k, tb * 512:(tb + 1) * 512], start=(k == 0), stop=(k == KD - 1))
                P.op("pe", f, reads=[R.wg.b] + R.xn, writes=[bank.b])
                if which == 0:
                    P.op("dve", lambda e: e.tensor_scalar(out=R.rows["ig"].t[:, cols], in0=bank.t[0:4, :],
                                                          scalar1=R.gb.t[:, 0:1], scalar2=None, op0=ALU.add),
                         reads=[bank.b, R.gb.b], writes=[R.rows["ig"].b])
                else:
                    P.op("act", lambda e: e.activation(out=R.rows["sp"].t[:, cols], in_=bank.t[0:4, :], func=AF.Exp,
                                                       scale=-1.0, bias=R.ngb.t[:, 1:2]),
                         reads=[bank.b, R.ngb.b], writes=[R.rows["sp"].b])
                    P.op("act", lambda e: e.activation(out=R.rows["sp"].t[:, cols], in_=R.rows["sp"].t[:, cols],
                                                       func=AF.Ln, bias=R.onec.t[0:4, :]),
                         reads=[R.rows["sp"].b, R.onec.b], writes=[R.rows["sp"].b])
        for grp in range(6):
            w = nxt("wtm", R.wtm)
            P.dma("pool", w.t[:], wtm[grp], w.slot, writes=[w.b])
            for ts in range(TT // 128):
                bank = nxt("pbank", R.ps[2:8])

                def f(e):
                    for k in range(KD):
                        e.matmul(bank.t[:], lhsT=R.xn_t[:, k, ts * 128:(ts + 1) * 128], rhs=w.t[:, k, :],
                                 start=(k == 0), stop=(k == KD - 1))
                P.op("pe", f, reads=[w.b] + R.xn, writes=[bank.b])
                st = nxt("stg", R.stg)
                P.op("act", lambda e: e.activation(out=st.t[:], in_=bank.t[:], func=AF.Copy),
                     reads=[bank.b], writes=[st.b])
                rows = slice(t0 + ts * 128, t0 + (ts + 1) * 128)
                if grp < 2:
                    dst = S_["ktok"][rows, grp * 512:(grp + 1) * 512]
                else:
                    dst = S_["vtok"][rows, (grp - 2) * 512:(grp - 1) * 512]
                tks.append(P.dma("sp", dst, st.t[:], st.slot, reads=[st.b]))
    ig, sp = R.rows["ig"], R.rows["sp"]
    P.op("dve", lambda e: e.tensor_tensor_scan(out=sp.t[:], data0=R.onesrow.t[0:4, :], data1=sp.t[:], initial=0.0,
                                               op0=ALU.mult, op1=ALU.add),
         reads=[sp.b, R.onesrow.b], writes=[sp.b])
    P.op("dve", lambda e: e.tensor_tensor(out=ig.t[:], in0=ig.t[:], in1=sp.t[:], op=ALU.add),
         reads=[ig.b, sp.b], writes=[ig.b])
    tks.append(P.dma("sp", S_["arow"], ig.t[:], ig.slot, reads=[ig.b]))
    tks.append(P.dma("sp", S_["csp"], sp.t[:], sp.slot, reads=[sp.b]))
    return tks


def alloc_mlstm_rec(P, R, phase):
    R.mr = {n: TB(P.sbuf("mr_" + n, [4, T], F32), "mr_" + n, P.slot("sl_mr_" + n))
            for n in (("a", "csp", "M", "wk") if phase == "A" else ("a", "csp", "M", "wk", "ai"))}
    R.Mi = TB(P.sbuf("Mi", [4, 1], F32), "Mi")
    R.nMend = TB(P.sbuf("nMend", [4, NT128], F32), "nMend")
    R.Mprev = TB(P.sbuf("Mprev", [4, NT128], F32), "Mprev")
    R.decay = TB(P.sbuf("decay", [4, NT128], F32), "decay")
    R.cols = TB(P.sbuf("cols", [128, NT128, 8], F32), "cols")
    R.decr = TB(P.sbuf("decr", [128, MH, NT128], F32), "decr")
    R.sel = TB(P.sbuf("sel", [4, MH, 128], F32), "sel", P.slot("sl_sel"))
    R.id4 = TB(P.sbuf("id4", [4, 4], F32), "id4", P.slot("sl_id4"))
    R.Cf = [TB(P.sbuf(f"Cf{h}", [128, 2, 513], F32), f"Cf{h}", P.slot(f"sl_Cf{h}")) for h in range(MH)]
    R.Cb = [TB(P.sbuf(f"Cb{h}", [128, 2, 513], BF16), f"Cb{h}") for h in range(MH)]
    R.ktok = [TB(P.sbuf(f"ktok{i}", [128, MH, DQK], BF16), f"ktok{i}", P.slot(f"sl_ktok{i}")) for i in range(2)]
    R.vtok = [TB(P.sbuf(f"vtok{i}", [128, MH, DV], BF16), f"vtok{i}", P.slot(f"sl_vtok{i}")) for i in range(2)]
    R.kw = [TB(P.sbuf(f"kw{i}", [128, MH, DQK], BF16), f"kw{i}") for i in range(2)]
    R.onecb = TB(P.sbuf("onecb", [128, 1], BF16), "onecb")
    P.op("dve", lambda e: e.memset(R.onecb.t[:], 1.0), writes=[R.onecb.b])
    R.sc2 = TB(P.sbuf("sc2", [4, 2], F32), "sc2", P.slot("sl_sc2"))
    R.psC = [R.ps[4], R.ps[5]]
    R.psn = TB(R.ps[6].t, "psn")
    R.denp = TB(R.ps[6].t, "denp")
    R.onesrow = TB(P.sbuf("onesrow8", [4, 8], F32), "onesrow8")
    P.op("dve", lambda e: e.memset(R.onesrow.t[:], 1.0), writes=[R.onesrow.b])
    if phase == "B":
        R.maskneg = TB(P.sbuf("maskneg", [128, 128], BF16), "maskneg", P.slot("sl_mneg"))
        R.ident = TB(P.sbuf("ident", [128, 128], BF16), "ident", P.slot("sl_ident"))
        R.qt = [TB(P.sbuf(f"qt{i}", [128, 2 * MH, 128], BF16), f"qt{i}", P.slot(f"sl_qt{i}")) for i in range(2)]
        R.kt = [TB(P.sbuf(f"kt{i}", [128, 2 * MH, 128], BF16), f"kt{i}", P.slot(f"sl_kt{i}")) for i in range(2)]
        R.qs = [TB(P.sbuf(f"qs{i}", [128, 2 * MH, 128], BF16), f"qs{i}") for i in range(2)]
        R.sgt = [TB(P.sbuf(f"sgt{i}", [128, KD, 128], BF16), f"sgt{i}", P.slot(f"sl_sgt{i}")) for i in range(2)]
        R.Dm = [TB(P.sbuf(f"Dm{i}", [128, 128], F32), f"Dm{i}") for i in range(2)]
        R.wT = [TB(P.sbuf(f"wT{i}", [128, 128], BF16), f"wT{i}") for i in range(2)]
        R.hn = [TB(P.sbuf(f"hn{i}", [128, MH * DV], BF16), f"hn{i}") for i in range(2)]
        R.junk = TB(P.sbuf("junk", [128, DV], F32), "junk")
        R.gts = [TB(P.sbuf(f"gts{i}", [128, KD, 128], BF16), f"gts{i}", P.slot(f"sl_gts{i}")) for i in range(2)]
        R.sm = {n: TB(P.sbuf("sm_" + n, [128, 4], F32), "sm_" + n) for n in ("ss", "da", "r", "t1", "t2", "fac")}
        R.cmb = {n: TB(P.sbuf("cmb_" + n, [4, 8], F32), "cmb_" + n) for n in
                 ("Ms", "Cs", "mask", "Pi", "val", "t1", "t2", "wgt")}
        R.cmb_scal = TB(P.sbuf("cmb_scal", [4, 8, 2], F32), "cmb_scal", P.slot("sl_cmbs"))
        R.cmb_mask = P.slot("sl_cmbm")
        R.cmb1 = {n: TB(P.sbuf("cmb1_" + n, [4, 1], F32), "cmb1_" + n) for n in ("G", "Pc")}
        R.wrep = TB(P.sbuf("wrep", [128, MH, 8], F32), "wrep")
        R.stin = [TB(P.sbuf(f"stin{i}", [128, 2, 513], F32), f"stin{i}", P.slot(f"sl_stin{i}")) for i in range(3)]


def emit_mlstm_rec(P, R, S_, C_, phase, st_out=None, scal_out=None, allst=None, allscal=None, mask_d=None,
                   gT_d=None):
    nxt = R.rot
    mr = R.mr
    ps = R.ps
    tks = []
    a, csp, M, wk = mr["a"], mr["csp"], mr["M"], mr["wk"]
    P.dma("sp", a.t[:], S_["arow"], a.slot, writes=[a.b])
    P.dma("sp", csp.t[:], S_["csp"], csp.slot, writes=[csp.b])
    P.dma("sp", R.sel.t[:], C_["sel"], R.sel.slot, writes=[R.sel.b])
    P.dma("sp", R.id4.t[:], C_["id4"], R.id4.slot, writes=[R.id4.b])
    if phase == "A":
        P.op("dve", lambda e: e.memset(R.Mi.t[:], -1e30), writes=[R.Mi.b])
        for h in range(MH):
            P.op("dve", lambda e: e.memset(R.Cf[h].t[:], 0.0), writes=[R.Cf[h].b])
    else:
        P.dma("sp", R.maskneg.t[:], C_["maskneg"], R.maskneg.slot, writes=[R.maskneg.b])
        P.dma("sp", R.ident.t[:], C_["ident"], R.ident.slot, writes=[R.ident.b])
        c = R.cmb
        P.dma("sp", R.cmb_scal.t[:], allscal, R.cmb_scal.slot, writes=[R.cmb_scal.b])
        P.dma("sp", c["mask"].t[:], mask_d, R.cmb_mask, writes=[c["mask"].b])
        P.op("dve", lambda e: e.tensor_copy(out=c["Ms"].t[:], in_=R.cmb_scal.t[:, :, 0]),
             reads=[R.cmb_scal.b], writes=[c["Ms"].b])
        P.op("dve", lambda e: e.tensor_copy(out=c["Cs"].t[:], in_=R.cmb_scal.t[:, :, 1]),
             reads=[R.cmb_scal.b], writes=[c["Cs"].b])
        P.op("dve", lambda e: e.tensor_tensor_scan(out=c["Pi"].t[:], data0=R.onesrow.t[0:4, 0:8], data1=c["Cs"].t[:],
                                                   initial=0.0, op0=ALU.mult, op1=ALU.add),
             reads=[c["Cs"].b, R.onesrow.b], writes=[c["Pi"].b])
        P.op("dve", lambda e: e.tensor_tensor(out=c["Pi"].t[:], in0=c["Pi"].t[:], in1=c["Cs"].t[:], op=ALU.subtract),
             reads=[c["Pi"].b, c["Cs"].b], writes=[c["Pi"].b])
        P.op("dve", lambda e: e.tensor_tensor(out=c["val"].t[:], in0=c["Ms"].t[:], in1=c["Pi"].t[:], op=ALU.add),
             reads=[c["Ms"].b, c["Pi"].b], writes=[c["val"].b])
        P.op("dve", lambda e: e.tensor_tensor(out=c["t1"].t[:], in0=c["val"].t[:], in1=c["mask"].t[:], op=ALU.mult),
             reads=[c["val"].b, c["mask"].b], writes=[c["t1"].b])
        P.op("dve", lambda e: e.tensor_scalar(out=c["t2"].t[:], in0=c["mask"].t[:], scalar1=1e30, scalar2=-1e30,
                                              op0=ALU.mult, op1=ALU.add), reads=[c["mask"].b], writes=[c["t2"].b])
        P.op("dve", lambda e: e.tensor_tensor(out=c["t1"].t[:], in0=c["t1"].t[:], in1=c["t2"].t[:], op=ALU.add),
             reads=[c["t1"].b, c["t2"].b], writes=[c["t1"].b])
        G, Pc = R.cmb1["G"], R.cmb1["Pc"]
        P.op("dve", lambda e: e.tensor_reduce(out=G.t[:], in_=c["t1"].t[:], axis=AX.X, op=ALU.max),
             reads=[c["t1"].b], writes=[G.b])
        P.op("dve", lambda e: e.tensor_scalar(out=G.t[:], in0=G.t[:], scalar1=0.0, scalar2=None, op0=ALU.max),
             reads=[G.b], writes=[G.b])
        P.op("dve", lambda e: e.tensor_tensor(out=c["t2"].t[:], in0=c["Cs"].t[:], in1=c["mask"].t[:], op=ALU.mult),
             reads=[c["Cs"].b, c["mask"].b], writes=[c["t2"].b])
        P.op("dve", lambda e: e.tensor_reduce(out=Pc.t[:], in_=c["t2"].t[:], axis=AX.X, op=ALU.add),
             reads=[c["t2"].b], writes=[Pc.b])
        P.op("dve", lambda e: e.tensor_tensor(out=R.Mi.t[:], in0=G.t[:], in1=Pc.t[:], op=ALU.subtract),
             reads=[G.b, Pc.b], writes=[R.Mi.b])
        P.op("dve", lambda e: e.tensor_scalar(out=c["wgt"].t[:], in0=c["val"].t[:], scalar1=G.t[:, 0:1], scalar2=0.0,
                                              op0=ALU.subtract, op1=ALU.min), reads=[c["val"].b, G.b],
             writes=[c["wgt"].b])
        P.op("act", lambda e: e.activation(out=c["wgt"].t[:], in_=c["wgt"].t[:], func=AF.Exp),
             reads=[c["wgt"].b], writes=[c["wgt"].b])
        P.op("dve", lambda e: e.tensor_tensor(out=c["wgt"].t[:], in0=c["wgt"].t[:], in1=c["mask"].t[:], op=ALU.mult),
             reads=[c["wgt"].b, c["mask"].b], writes=[c["wgt"].b])

        def f(e):
            for h in range(MH):
                e.matmul(ps[7].t[:, h * 8:(h + 1) * 8], lhsT=R.sel.t[:, h, :], rhs=c["wgt"].t[:], start=True, stop=True)
        P.op("pe", f, reads=[R.sel.b, c["wgt"].b], writes=[ps[7].b])
        P.op("dve", lambda e: e.tensor_copy(out=R.wrep.t[:], in_=ps[7].t[:, 0:32]), reads=[ps[7].b], writes=[R.wrep.b])
        for h in range(MH):
            for cc in range(NCORES):
                sb = nxt("stin", R.stin)
                P.dma("sp", sb.t[:], allst[cc, h], sb.slot, writes=[sb.b])
                if cc == 0:
                    P.op("dve", lambda e: e.tensor_scalar(out=R.Cf[h].t[:], in0=sb.t[:], scalar1=R.wrep.t[:, h, 0:1],
                                                          scalar2=None, op0=ALU.mult),
                         reads=[sb.b, R.wrep.b], writes=[R.Cf[h].b])
                else:
                    for half in range(2):
                        P.op("dve", lambda e: e.scalar_tensor_tensor(
                            out=R.Cf[h].t[:, half, :], in0=sb.t[:, half, :], scalar=R.wrep.t[:, h, cc:cc + 1],
                            in1=R.Cf[h].t[:, half, :], op0=ALU.mult, op1=ALU.add),
                            reads=[sb.b, R.wrep.b, R.Cf[h].b], writes=[R.Cf[h].b])
    for h in range(MH):
        P.op("act", lambda e: e.activation(out=R.Cb[h].t[:], in_=R.Cf[h].t[:], func=AF.Copy),
             reads=[R.Cf[h].b], writes=[R.Cb[h].b])
    P.op("dve", lambda e: e.tensor_tensor_scan(out=M.t[:], data0=a.t[:], data1=a.t[:], initial=R.Mi.t[:, 0:1],
                                               op0=ALU.max, op1=ALU.max), reads=[a.b, R.Mi.b], writes=[M.b])
    if phase == "B":
        P.op("dve", lambda e: e.tensor_tensor(out=csp.t[:], in0=csp.t[:], in1=M.t[:], op=ALU.subtract),
             reads=[csp.b, M.b], writes=[csp.b])
        P.op("act", lambda e: e.activation(out=csp.t[:], in_=csp.t[:], func=AF.Exp), reads=[csp.b], writes=[csp.b])
    else:
        P.op("dve", lambda e: e.tensor_copy(out=R.sc2.t[:, 1:2], in_=csp.t[:, T - 1:T]), reads=[csp.b],
             writes=[R.sc2.b])
    P.op("dve", lambda e: e.tensor_scalar(out=M.t[:], in0=M.t[:], scalar1=-1.0, scalar2=None, op0=ALU.mult),
         reads=[M.b], writes=[M.b])
    nM = M
    Mv = nM.t[:].rearrange("p (c l) -> p c l", l=128)
    P.op("dve", lambda e: e.tensor_copy(out=R.nMend.t[:], in_=Mv[:, :, 127]), reads=[nM.b], writes=[R.nMend.b])
    P.op("dve", lambda e: e.tensor_copy(out=R.Mprev.t[:, 0:1], in_=R.Mi.t[:]), reads=[R.Mi.b], writes=[R.Mprev.b])
    P.op("dve", lambda e: e.tensor_scalar(out=R.Mprev.t[:, 1:NT128], in0=R.nMend.t[:, 0:NT128 - 1], scalar1=-1.0,
                                          scalar2=None, op0=ALU.mult), reads=[R.nMend.b], writes=[R.Mprev.b])
    P.op("dve", lambda e: e.tensor_tensor(out=R.decay.t[:], in0=R.Mprev.t[:], in1=R.nMend.t[:], op=ALU.add),
         reads=[R.Mprev.b, R.nMend.b], writes=[R.decay.b])
    P.op("act", lambda e: e.activation(out=R.decay.t[:], in_=R.decay.t[:], func=AF.Exp),
         reads=[R.decay.b], writes=[R.decay.b])
    if phase == "A":
        P.op("dve", lambda e: e.tensor_scalar(out=R.sc2.t[:, 0:1], in0=R.nMend.t[:, NT128 - 1:NT128], scalar1=-1.0,
                                              scalar2=None, op0=ALU.mult), reads=[R.nMend.b], writes=[R.sc2.b])
        tks.append(P.dma("sp", scal_out, R.sc2.t[:], R.sc2.slot, reads=[R.sc2.b]))
    for i in range(NT128):
        cs = slice(i * 128, (i + 1) * 128)
        P.op("act", lambda e: e.activation(out=wk.t[:, cs], in_=a.t[:, cs], func=AF.Exp, bias=R.nMend.t[:, i:i + 1]),
             reads=[a.b, R.nMend.b], writes=[wk.b])
        if phase == "B":
            P.op("act", lambda e: e.activation(out=mr["ai"].t[:, cs], in_=nM.t[:, cs], func=AF.Exp,
                                               bias=R.Mprev.t[:, i:i + 1]),
                 reads=[nM.b, R.Mprev.b], writes=[mr["ai"].b])

    def f(e):
        for i in range(NT128):
            cs = slice(i * 128, (i + 1) * 128)
            e.matmul(ps[7].t[:, i * 8:i * 8 + 4], lhsT=wk.t[:, cs], rhs=R.id4.t[:], start=True, stop=True)
            if phase == "B":
                e.matmul(ps[7].t[:, i * 8 + 4:i * 8 + 8], lhsT=csp.t[:, cs], rhs=R.id4.t[:], start=True, stop=True)
        for h in range(MH):
            e.matmul(ps[7].t[:, 256 + h * NT128:256 + (h + 1) * NT128], lhsT=R.sel.t[:, h, :], rhs=R.decay.t[:],
                     start=True, stop=True)
    P.op("pe", f, reads=[wk.b, csp.b, R.id4.b, R.sel.b, R.decay.b], writes=[ps[7].b])
    if phase == "A":
        P.op("dve", lambda e: e.memset(R.cols.t[:], 0.0), writes=[R.cols.b])
        P.op("dve", lambda e: e.tensor_copy(out=R.cols.t[:, :, 0:4],
                                            in_=ps[7].t[:, 0:NT128 * 8].rearrange("p (i c) -> p i c", c=8)[:, :, 0:4]),
             reads=[ps[7].b], writes=[R.cols.b])
    else:
        P.op("dve", lambda e: e.tensor_copy(out=R.cols.t[:],
                                            in_=ps[7].t[:, 0:NT128 * 8].rearrange("p (i c) -> p i c", c=8)),
             reads=[ps[7].b], writes=[R.cols.b])
    P.op("dve", lambda e: e.tensor_copy(out=R.decr.t[:],
                                        in_=ps[7].t[:, 256:256 + MH * NT128].rearrange("p (h i) -> p h i", h=MH)),
         reads=[ps[7].b], writes=[R.decr.b])

    for i in range(NT128 if DBG >= 2 else 0):
        cs = slice(i * 128, (i + 1) * 128)
        kt_ = nxt("ktok", R.ktok)
        vt_ = nxt("vtok", R.vtok)
        P.dma("sp", kt_.t[:], S_["ktok"][cs, :].rearrange("t (h d) -> t h d", h=MH), kt_.slot, writes=[kt_.b])
        P.dma("sp", vt_.t[:], S_["vtok"][cs, :].rearrange("t (h d) -> t h d", h=MH), vt_.slot, writes=[vt_.b])
        last = (i == NT128 - 1)
        need_update = (phase == "A") or not last
        kw = nxt("kw", R.kw)
        if need_update:
            for h in range(MH):
                P.op("pool", lambda e: e.tensor_scalar(out=kw.t[:, h, :], in0=kt_.t[:, h, :],
                                                       scalar1=R.cols.t[:, i, h:h + 1], scalar2=1.0,
                                                       op0=ALU.mult, op1=ALU.mult),
                     reads=[kt_.b, R.cols.b], writes=[kw.b])
        if phase == "B":
            qt, kT, qs, sgt = nxt("qt", R.qt), nxt("kt", R.kt), nxt("qs", R.qs), nxt("sgt", R.sgt)
            P.dma("sp", qt.t[:], S_["qT"][:, cs].rearrange("(a p) t -> p a t", p=128), qt.slot, writes=[qt.b])
            P.dma("sp", kT.t[:], S_["kT"][:, cs].rearrange("(a p) t -> p a t", p=128), kT.slot, writes=[kT.b])
            P.dma("sp", sgt.t[:], S_["sg"][:, cs].rearrange("(a p) t -> p a t", p=128), sgt.slot, writes=[sgt.b])
            arep = ps[7]

            def f(e):
                for h in range(MH):
                    e.matmul(arep.t[:, h * 128:(h + 1) * 128], lhsT=R.sel.t[:, h, :], rhs=mr["ai"].t[:, cs],
                             start=True, stop=True)
            P.op("pe", f, reads=[R.sel.b, mr["ai"].b], writes=[arep.b])
            for half in range(2):
                P.op("dve", lambda e: e.tensor_tensor(
                    out=qs.t[:].rearrange("p (h two) t -> p two h t", two=2)[:, half],
                    in0=qt.t[:].rearrange("p (h two) t -> p two h t", two=2)[:, half],
                    in1=arep.t[:].rearrange("p (h t) -> p h t", h=MH), op=ALU.mult),
                    reads=[qt.b, arep.b], writes=[qs.b])
            hn = nxt("hn", R.hn)
            denp = R.denp
            for h in range(MH if DBG >= 21 else 0):
                small = ps[h % 2]
                nump = ps[2 + h % 2]
                Dm, wT = nxt("Dm", R.Dm), nxt("wT", R.wT)

                def f(e):
                    for half in range(2):
                        e.matmul(small.t[:, 0:128], lhsT=kT.t[:, 2 * h + half, :], rhs=qt.t[:, 2 * h + half, :],
                                 start=(half == 0), stop=(half == 1))
                    e.matmul(small.t[:, 128:256], lhsT=a.t[:, cs], rhs=R.sel.t[:, h, :], start=True, stop=False)
                    e.matmul(small.t[:, 128:256], lhsT=R.sel.t[:, h, :], rhs=nM.t[:, cs], start=False, stop=False)
                    e.matmul(small.t[:, 128:256], lhsT=R.ident.t[:], rhs=R.maskneg.t[:], start=False, stop=True)
                P.op("pe", f, reads=[kT.b, qt.b, a.b, nM.b, R.sel.b, R.ident.b, R.maskneg.b], writes=[small.b])
                P.op("act", lambda e: e.activation(out=Dm.t[:], in_=small.t[:, 128:256], func=AF.Exp),
                     reads=[small.b], writes=[Dm.b])
                P.op("dve", lambda e: e.tensor_tensor(out=wT.t[:], in0=small.t[:, 0:128], in1=Dm.t[:], op=ALU.mult),
                     reads=[small.b, Dm.b], writes=[wT.b])

                if DBG < 22:
                    continue

                def f(e):
                    e.matmul(nump.t[:], lhsT=wT.t[:], rhs=vt_.t[:, h, :], start=True, stop=False)
                    for half in range(2):
                        e.matmul(nump.t[:], lhsT=qs.t[:, 2 * h + half, :], rhs=R.Cb[h].t[:, half, 0:512],
                                 start=False, stop=(half == 1))
                    e.matmul(denp.t[:, h:h + 1], lhsT=wT.t[:], rhs=R.onecb.t[:], start=True, stop=False)
                    for half in range(2):
                        e.matmul(denp.t[:, h:h + 1], lhsT=qs.t[:, 2 * h + half, :], rhs=R.Cb[h].t[:, half, 512:513],
                                 start=False, stop=(half == 1))
                P.op("pe", f, reads=[wT.b, vt_.b, qs.b, R.Cb[h].b, R.onecb.b], writes=[nump.b, denp.b])
                if DBG < 23:
                    continue
                P.op("act", lambda e: e.activation(out=R.junk.t[:], in_=nump.t[:], func=AF.Square),
                     reads=[nump.b], writes=[R.junk.b])
                P.op("dve", lambda e: e.reduce_sum(out=R.sm["ss"].t[:, h:h + 1], in_=R.junk.t[:], axis=AX.X),
                     reads=[R.junk.b], writes=[R.sm["ss"].b])
                P.op("dve", lambda e: e.tensor_copy(out=hn.t[:, h * DV:(h + 1) * DV], in_=nump.t[:]),
                     reads=[nump.b], writes=[hn.b])
                if need_update:
                    emit_state_update(P, R, h, i, kw, vt_)
            if DBG < 24:
                continue
            sm = R.sm
            P.op("dve", lambda e: e.tensor_scalar(out=sm["t1"].t[:], in0=denp.t[:, 0:4], scalar1=-1.0, scalar2=None,
                                                  op0=ALU.mult), reads=[denp.b], writes=[sm["t1"].b])
            P.op("dve", lambda e: e.tensor_tensor(out=sm["da"].t[:], in0=denp.t[:, 0:4], in1=sm["t1"].t[:],
                                                  op=ALU.max), reads=[denp.b, sm["t1"].b], writes=[sm["da"].b])
            P.op("dve", lambda e: e.tensor_tensor(out=sm["da"].t[:], in0=sm["da"].t[:], in1=R.cols.t[:, i, 4:8],
                                                  op=ALU.max), reads=[sm["da"].b, R.cols.b], writes=[sm["da"].b])
            P.op("dve", lambda e: e.reciprocal(out=sm["r"].t[:], in_=sm["da"].t[:]), reads=[sm["da"].b],
                 writes=[sm["r"].b])
            P.op("dve", lambda e: e.tensor_tensor(out=sm["t1"].t[:], in0=sm["ss"].t[:], in1=sm["r"].t[:], op=ALU.mult),
                 reads=[sm["ss"].b, sm["r"].b], writes=[sm["t1"].b])
            P.op("dve", lambda e: e.tensor_tensor(out=sm["t1"].t[:], in0=sm["t1"].t[:], in1=sm["r"].t[:], op=ALU.mult),
                 reads=[sm["t1"].b, sm["r"].b], writes=[sm["t1"].b])
            P.op("act", lambda e: e.activation(out=sm["t2"].t[:], in_=sm["t1"].t[:], func=AF.Sqrt, scale=1.0 / DV,
                                               bias=R.epsb.t[:]), reads=[sm["t1"].b, R.epsb.b], writes=[sm["t2"].b])
            P.op("dve", lambda e: e.reciprocal(out=sm["t2"].t[:], in_=sm["t2"].t[:]), reads=[sm["t2"].b],
                 writes=[sm["t2"].b])
            P.op("dve", lambda e: e.tensor_tensor(out=sm["fac"].t[:], in0=sm["t2"].t[:], in1=sm["r"].t[:], op=ALU.mult),
                 reads=[sm["t2"].b, sm["r"].b], writes=[sm["fac"].b])
            for h in range(MH):
                P.op("pool", lambda e: e.tensor_scalar(out=hn.t[:, h * DV:(h + 1) * DV], in0=hn.t[:, h * DV:(h + 1) * DV],
                                                       scalar1=sm["fac"].t[:, h:h + 1], scalar2=1.0,
                                                       op0=ALU.mult, op1=ALU.mult),
                     reads=[hn.b, sm["fac"].b], writes=[hn.b])
            gts = nxt("gts", R.gts)
            for grp in range(2 if DBG >= 3 else 0):
                tp = ps[7]
                tpv = tp.t[:].bitcast(BF16).rearrange("p (j t) -> p j t", t=128)

                def f(e):
                    for jj in range(8):
                        j = grp * 8 + jj
                        e.transpose(tpv[:, jj, :], hn.t[:, j * 128:(j + 1) * 128], R.ident.t[:])
                P.op("pe", f, reads=[hn.b, R.ident.b], writes=[tp.b])
                P.op("dve", lambda e: e.tensor_tensor(out=gts.t[:, grp * 8:(grp + 1) * 8, :], in0=tpv,
                                                      in1=sgt.t[:, grp * 8:(grp + 1) * 8, :], op=ALU.mult),
                     reads=[tp.b, sgt.b], writes=[gts.b])
            tks.append(P.dma("sp", gT_d[:, cs].rearrange("(a p) t -> p a t", p=128), gts.t[:], gts.slot,
                             reads=[gts.b]))
        else:
            for h in range(MH):
                emit_state_update(P, R, h, i, kw, vt_)
    if phase == "A":
        for h in range(MH):
            tks.append(P.dma("sp", st_out[h], R.Cf[h].t[:], R.Cf[h].slot, reads=[R.Cf[h].b]))
    return tks


def emit_state_update(P, R, h, i, kw, vt_):
    def f(e):
        for half in range(2):
            e.matmul(R.psC[half].t[:], lhsT=kw.t[:, h, half * 128:(half + 1) * 128], rhs=vt_.t[:, h, :],
                     start=True, stop=True)
            e.matmul(R.psn.t[:, 8 + half:9 + half], lhsT=kw.t[:, h, half * 128:(half + 1) * 128], rhs=R.onecb.t[:],
                     start=True, stop=True)
    P.op("pe", f, reads=[kw.b, vt_.b, R.onecb.b], writes=[R.psC[0].b, R.psC[1].b, R.psn.b])
    dec = R.decr.t[:, h, i:i + 1]
    for half in range(2):
        P.op("dve", lambda e: e.scalar_tensor_tensor(out=R.Cf[h].t[:, half, 0:512], in0=R.Cf[h].t[:, half, 0:512],
                                                     scalar=dec, in1=R.psC[half].t[:], op0=ALU.mult, op1=ALU.add),
             reads=[R.Cf[h].b, R.decr.b, R.psC[half].b], writes=[R.Cf[h].b])
    P.op("dve", lambda e: e.scalar_tensor_tensor(out=R.Cf[h].t[:, :, 512], in0=R.Cf[h].t[:, :, 512], scalar=dec,
                                                 in1=R.psn.t[:, 8:10], op0=ALU.mult, op1=ALU.add),
         reads=[R.Cf[h].b, R.decr.b, R.psn.b], writes=[R.Cf[h].b])
    P.op("act", lambda e: e.activation(out=R.Cb[h].t[:], in_=R.Cf[h].t[:], func=AF.Copy),
         reads=[R.Cf[h].b], writes=[R.Cb[h].b])


def emit_proj_resid(P, R, src_d, wtiles, xin, xout, scale=1.0):
    nxt = R.rot
    tks = []
    for tt in range(T // TT):
        t0 = tt * TT
        P.dma("sp", R.xn_t[:], src_d[:, t0:t0 + TT].rearrange("(a p) t -> p a t", p=128), R.xr[0].slot,
              writes=R.xn)

        def evac(oc, tb, bank):
            cols = slice(t0 + tb * 512, t0 + (tb + 1) * 512)
            xr = nxt("xr", R.xr)
            P.dma("sp", xr.t[:], xin[oc * 128:(oc + 1) * 128, cols], xr.slot, writes=[xr.b])
            ob = nxt("osb", R.osb)
            P.op("dve", lambda e: e.scalar_tensor_tensor(out=ob.t[:], in0=bank.t[:], scalar=scale, in1=xr.t[:],
                                                         op0=ALU.mult, op1=ALU.add),
                 reads=[bank.b, xr.b], writes=[ob.b])
            tks.append(P.dma("sp", xout[oc * 128:(oc + 1) * 128, cols], ob.t[:], ob.slot, reads=[ob.b]))
        emit_proj_fm(P, R, R.xn_t, R.xn, KD, wtiles, 8, TT, R.ps[0:8], evac)
    return tks


def dram(nc, name, shape, dt, kind):
    return nc.dram_tensor(name, list(shape), dt, kind=kind).ap()


def mlstm_scratch(nc, kind):
    return {
        "qT": dram(nc, "m_qT", [MH * DQK, T], BF16, kind), "kT": dram(nc, "m_kT", [MH * DQK, T], BF16, kind),
        "sg": dram(nc, "m_sg", [D, T], BF16, kind), "ktok": dram(nc, "m_ktok", [T, MH * DQK], BF16, kind),
        "vtok": dram(nc, "m_vtok", [T, D], BF16, kind), "arow": dram(nc, "m_arow", [4, T], F32, kind),
        "csp": dram(nc, "m_csp", [4, T], F32, kind),
    }


def mlstm_consts(nc):
    return {"sel": dram(nc, "c_sel", [4, MH, 128], F32, "ExternalInput"),
            "id4": dram(nc, "c_id4", [4, 4], F32, "ExternalInput"),
            "maskneg": dram(nc, "c_maskneg", [128, 128], BF16, "ExternalInput"),
            "ident": dram(nc, "c_ident", [128, 128], BF16, "ExternalInput")}


def host_consts():
    import ml_dtypes
    sel = np.zeros((4, MH, 128), np.float32)
    for h in range(MH):
        sel[h, h, :] = 1.0
    s_ = np.arange(128)[:, None]
    t_ = np.arange(128)[None, :]
    maskneg = np.where(s_ <= t_, 0.0, -30000.0).astype(ml_dtypes.bfloat16)
    return {"c_sel": sel, "c_id4": np.eye(4, dtype=np.float32), "c_maskneg": maskneg,
            "c_ident": np.eye(128, dtype=np.float32).astype(ml_dtypes.bfloat16)}


def build_mlstm_A():
    nc = bass.Bass("TRN2", target_bir_lowering=False)
    x1 = dram(nc, "x1", [D, T], F32, "ExternalInput")
    g2 = dram(nc, "g2", [128, KD], F32, "ExternalInput")
    wfm = dram(nc, "m_wfm", [16, 128, KD, 256], F32, "ExternalInput")
    wtm = dram(nc, "m_wtm", [6, 128, KD, 512], F32, "ExternalInput")
    wg = dram(nc, "m_wg", [128, KD, 8], F32, "ExternalInput")
    gb = dram(nc, "m_gb", [4, 2], F32, "ExternalInput")
    hng = dram(nc, "m_hng", [128, KD], F32, "ExternalInput")
    S_ = mlstm_scratch(nc, "ExternalOutput")
    C_ = mlstm_consts(nc)
    st_out = dram(nc, "st_out", [MH, 128, 2, 513], F32, "ExternalOutput")
    scal_out = dram(nc, "scal_out", [4, 2], F32, "ExternalOutput")
    with contextlib.ExitStack() as es:
        P = Prog(nc, es)
        R = alloc_common(P)
        P.push()
        alloc_normproj(P, R)
        alloc_mlstm_proj(P, R)
        tks = emit_mlstm_proj(P, R, x1, g2, wfm, wtm, wg, gb, hng, S_)
        P.wait("sp", tks)
        P.pop()
        P.push()
        alloc_mlstm_rec(P, R, "A")
        tks = emit_mlstm_rec(P, R, S_, C_, "A", st_out=st_out, scal_out=scal_out)
        P.wait("sp", tks)
        P.pop()
        P.emit()
    return nc


def build_mlstm_B():
    nc = bass.Bass("TRN2", target_bir_lowering=False)
    x1 = dram(nc, "x1", [D, T], F32, "ExternalInput")
    wo = dram(nc, "m_wo", [8, 128, KD, 256], F32, "ExternalInput")
    S_ = mlstm_scratch(nc, "ExternalInput")
    C_ = mlstm_consts(nc)
    allst = dram(nc, "allst", [NCORES, MH, 128, 2, 513], F32, "ExternalInput")
    allscal = dram(nc, "allscal", [4, NCORES, 2], F32, "ExternalInput")
    mask = dram(nc, "cmask", [4, NCORES], F32, "ExternalInput")
    gT = dram(nc, "m_gT", [D, T], BF16, "Internal")
    x2 = dram(nc, "x2", [D, T], F32, "ExternalOutput")
    with contextlib.ExitStack() as es:
        P = Prog(nc, es)
        R = alloc_common(P)
        P.push()
        alloc_mlstm_rec(P, R, "B")
        tks = emit_mlstm_rec(P, R, S_, C_, "B", allst=allst, allscal=allscal, mask_d=mask, gT_d=gT)
        P.wait("sp", tks)
        P.pop()
        P.push()
        alloc_normproj(P, R)
        alloc_resid(P, R)
        tks = emit_proj_resid(P, R, gT, wo, x1, x2)
        P.wait("sp", tks)
        P.pop()
        P.emit()
    return nc


def lay_tiles(w, ncol):
    K_, N_ = w.shape
    return np.ascontiguousarray(w.reshape(K_ // 128, 128, N_ // ncol, ncol).transpose(2, 1, 0, 3))


def lay_mlstm(w_in, gate_bias, head_norm, w_out):
    q, k, v, o = w_in[:, 0:1024], w_in[:, 1024:2048], w_in[:, 2048:4096], w_in[:, 4096:6144]
    wfm = lay_tiles(np.concatenate([q, k, o], axis=1), 256)
    wtm = lay_tiles(np.concatenate([k, v], axis=1), 512)
    wg = np.ascontiguousarray(w_in[:, 6144:6152].reshape(KD, 128, 8).transpose(1, 0, 2))
    gb = np.ascontiguousarray(gate_bias.T)
    return {"m_wfm": wfm, "m_wtm": wtm, "m_wg": wg, "m_gb": gb, "m_hng": lay_gain(head_norm),
            "m_wo": lay_tiles(w_out, 256)}


AH = 16
TWO_PI = 6.283185307179586
CW1 = 6.28125
CW2 = float(np.float32(TWO_PI - CW1))
CW3 = float(TWO_PI - CW1 - float(np.float32(TWO_PI - CW1)))
ASCALE = 192 ** -0.5


def alloc_mla_proj(P, R):
    R.g3 = TB(P.sbuf("g3", [128, KD], F32), "g3", P.slot("sl_g3"))
    R.lg = TB(P.sbuf("lg", [128, 12], F32), "lg", P.slot("sl_lg"))
    R.freq = TB(P.sbuf("freq", [64, 1], F32), "freq", P.slot("sl_freq"))
    R.PT = TB(P.sbuf("PT", [64, 64], F32), "PT", P.slot("sl_PT"))
    R.lat_t = P.sbuf("lat", [128, 9, TT], F32)
    R.lat = [Buf(f"lat{j}") for j in range(9)]
    R.cqn_t = P.sbuf("cqn", [128, 4, TT], BF16)
    R.cqn = [Buf(f"cqn{j}") for j in range(4)]
    R.ckvn_t = P.sbuf("ckvn", [128, 4, TT], BF16)
    R.ckvn = [Buf(f"ckvn{j}") for j in range(4)]
    R.posi = TB(P.sbuf("posi", [64, TT], I32), "posi", P.slot("sl_posi"))
    R.ang = TB(P.sbuf("ang", [64, TT], F32), "ang")
    R.kk = TB(P.sbuf("kk", [64, TT], F32), "kk")
    R.kki = TB(P.sbuf("kki", [64, TT], I32), "kki")
    R.fx = TB(P.sbuf("fx", [64, TT], F32), "fx")
    R.cos = TB(P.sbuf("cos", [64, TT], F32), "cos")
    R.sin = TB(P.sbuf("sin", [64, TT], F32), "sin")
    R.qf = [TB(P.sbuf(f"qf{i}", [128, 512], F32), f"qf{i}") for i in range(2)]
    R.qq = [TB(P.sbuf(f"qq{i}", [128, 512], BF16), f"qq{i}") for i in range(2)]
    R.qr = [TB(P.sbuf(f"qr{i}", [128, 512], F32), f"qr{i}") for i in range(2)]
    R.qn2 = [TB(P.sbuf(f"qn2{i}", [128, 512], F32), f"qn2{i}") for i in range(2)]
    R.stg = [TB(P.sbuf(f"astg{i}", [128, 512], BF16), f"astg{i}", P.slot(f"sl_astg{i}")) for i in range(4)]
    R.wv = [TB(P.sbuf(f"wv{i}", [128, 4, 512], BF16), f"wv{i}", P.slot(f"sl_wv{i}")) for i in range(2)]


def emit_headnorm(P, R, bank, npart, gcol, scale, rope, cols_local, dst, tks):
    nxt = R.rot
    qq, qr, qn2 = nxt("qq", R.qq), nxt("qr", R.qr), nxt("qn2", R.qn2)
    ssb = nxt("ssb", R.ps[0:2])
    pp = slice(0, npart)
    epsb = R.epsb if scale == 1.0 else R.epsq
    P.op("act", lambda e: e.activation(out=qq.t[:], in_=bank.t[:], func=AF.Square), reads=[bank.b],
         writes=[qq.b])
    P.op("pe", lambda e: e.matmul(ssb.t[:], lhsT=R.ones.t[:], rhs=qq.t[:], start=True, stop=True),
         reads=[qq.b, R.ones.b], writes=[ssb.b])
    P.op("act", lambda e: e.activation(out=qr.t[:], in_=ssb.t[:], func=AF.Ln, scale=1.0 / (npart * scale * scale),
                                       bias=epsb.t[:]), reads=[ssb.b, epsb.b], writes=[qr.b])
    P.op("act", lambda e: e.activation(out=qr.t[:], in_=qr.t[:], func=AF.Exp, scale=-0.5), reads=[qr.b],
         writes=[qr.b])
    st = nxt("stg", R.stg)
    if not rope:
        P.op("dve", lambda e: e.scalar_tensor_tensor(out=st.t[pp, :], in0=bank.t[pp, :],
                                                     scalar=R.lg.t[pp, gcol:gcol + 1], in1=qr.t[pp, :],
                                                     op0=ALU.mult, op1=ALU.mult),
             reads=[bank.b, R.lg.b, qr.b], writes=[st.b])
    else:
        qf, qt2 = nxt("qf", R.qf), nxt("qt2", R.qt2)
        P.op("dve", lambda e: e.scalar_tensor_tensor(out=qn2.t[:], in0=bank.t[:],
                                                     scalar=R.lg.t[:, gcol:gcol + 1], in1=qr.t[:],
                                                     op0=ALU.mult, op1=ALU.mult),
             reads=[bank.b, R.lg.b, qr.b], writes=[qn2.b])
        rb = nxt("ssb", R.ps[0:2])
        P.op("pe", lambda e: e.matmul(rb.t[:], lhsT=R.PT128.t[:], rhs=qn2.t[:], start=True, stop=True),
             reads=[qn2.b, R.PT128.b], writes=[rb.b])
        P.op("pool", lambda e: e.tensor_tensor(out=qf.t[pp, :], in0=qn2.t[pp, :], in1=R.cos.t[:, cols_local],
                                               op=ALU.mult), reads=[qn2.b, R.cos.b], writes=[qf.b])
        P.op("dve", lambda e: e.tensor_tensor(out=qt2.t[pp, :], in0=rb.t[pp, :], in1=R.sin.t[:, cols_local],
                                              op=ALU.mult), reads=[rb.b, R.sin.b], writes=[qt2.b])
        P.op("dve", lambda e: e.tensor_tensor(out=st.t[pp, :], in0=qt2.t[pp, :], in1=qf.t[pp, :], op=ALU.add),
             reads=[qt2.b, qf.b], writes=[st.b])
    tks.append(P.dma("sp", dst, st.t[pp, :], st.slot, reads=[st.b]))


def emit_mla_proj(P, R, x3, g3_d, lg_d, freq_d, PT_d, pos_d, win, wuq, wkn, wv, A_):
    nxt = R.rot
    tks = []
    for tb_, d_ in ((R.g3, g3_d), (R.lg, lg_d), (R.freq, freq_d), (R.PT, PT_d)):
        P.dma("sp", tb_.t[:], d_, tb_.slot, writes=[tb_.b])
    for tt in range(T // TT):
        t0 = tt * TT
        emit_norm(P, R, x3, t0, R.g3, nxt)
        P.dma("sp", R.posi.t[:], pos_d[0:1, t0:t0 + TT].partition_broadcast(64), R.posi.slot, writes=[R.posi.b])
        ang, kk, kki, fx = R.ang, R.kk, R.kki, R.fx
        P.op("dve", lambda e: e.tensor_copy(out=ang.t[:], in_=R.posi.t[:]), reads=[R.posi.b], writes=[ang.b])
        P.op("dve", lambda e: e.tensor_scalar(out=ang.t[:], in0=ang.t[:], scalar1=R.freq.t[:, 0:1], scalar2=None,
                                              op0=ALU.mult), reads=[ang.b, R.freq.b], writes=[ang.b])
        for which, shift in (("sin", 0.0), ("cos", np.pi / 2)):
            dst = R.sin if which == "sin" else R.cos
            P.op("dve", lambda e: e.tensor_scalar(out=kk.t[:], in0=ang.t[:], scalar1=1.0 / TWO_PI, scalar2=0.5,
                                                  op0=ALU.mult, op1=ALU.add), reads=[ang.b], writes=[kk.b])
            P.op("dve", lambda e: e.tensor_copy(out=kki.t[:], in_=kk.t[:]), reads=[kk.b], writes=[kki.b])
            P.op("dve", lambda e: e.tensor_copy(out=kk.t[:], in_=kki.t[:]), reads=[kki.b], writes=[kk.b])
            P.op("dve", lambda e: e.scalar_tensor_tensor(out=dst.t[:], in0=kk.t[:], scalar=-CW1, in1=ang.t[:],
                                                         op0=ALU.mult, op1=ALU.add), reads=[kk.b, ang.b],
                 writes=[dst.b])
            P.op("dve", lambda e: e.scalar_tensor_tensor(out=dst.t[:], in0=kk.t[:], scalar=-CW2, in1=dst.t[:],
                                                         op0=ALU.mult, op1=ALU.add), reads=[kk.b, dst.b],
                 writes=[dst.b])
            P.op("dve", lambda e: e.scalar_tensor_tensor(out=dst.t[:], in0=kk.t[:], scalar=-CW3, in1=dst.t[:],
                                                         op0=ALU.mult, op1=ALU.add), reads=[kk.b, dst.b],
                 writes=[dst.b])
            if shift:
                P.op("dve", lambda e: e.tensor_scalar(out=dst.t[:], in0=dst.t[:], scalar1=shift, scalar2=None,
                                                      op0=ALU.add), reads=[dst.b], writes=[dst.b])
            for cmp_, sgn in ((ALU.is_gt, -TWO_PI), (ALU.is_lt, TWO_PI)):
                thr = np.pi if sgn < 0 else -np.pi
                P.op("dve", lambda e: e.tensor_scalar(out=fx.t[:], in0=dst.t[:], scalar1=thr, scalar2=sgn,
                                                      op0=cmp_, op1=ALU.mult), reads=[dst.b], writes=[fx.b])
                P.op("dve", lambda e: e.tensor_tensor(out=dst.t[:], in0=dst.t[:], in1=fx.t[:], op=ALU.add),
                     reads=[dst.b, fx.b], writes=[dst.b])
            P.op("act", lambda e: e.activation(out=dst.t[:], in_=dst.t[:], func=AF.Sin), reads=[dst.b], writes=[dst.b])

        ssq = [R.ps[2], R.ps[3]]

        def evac(oc, tb, bank):
            cl = slice(tb * 512, (tb + 1) * 512)
            if oc > 8:
                return
            P.op("act", lambda e: e.activation(out=R.lat_t[:, oc, cl], in_=bank.t[:], func=AF.Copy),
                 reads=[bank.b], writes=[R.lat[oc]])
        emit_proj_fm(P, R, R.xn_t, R.xn, KD, win, 5, TT, R.ps[4:8], evac)
        for grp, dst_t, dst_b, g0 in ((0, R.cqn_t, R.cqn, 0), (1, R.ckvn_t, R.ckvn, 4)):
            for tb in range(TT // 512):
                cl = slice(tb * 512, (tb + 1) * 512)
                ssb = R.ps[2 + tb]
                for j in range(4):
                    qq = nxt("qq", R.qq)
                    P.op("act", lambda e: e.activation(out=qq.t[:], in_=R.lat_t[:, grp * 4 + j, cl], func=AF.Square),
                         reads=[R.lat[grp * 4 + j]], writes=[qq.b])
                    P.op("pe", lambda e: e.matmul(ssb.t[:], lhsT=R.ones.t[:], rhs=qq.t[:], start=(j == 0),
                                                  stop=(j == 3)), reads=[qq.b, R.ones.b], writes=[ssb.b])
                qr = nxt("qr", R.qr)
                P.op("act", lambda e: e.activation(out=qr.t[:], in_=ssb.t[:], func=AF.Sqrt, scale=1.0 / 512,
                                                   bias=R.epsb.t[:]), reads=[ssb.b, R.epsb.b], writes=[qr.b])
                P.op("dve", lambda e: e.reciprocal(out=qr.t[:], in_=qr.t[:]), reads=[qr.b], writes=[qr.b])
                for j in range(4):
                    P.op("dve", lambda e: e.scalar_tensor_tensor(
                        out=dst_t[:, j, cl], in0=R.lat_t[:, grp * 4 + j, cl], scalar=R.lg.t[:, g0 + j:g0 + j + 1],
                        in1=qr.t[:], op0=ALU.mult, op1=ALU.mult),
                        reads=[R.lat[grp * 4 + j], R.lg.b, qr.b], writes=[dst_b[j]])
        for tb in range(TT // 512):
            cl = slice(tb * 512, (tb + 1) * 512)
            emit_headnorm_sb(P, R, R.lat_t[0:64, 8, cl], R.lat[8], 64, 11, 1.0, True, cl,
                             A_["krT"][:, t0 + tb * 512:t0 + (tb + 1) * 512], tks)

        def evac_q(oc, tb, bank):
            h, isr = oc // 2, oc % 2
            cl = slice(tb * 512, (tb + 1) * 512)
            gc = slice(t0 + tb * 512, t0 + (tb + 1) * 512)
            if not isr:
                emit_headnorm(P, R, bank, 128, 8, ASCALE, False, cl, A_["qnT"][h, :, gc], tks)
            else:
                emit_headnorm(P, R, bank, 64, 9, ASCALE, True, cl, A_["qrT"][h, :, gc], tks)
        emit_proj_fm(P, R, R.cqn_t, R.cqn, 4, wuq, 16, TT, R.ps[4:8], evac_q)

        def evac_k(oc, tb, bank):
            cl = slice(tb * 512, (tb + 1) * 512)
            gc = slice(t0 + tb * 512, t0 + (tb + 1) * 512)
            emit_headnorm(P, R, bank, 128, 10, 1.0, False, cl, A_["knT"][oc, :, gc], tks)
        emit_proj_fm(P, R, R.ckvn_t, R.ckvn, 4, wkn, 8, TT, R.ps[4:8], evac_k)
        for grp in range(4):
            w = nxt("wv", R.wv)
            P.dma("pool", w.t[:], wv[grp], w.slot, writes=[w.b])
            for ts in range(TT // 128):
                bank = nxt("pbank", R.ps[4:8])

                def f(e):
                    for k in range(4):
                        e.matmul(bank.t[:], lhsT=R.ckvn_t[:, k, ts * 128:(ts + 1) * 128], rhs=w.t[:, k, :],
                                 start=(k == 0), stop=(k == 3))
                P.op("pe", f, reads=[w.b] + R.ckvn, writes=[bank.b])
                st = nxt("stg", R.stg)
                P.op("act", lambda e: e.activation(out=st.t[:], in_=bank.t[:], func=AF.Copy), reads=[bank.b],
                     writes=[st.b])
                tks.append(P.dma("sp", A_["V"][t0 + ts * 128:t0 + (ts + 1) * 128, grp * 512:(grp + 1) * 512], st.t[:],
                                 st.slot, reads=[st.b]))
    return tks


def emit_headnorm_sb(P, R, src_ap, src_buf, npart, gcol, scale, rope, cols_local, dst, tks):
    nxt = R.rot
    qf, qq, qr, qn2 = nxt("qf", R.qf), nxt("qq", R.qq), nxt("qr", R.qr), nxt("qn2", R.qn2)
    ssb = nxt("ssb", R.ps[0:2])
    pp = slice(0, npart)
    P.op("act", lambda e: e.activation(out=qq.t[pp, :], in_=src_ap, func=AF.Square), reads=[src_buf], writes=[qq.b])
    P.op("pe", lambda e: e.matmul(ssb.t[:], lhsT=R.ones.t[pp, :], rhs=qq.t[pp, :], start=True, stop=True),
         reads=[qq.b, R.ones.b], writes=[ssb.b])
    P.op("act", lambda e: e.activation(out=qr.t[:], in_=ssb.t[:], func=AF.Sqrt, scale=1.0 / npart, bias=R.epsb.t[:]),
         reads=[ssb.b, R.epsb.b], writes=[qr.b])
    P.op("dve", lambda e: e.reciprocal(out=qr.t[:], in_=qr.t[:]), reads=[qr.b], writes=[qr.b])
    st = nxt("stg", R.stg)
    P.op("dve", lambda e: e.tensor_scalar(out=qn2.t[pp, :], in0=src_ap, scalar1=R.lg.t[pp, gcol:gcol + 1],
                                          scalar2=scale, op0=ALU.mult, op1=ALU.mult),
         reads=[src_buf, R.lg.b], writes=[qn2.b])
    P.op("dve", lambda e: e.tensor_tensor(out=qn2.t[pp, :], in0=qn2.t[pp, :], in1=qr.t[pp, :], op=ALU.mult),
         reads=[qn2.b, qr.b], writes=[qn2.b])
    rb = nxt("ssb", R.ps[0:2])
    P.op("pe", lambda e: e.matmul(rb.t[pp, :], lhsT=R.PT.t[:], rhs=qn2.t[pp, :], start=True, stop=True),
         reads=[qn2.b, R.PT.b], writes=[rb.b])
    P.op("dve", lambda e: e.tensor_tensor(out=qf.t[pp, :], in0=rb.t[pp, :], in1=R.sin.t[:, cols_local], op=ALU.mult),
         reads=[rb.b, R.sin.b], writes=[qf.b])
    P.op("dve", lambda e: e.tensor_tensor(out=qn2.t[pp, :], in0=qn2.t[pp, :], in1=R.cos.t[:, cols_local], op=ALU.mult),
         reads=[qn2.b, R.cos.b], writes=[qn2.b])
    P.op("dve", lambda e: e.tensor_tensor(out=st.t[pp, :], in0=qn2.t[pp, :], in1=qf.t[pp, :], op=ALU.add),
         reads=[qn2.b, qf.b], writes=[st.b])
    tks.append(P.dma("sp", dst, st.t[pp, :], st.slot, reads=[st.b]))


def alloc_mla_attn(P, R):
    R.KnA = [TB(P.sbuf(f"KnA{i}", [128, 4, T], BF16), f"KnA{i}", P.slot(f"sl_KnA{i}")) for i in range(2)]
    R.VA = [TB(P.sbuf(f"VA{i}", [128, 4 * NT128, 128], BF16), f"VA{i}", P.slot(f"sl_VA{i}")) for i in range(2)]
    R.KrA = TB(P.sbuf("KrA", [64, NCORES, T], BF16), "KrA", P.slot("sl_KrA"))
    R.KrO = TB(P.sbuf("KrO", [64, T], BF16), "KrO", P.slot("sl_KrO"))
    R.KnO = TB(P.sbuf("KnO", [128, T], BF16), "KnO", P.slot("sl_KnO"))
    R.VO = TB(P.sbuf("VO", [128, NT128, 128], BF16), "VO", P.slot("sl_VO"))
    R.qn = TB(P.sbuf("qn", [128, T], BF16), "qn", P.slot("sl_qn"))
    R.qrr = TB(P.sbuf("qrr", [64, T], BF16), "qrr", P.slot("sl_qrr"))
    R.m01 = TB(P.sbuf("m01", [128, 4, 512], BF16), "m01", P.slot("sl_m01"))
    R.biasm = TB(P.sbuf("biasm", [128, NCORES], F32), "biasm", P.slot("sl_biasm"))
    R.PTt = [TB(P.sbuf(f"PTt{i}", [128, 512], BF16), f"PTt{i}") for i in range(4)]
    R.rd = TB(P.sbuf("rd", [128, 512], F32), "rd")
    R.ost = [TB(P.sbuf(f"ost{i}", [128, 512], BF16), f"ost{i}", P.slot(f"sl_ost{i}")) for i in range(2)]


def emit_mla_attn(P, R, A_, G_, m01_d, biasm_d, oT_d):
    nxt = R.rot
    ps = R.ps
    tks = []
    P.dma("sp", R.m01.t[:], m01_d, R.m01.slot, writes=[R.m01.b])
    P.dma("sp", R.biasm.t[:], biasm_d, R.biasm.slot, writes=[R.biasm.b])
    P.dma("sp", R.KrA.t[:], G_["krT"].rearrange("c r t -> r c t"), R.KrA.slot, writes=[R.KrA.b])
    P.dma("sp", R.KrO.t[:], A_["krT"], R.KrO.slot, writes=[R.KrO.b])
    it = 0
    for h in range(AH):
        for half in range(2):
            P.dma("sp", R.KnA[half].t[:], G_["knT"][half * 4:(half + 1) * 4, h].rearrange("c d t -> d c t"),
                  R.KnA[half].slot, writes=[R.KnA[half].b])
            P.dma("sp", R.VA[half].t[:],
                  G_["V"][half * 4:(half + 1) * 4, :, h * 128:(h + 1) * 128].rearrange("c (i p) v -> p (c i) v", p=128),
                  R.VA[half].slot, writes=[R.VA[half].b])
        P.dma("sp", R.KnO.t[:], A_["knT"][h], R.KnO.slot, writes=[R.KnO.b])
        P.dma("sp", R.VO.t[:], A_["V"][:, h * 128:(h + 1) * 128].rearrange("(i p) v -> p i v", p=128), R.VO.slot,
              writes=[R.VO.b])
        P.dma("sp", R.qn.t[:], A_["qnT"][h], R.qn.slot, writes=[R.qn.b])
        P.dma("sp", R.qrr.t[:], A_["qrT"][h], R.qrr.slot, writes=[R.qrr.b])
        for qb in range(T // 512):
            qs_ = slice(qb * 512, (qb + 1) * 512)
            tiles = []
            for j in range(NCORES):
                for i in range(NT128):
                    ks = slice(i * 128, (i + 1) * 128)
                    tiles.append(dict(kn=R.KnA[j // 4].t[:, j % 4, ks], knb=R.KnA[j // 4].b,
                                      kr=R.KrA.t[:, j, ks], krb=R.KrA.b,
                                      v=R.VA[j // 4].t[:, (j % 4) * NT128 + i, :], vb=R.VA[j // 4].b,
                                      bias=R.biasm.t[:, j:j + 1], m=None))
            for i in range(4 * (qb + 1)):
                ks = slice(i * 128, (i + 1) * 128)
                m = i - 4 * qb
                tiles.append(dict(kn=R.KnO.t[:, ks], knb=R.KnO.b, kr=R.KrO.t[:, ks], krb=R.KrO.b,
                                  v=R.VO.t[:, i, :], vb=R.VO.b, bias=None, m=(m if m >= 0 else None)))
            outp, denp = ps[4 + it % 2], ps[6 + it % 2]
            it += 1
            n = len(tiles)
            LOOK = 2
            sbanks = {}

            def issue_s(idx):
                tl = tiles[idx]
                sb = nxt("sbank", ps[0:4])
                sbanks[idx] = sb

                def f(e):
                    e.matmul(sb.t[:], lhsT=tl["kn"], rhs=R.qn.t[:, qs_], start=True, stop=False)
                    e.matmul(sb.t[:], lhsT=tl["kr"], rhs=R.qrr.t[:, qs_], start=False, stop=True)
                P.op("pe", f, reads=[tl["knb"], tl["krb"], R.qn.b, R.qrr.b], writes=[sb.b])
            for idx in range(min(LOOK, n)):
                issue_s(idx)
            for idx in range(n):
                tl = tiles[idx]
                sb = sbanks.pop(idx)
                pt = nxt("PTt", R.PTt)
                if tl["bias"] is not None:
                    P.op("act", lambda e: e.activation(out=pt.t[:], in_=sb.t[:], func=AF.Exp, bias=tl["bias"]),
                         reads=[sb.b, R.biasm.b], writes=[pt.b])
                else:
                    P.op("act", lambda e: e.activation(out=pt.t[:], in_=sb.t[:], func=AF.Exp),
                         reads=[sb.b], writes=[pt.b])
                if tl["m"] is not None:
                    P.op("dve", lambda e: e.tensor_tensor(out=pt.t[:], in0=pt.t[:], in1=R.m01.t[:, tl["m"], :],
                                                          op=ALU.mult), reads=[pt.b, R.m01.b], writes=[pt.b])
                if idx + LOOK < n:
                    issue_s(idx + LOOK)

                def f(e):
                    e.matmul(outp.t[:], lhsT=tl["v"], rhs=pt.t[:], start=(idx == 0), stop=(idx == n - 1))
                    e.matmul(denp.t[:], lhsT=R.ones.t[:], rhs=pt.t[:], start=(idx == 0), stop=(idx == n - 1))
                P.op("pe", f, reads=[tl["vb"], pt.b, R.ones.b], writes=[outp.b, denp.b])
            P.op("dve", lambda e: e.reciprocal(out=R.rd.t[:], in_=denp.t[:]), reads=[denp.b], writes=[R.rd.b])
            ost = nxt("ost", R.ost)
            P.op("dve", lambda e: e.tensor_tensor(out=ost.t[:], in0=outp.t[:], in1=R.rd.t[:], op=ALU.mult),
                 reads=[outp.b, R.rd.b], writes=[ost.b])
            tks.append(P.dma("sp", oT_d[h * 128:(h + 1) * 128, qs_], ost.t[:], ost.slot, reads=[ost.b]))
    return tks


def mla_scratch(nc, kind):
    return {"qnT": dram(nc, "a_qnT", [AH, 128, T], BF16, kind), "qrT": dram(nc, "a_qrT", [AH, 64, T], BF16, kind),
            "knT": dram(nc, "a_knT", [AH, 128, T], BF16, kind), "krT": dram(nc, "a_krT", [64, T], BF16, kind),
            "V": dram(nc, "a_V", [T, AH * 128], BF16, kind)}


def build_mla_A():
    nc = bass.Bass("TRN2", target_bir_lowering=False)
    x3 = dram(nc, "x3", [D, T], F32, "ExternalInput")
    g3 = dram(nc, "g3", [128, KD], F32, "ExternalInput")
    lg = dram(nc, "a_lg", [128, 12], F32, "ExternalInput")
    freq = dram(nc, "c_freq", [64, 1], F32, "ExternalInput")
    PT = dram(nc, "c_PT", [64, 64], F32, "ExternalInput")
    pos = dram(nc, "pos", [1, T], I32, "ExternalInput")
    win = dram(nc, "a_win", [5, 128, KD, 256], F32, "ExternalInput")
    wuq = dram(nc, "a_wuq", [16, 128, 4, 256], F32, "ExternalInput")
    wkn = dram(nc, "a_wkn", [8, 128, 4, 256], F32, "ExternalInput")
    wv = dram(nc, "a_wv", [4, 128, 4, 512], F32, "ExternalInput")
    A_ = mla_scratch(nc, "ExternalOutput")
    with contextlib.ExitStack() as es:
        P = Prog(nc, es)
        R = alloc_common(P)
        P.push()
        alloc_normproj(P, R)
        alloc_mla_proj(P, R)
        tks = emit_mla_proj(P, R, x3, g3, lg, freq, PT, pos, win, wuq, wkn, wv, A_)
        P.wait("sp", tks)
        P.pop()
        P.emit()
    return nc


def build_mla_B():
    nc = bass.Bass("TRN2", target_bir_lowering=False)
    x3 = dram(nc, "x3", [D, T], F32, "ExternalInput")
    wo = dram(nc, "a_wo", [8, 128, KD, 256], F32, "ExternalInput")
    A_ = mla_scratch(nc, "ExternalInput")
    G_ = {"knT": dram(nc, "g_knT", [NCORES, AH, 128, T], BF16, "ExternalInput"),
          "krT": dram(nc, "g_krT", [NCORES, 64, T], BF16, "ExternalInput"),
          "V": dram(nc, "g_V", [NCORES, T, AH * 128], BF16, "ExternalInput")}
    m01 = dram(nc, "c_m01", [128, 4, 512], BF16, "ExternalInput")
    biasm = dram(nc, "biasm", [128, NCORES], F32, "ExternalInput")
    oT = dram(nc, "a_oT", [D, T], BF16, "Internal")
    x4 = dram(nc, "x4", [D, T], F32, "ExternalOutput")
    with contextlib.ExitStack() as es:
        P = Prog(nc, es)
        R = alloc_common(P)
        P.push()
        alloc_mla_attn(P, R)
        tks = emit_mla_attn(P, R, A_, G_, m01, biasm, oT)
        P.wait("sp", tks)
        P.pop()
        P.push()
        alloc_normproj(P, R)
        alloc_resid(P, R)
        tks = emit_proj_resid(P, R, oT, wo, x3, x4)
        P.wait("sp", tks)
        P.pop()
        P.emit()
    return nc


def lay_mla(w_in, q_norm, kv_norm, w_uq, w_ukv, qk_norm, w_out):
    win = np.zeros((D, 1280), np.float32)
    win[:, :1088] = w_in
    uq = w_uq.reshape(512, AH, 192)
    wuq = np.zeros((512, AH, 256), np.float32)
    wuq[:, :, 0:192] = uq
    ukv = w_ukv.reshape(512, AH, 256)
    wkn = np.ascontiguousarray(ukv[:, :, 0:128]).reshape(512, AH * 128)
    wv = np.ascontiguousarray(ukv[:, :, 128:256]).reshape(512, AH * 128)
    lg = np.zeros((128, 12), np.float32)
    lg[:, 0:4] = q_norm.reshape(4, 128).T
    lg[:, 4:8] = kv_norm.reshape(4, 128).T
    lg[:, 8] = qk_norm[0, :128]
    lg[:64, 9] = qk_norm[0, 128:]
    lg[:, 10] = qk_norm[1, :128]
    lg[:64, 11] = qk_norm[1, 128:]
    wv2 = np.ascontiguousarray(wv.reshape(4, 128, NCORES, HL * 128).transpose(2, 1, 0, 3))
    return {"a_wv2": wv2, "a_win": lay_tiles(win, 256), "a_wuq": lay_tiles(wuq.reshape(512, AH * 256), 256),
            "a_wkn": lay_tiles(wkn, 256), "a_wv": lay_tiles(wv, 512), "a_lg": lg, "a_wo": lay_tiles(w_out, 256)}


def host_consts_mla():
    import ml_dtypes
    freqs = (10000.0 ** (-np.arange(0, 64, 2, dtype=np.float32) / 64)).astype(np.float32)
    freq2 = np.concatenate([freqs, freqs]).reshape(64, 1).astype(np.float32)
    PT = np.zeros((64, 64), np.float32)
    for i in range(32):
        PT[i + 32, i] = -1.0
        PT[i, i + 32] = 1.0
    k_ = np.arange(128)[:, None, None]
    m_ = np.arange(4)[None, :, None]
    q_ = np.arange(512)[None, None, :]
    m01 = ((m_ * 128 + k_) <= q_).astype(np.float32).astype(ml_dtypes.bfloat16)
    return {"c_freq": freq2, "c_PT": PT, "c_m01": m01}


_PROGS = {}


def _prog(name, builder):
    if name not in _PROGS:
        _PROGS[name] = builder()
    return _PROGS[name]


def _launch(name, builder, in_maps):
    res = run_bass_kernel_spmd(_prog(name, builder), in_maps, core_ids=list(range(NCORES)))
    return res.results


def _run_ffn(xT, gain, wgu, wd):
    gl, wl = lay_gain(gain), lay_wgu(wgu)
    wdc = np.ascontiguousarray(wd)
    res = _launch("ffn", lambda: build_ffn_prog(T), [{"xin": xT[c], "g": gl, "wgu": wl, "wd": wdc} for c in range(NCORES)])
    return [np.ascontiguousarray(res[c]["xout"]) for c in range(NCORES)]


def _run_mlstm(xT, g2, w_in, gate_bias, head_norm, w_out):
    L = lay_mlstm(w_in, gate_bias, head_norm, w_out)
    C = host_consts()
    g2l = lay_gain(g2)
    resA = _launch("mlA", build_mlstm_A, [dict(x1=xT[c], g2=g2l, m_wfm=L["m_wfm"], m_wtm=L["m_wtm"], m_wg=L["m_wg"],
                                               m_gb=L["m_gb"], m_hng=L["m_hng"], **C) for c in range(NCORES)])
    allst = np.stack([resA[c]["st_out"] for c in range(NCORES)])
    allscal = np.ascontiguousarray(np.stack([resA[c]["scal_out"] for c in range(NCORES)], axis=1))
    ins = []
    for c in range(NCORES):
        m = np.zeros((4, NCORES), np.float32)
        m[:, :c] = 1.0
        d = dict(x1=xT[c], m_wo=L["m_wo"], allst=allst, allscal=allscal, cmask=m, **C)
        for k in ("m_qT", "m_kT", "m_sg", "m_ktok", "m_vtok", "m_arow", "m_csp"):
            d[k] = resA[c][k]
        ins.append(d)
    resB = _launch("mlB", build_mlstm_B, ins)
    return [np.ascontiguousarray(resB[c]["x2"]) for c in range(NCORES)]


def _run_mla(xT, pos, g3, w_in, q_norm, kv_norm, w_uq, w_ukv, qk_norm, w_out):
    L = lay_mla(w_in, q_norm, kv_norm, w_uq, w_ukv, qk_norm, w_out)
    C = host_consts_mla()
    g3l = lay_gain(g3)
    resA = _launch("mlaA", build_mla_A, [dict(x3=xT[c], g3=g3l, a_lg=L["a_lg"], c_freq=C["c_freq"], c_PT=C["c_PT"],
                                              pos=np.ascontiguousarray(pos[:, c * T:(c + 1) * T]), a_win=L["a_win"],
                                              a_wuq=L["a_wuq"], a_wkn=L["a_wkn"], a_wv=L["a_wv"])
                                         for c in range(NCORES)])
    gk = np.stack([resA[c]["a_knT"] for c in range(NCORES)])
    gr = np.stack([resA[c]["a_krT"] for c in range(NCORES)])
    gv = np.stack([resA[c]["a_V"] for c in range(NCORES)])
    ins = []
    for c in range(NCORES):
        bm = np.full((128, NCORES), -30000.0, np.float32)
        bm[:, :c] = 0.0
        d = dict(x3=xT[c], a_wo=L["a_wo"], g_knT=gk, g_krT=gr, g_V=gv, c_m01=C["c_m01"], biasm=bm)
        for k in ("a_qnT", "a_qrT", "a_knT", "a_krT", "a_V"):
            d[k] = resA[c][k]
        ins.append(d)
    resB = _launch("mlaB", build_mla_B, ins)
    return [np.ascontiguousarray(resB[c]["x4"]) for c in range(NCORES)]


def kernel_unfused(x, positions, ffn1_norm, ffn1_w_gate_up, ffn1_w_down, mix_norm, ffn2_norm, ffn2_w_gate_up,
           ffn2_w_down, mlstm_w_in, mlstm_gate_bias, mlstm_head_norm, mlstm_w_out, mla_w_in, mla_q_norm,
           mla_kv_norm, mla_w_uq, mla_w_ukv, mla_qk_norm, mla_w_out):
    f = lambda a: np.asarray(a, dtype=np.float32)
    x = f(x)[0]
    pos = np.asarray(positions).astype(np.int32)
    xT = [np.ascontiguousarray(x[c * T:(c + 1) * T].T) for c in range(NCORES)]
    xT = _run_ffn(xT, f(ffn1_norm)[0], f(ffn1_w_gate_up)[0], f(ffn1_w_down)[0])
    xT = _run_mlstm(xT, f(mix_norm)[0], f(mlstm_w_in)[0], f(mlstm_gate_bias)[0], f(mlstm_head_norm)[0], f(mlstm_w_out)[0])
    xT = _run_ffn(xT, f(ffn2_norm)[0], f(ffn2_w_gate_up)[0], f(ffn2_w_down)[0])
    xT = _run_ffn(xT, f(ffn1_norm)[1], f(ffn1_w_gate_up)[1], f(ffn1_w_down)[1])
    xT = _run_mla(xT, pos, f(mix_norm)[1], f(mla_w_in)[0], f(mla_q_norm)[0], f(mla_kv_norm)[0], f(mla_w_uq)[0],
                  f(mla_w_ukv)[0], f(mla_qk_norm)[0], f(mla_w_out)[0])
    xT = _run_ffn(xT, f(ffn2_norm)[1], f(ffn2_w_gate_up)[1], f(ffn2_w_down)[1])
    out = np.concatenate([s.T for s in xT], axis=0)[None]
    return np.ascontiguousarray(out.astype(np.float32))


def build_fused():
    nc = bass.Bass("TRN2", target_bir_lowering=False)
    EI = "ExternalInput"
    xin = dram(nc, "xin", [D, T], F32, EI)
    pos = dram(nc, "pos", [1, T], I32, EI)
    ffn = []
    for i in range(4):
        ffn.append((dram(nc, f"f{i}_g", [128, KD], F32, EI), dram(nc, f"f{i}_wgu", [NH, 128, KD, 256], F32, EI),
                    dram(nc, f"f{i}_wd", [DFF, D], F32, EI)))
    g2 = dram(nc, "g2", [128, KD], F32, EI)
    m_wfm = dram(nc, "m_wfm", [16, 128, KD, 256], F32, EI)
    m_wtm = dram(nc, "m_wtm", [6, 128, KD, 512], F32, EI)
    m_wg = dram(nc, "m_wg", [128, KD, 8], F32, EI)
    m_gb = dram(nc, "m_gb", [4, 2], F32, EI)
    m_hng = dram(nc, "m_hng", [128, KD], F32, EI)
    m_wo = dram(nc, "m_wo", [8, 128, KD, 256], F32, EI)
    C_ = mlstm_consts(nc)
    cmask = dram(nc, "cmask", [4, NCORES], F32, EI)
    g3 = dram(nc, "g3", [128, KD], F32, EI)
    a_lg = dram(nc, "a_lg", [128, 12], F32, EI)
    c_freq = dram(nc, "c_freq", [64, 1], F32, EI)
    c_PT = dram(nc, "c_PT", [64, 64], F32, EI)
    a_win = dram(nc, "a_win", [5, 128, KD, 256], F32, EI)
    a_wuq = dram(nc, "a_wuq", [16, 128, 4, 256], F32, EI)
    a_wkn = dram(nc, "a_wkn", [8, 128, 4, 256], F32, EI)
    a_wv = dram(nc, "a_wv", [4, 128, 4, 512], F32, EI)
    a_wo = dram(nc, "a_wo", [8, 128, KD, 256], F32, EI)
    c_m01 = dram(nc, "c_m01", [128, 4, 512], BF16, EI)
    biasm = dram(nc, "biasm", [128, NCORES], F32, EI)
    xout = dram(nc, "xout", [D, T], F32, "ExternalOutput")
    IN = "Internal"
    xa = dram(nc, "xa", [D, T], F32, IN)
    xb = dram(nc, "xb", [D, T], F32, IN)
    S_ = mlstm_scratch(nc, IN)
    st_loc = dram(nc, "st_loc", [MH * 128, 2 * 513], F32, IN)
    scal_loc = dram(nc, "scal_loc", [4, 2], F32, IN)
    st_all = nc.dram_tensor("st_all", [NCORES * MH * 128, 2 * 513], F32, kind=IN, addr_space="Local").ap()
    scal_all = nc.dram_tensor("scal_all", [NCORES * 4, 2], F32, kind=IN, addr_space="Local").ap()
    gT = dram(nc, "m_gT", [D, T], BF16, IN)
    A2 = {"qnT": dram(nc, "a_qnT", [AH * 128, T], BF16, IN), "qrT": dram(nc, "a_qrT", [AH * 64, T], BF16, IN),
          "knT": dram(nc, "a_knT", [AH * 128, T], BF16, IN), "krT": dram(nc, "a_krT", [64, T], BF16, IN),
          "V": dram(nc, "a_V", [T, AH * 128], BF16, IN)}
    A_ = {"qnT": A2["qnT"].rearrange("(h d) t -> h d t", h=AH), "qrT": A2["qrT"].rearrange("(h d) t -> h d t", h=AH),
          "knT": A2["knT"].rearrange("(h d) t -> h d t", h=AH), "krT": A2["krT"], "V": A2["V"]}
    g_kn = nc.dram_tensor("g_knT", [NCORES * AH * 128, T], BF16, kind=IN, addr_space="Local").ap()
    g_kr = nc.dram_tensor("g_krT", [NCORES * 64, T], BF16, kind=IN, addr_space="Local").ap()
    g_v = nc.dram_tensor("g_V", [NCORES * T, AH * 128], BF16, kind=IN, addr_space="Local").ap()
    G_ = {"knT": g_kn.rearrange("(c h d) t -> c h d t", c=NCORES, h=AH),
          "krT": g_kr.rearrange("(c r) t -> c r t", c=NCORES),
          "V": g_v.rearrange("(c t) v -> c t v", c=NCORES)}
    oT = dram(nc, "a_oT", [D, T], BF16, IN)
    pos_full = dram(nc, "pos_full", [1, S], I32, EI)
    a_wv2 = dram(nc, "a_wv2", [NCORES, 128, 4, HL * 128], F32, EI)
    L_ = {"cq": dram(nc, "l_cq", [512, T], BF16, IN), "ckv": dram(nc, "l_ckv", [512, T], BF16, IN), "kr": A2["krT"],
          "rt": dram(nc, "l_rt", [128, T], F32, IN)}
    g_rt = nc.dram_tensor("g_rt", [NCORES * 128, T], F32, kind=IN, addr_space="Local").ap()
    g_cq2 = nc.dram_tensor("g_cq", [NCORES * 512, T], BF16, kind=IN, addr_space="Local").ap()
    g_ckv2 = nc.dram_tensor("g_ckv", [NCORES * 512, T], BF16, kind=IN, addr_space="Local").ap()
    Q_ = {"qn": dram(nc, "q_qn", [HL, 128, S], BF16, IN), "qr": dram(nc, "q_qr", [HL, 64, S], BF16, IN),
          "kn": dram(nc, "q_kn", [HL, 128, S], BF16, IN), "V": dram(nc, "q_V", [HL, S, 128], BF16, IN)}
    o_loc = [dram(nc, f"o_loc{h}", [NCORES * 128, T], BF16, IN) for h in range(HL)]
    g_o = [nc.dram_tensor(f"g_o{h}", [NCORES * NCORES * 128, T], BF16, kind=IN, addr_space="Local").ap()
           for h in range(HL)]
    with contextlib.ExitStack() as es:
        P = Prog(nc, es)
        R = alloc_common(P)

        def ffn_stage(i, src, dst):
            P.push()
            alloc_ffn(P, R)
            tks = emit_ffn(P, R, src, dst, ffn[i][0], ffn[i][1], ffn[i][2], T)
            P.wait("sp", tks)
            P.pop()
            return tks

        ffn_stage(0, xin, xa)
        P.push()
        alloc_normproj(P, R, 3)
        alloc_mlstm_proj(P, R)
        P.wait("sp", emit_mlstm_proj(P, R, xa, g2, m_wfm, m_wtm, m_wg, m_gb, m_hng, S_))
        P.pop()
        P.push()
        alloc_mlstm_rec(P, R, "A")
        P.wait("sp", emit_mlstm_rec(P, R, S_, C_, "A",
                                    st_out=st_loc.rearrange("(h p) (a v) -> h p a v", h=MH, a=2), scal_out=scal_loc))
        P.pop()
        cc = P.slot("sl_cc")
        P.collective("AllGather", [st_loc], [st_all], cc)
        P.collective("AllGather", [scal_loc], [scal_all], cc)
        P.barrier()
        P.push()
        alloc_mlstm_rec(P, R, "B")
        P.wait("sp", emit_mlstm_rec(P, R, S_, C_, "B",
                                    allst=st_all.rearrange("(c h p) (a v) -> c h p a v", c=NCORES, h=MH, a=2),
                                    allscal=scal_all.rearrange("(c h) s -> h c s", c=NCORES), mask_d=cmask, gT_d=gT))
        P.pop()
        P.push()
        alloc_normproj(P, R, 3)
        alloc_resid(P, R)
        P.wait("sp", emit_proj_resid(P, R, gT, m_wo, xa, xb))
        P.pop()
        ffn_stage(1, xb, xa)
        ffn_stage(2, xa, xb)
        P.push()
        alloc_normproj(P, R, 3)
        alloc_mla_p1(P, R)
        P.wait("sp", emit_mla_p1(P, R, xb, g3, a_lg, c_freq, c_PT, pos, a_win, L_))
        P.pop()
        P.collective("AllGather", [L_["cq"]], [g_cq2], cc)
        P.collective("AllGather", [L_["ckv"]], [g_ckv2], cc)
        P.collective("AllGather", [L_["kr"]], [g_kr], cc)
        P.collective("AllGather", [L_["rt"]], [g_rt], cc)
        P.barrier()
        P.push()
        alloc_mla_p2(P, R)
        P.wait("sp", emit_mla_p2(P, R, a_lg, c_freq, c_PT, g_rt.rearrange("(c f) t -> c f t", c=NCORES),
                                 g_cq2.rearrange("(c f) t -> c f t", c=NCORES),
                                 g_ckv2.rearrange("(c f) t -> c f t", c=NCORES), a_wuq, a_wkn, a_wv2, Q_))
        P.pop()
        P.push()
        alloc_mla_attn2(P, R)
        gob = [Buf("go0"), Buf("go1")]

        def after_head(hl):
            P.collective("AllGather", [o_loc[hl]], [g_o[hl]], cc, reads=[R.oloc_b[hl]], writes=[gob[hl]])
        P.wait("sp", emit_mla_attn2(P, R, Q_, G_["krT"], c_m01,
                                    [o.rearrange("(c f) t -> c f t", c=NCORES) for o in o_loc], after_head))
        P.pop()
        P.push()
        alloc_normproj(P, R, 3)
        alloc_resid(P, R)
        pid_s = nc.sync.partition_id()
        g_o5 = [g.rearrange("(s d f) t -> s d f t", s=NCORES, d=NCORES) for g in g_o]

        def load_o(t0):
            return [g_o5[hl][a, bass.ds(pid_s, 1), :, t0:t0 + TT].rearrange("d p t -> p d t")
                    for a in range(NCORES) for hl in range(HL)]
        P.wait("sp", emit_proj_resid_v2(P, R, load_o, a_wo, xb, xa))
        P.pop()
        tks = ffn_stage(3, xa, xout)
        P.wait("sp", tks)
        P.emit()
    return nc


def kernel(x, positions, ffn1_norm, ffn1_w_gate_up, ffn1_w_down, mix_norm, ffn2_norm, ffn2_w_gate_up,
                 ffn2_w_down, mlstm_w_in, mlstm_gate_bias, mlstm_head_norm, mlstm_w_out, mla_w_in, mla_q_norm,
                 mla_kv_norm, mla_w_uq, mla_w_ukv, mla_qk_norm, mla_w_out):
    f = lambda a: np.asarray(a, dtype=np.float32)
    x = f(x)[0]
    pos = np.asarray(positions).astype(np.int32)
    common = {}
    order = [(ffn1_norm, ffn1_w_gate_up, ffn1_w_down, 0), (ffn2_norm, ffn2_w_gate_up, ffn2_w_down, 0),
             (ffn1_norm, ffn1_w_gate_up, ffn1_w_down, 1), (ffn2_norm, ffn2_w_gate_up, ffn2_w_down, 1)]
    for i, (g, wgu, wd, l) in enumerate(order):
        common[f"f{i}_g"] = lay_gain(f(g)[l])
        common[f"f{i}_wgu"] = lay_wgu(f(wgu)[l])
        common[f"f{i}_wd"] = np.ascontiguousarray(f(wd)[l])
    common["g2"] = lay_gain(f(mix_norm)[0])
    common.update({k: v for k, v in lay_mlstm(f(mlstm_w_in)[0], f(mlstm_gate_bias)[0], f(mlstm_head_norm)[0],
                                              f(mlstm_w_out)[0]).items()})
    common.update(host_consts())
    common["g3"] = lay_gain(f(mix_norm)[1])
    common.update(lay_mla(f(mla_w_in)[0], f(mla_q_norm)[0], f(mla_kv_norm)[0], f(mla_w_uq)[0], f(mla_w_ukv)[0],
                          f(mla_qk_norm)[0], f(mla_w_out)[0]))
    common.update(host_consts_mla())
    in_maps = []
    for c in range(NCORES):
        d = dict(common)
        d["xin"] = np.ascontiguousarray(x[c * T:(c + 1) * T].T)
        d["pos"] = np.ascontiguousarray(pos[:, c * T:(c + 1) * T])
        d["pos_full"] = np.ascontiguousarray(pos)
        m = np.zeros((4, NCORES), np.float32)
        m[:, :c] = 1.0
        d["cmask"] = m
        bm = np.full((128, NCORES), -30000.0, np.float32)
        bm[:, :c] = 0.0
        d["biasm"] = bm
        in_maps.append(d)
    res = run_bass_kernel_spmd(_prog("fused", build_fused), in_maps, core_ids=list(range(NCORES)))
    out = np.concatenate([res.results[c]["xout"].T for c in range(NCORES)], axis=0)[None]
    return np.ascontiguousarray(out.astype(np.float32))


HL = AH // NCORES
NGT = S // TT


def emit_rope_tables(P, R, pos_ap):
    P.dma("sp", R.posi.t[:], pos_ap.partition_broadcast(64), R.posi.slot, writes=[R.posi.b])
    ang, kk, kki, fx = R.ang, R.kk, R.kki, R.fx
    P.op("dve", lambda e: e.tensor_copy(out=ang.t[:], in_=R.posi.t[:]), reads=[R.posi.b], writes=[ang.b])
    P.op("dve", lambda e: e.tensor_scalar(out=ang.t[:], in0=ang.t[:], scalar1=R.freq.t[:, 0:1], scalar2=None,
                                          op0=ALU.mult), reads=[ang.b, R.freq.b], writes=[ang.b])
    for which, shift in (("sin", 0.0), ("cos", np.pi / 2)):
        dst = R.sin if which == "sin" else R.cos
        P.op("dve", lambda e: e.tensor_scalar(out=kk.t[:], in0=ang.t[:], scalar1=1.0 / TWO_PI, scalar2=0.5,
                                              op0=ALU.mult, op1=ALU.add), reads=[ang.b], writes=[kk.b])
        P.op("dve", lambda e: e.tensor_copy(out=kki.t[:], in_=kk.t[:]), reads=[kk.b], writes=[kki.b])
        P.op("dve", lambda e: e.tensor_copy(out=kk.t[:], in_=kki.t[:]), reads=[kki.b], writes=[kk.b])
        for cw, src in ((CW1, ang), (CW2, dst), (CW3, dst)):
            P.op("dve", lambda e: e.scalar_tensor_tensor(out=dst.t[:], in0=kk.t[:], scalar=-cw, in1=src.t[:],
                                                         op0=ALU.mult, op1=ALU.add), reads=[kk.b, src.b],
                 writes=[dst.b])
        if shift:
            P.op("dve", lambda e: e.tensor_scalar(out=dst.t[:], in0=dst.t[:], scalar1=shift, scalar2=None,
                                                  op0=ALU.add), reads=[dst.b], writes=[dst.b])
        for cmp_, sgn in ((ALU.is_gt, -TWO_PI), (ALU.is_lt, TWO_PI)):
            thr = np.pi if sgn < 0 else -np.pi
            P.op("dve", lambda e: e.tensor_scalar(out=fx.t[:], in0=dst.t[:], scalar1=thr, scalar2=sgn,
                                                  op0=cmp_, op1=ALU.mult), reads=[dst.b], writes=[fx.b])
            P.op("dve", lambda e: e.tensor_tensor(out=dst.t[:], in0=dst.t[:], in1=fx.t[:], op=ALU.add),
                 reads=[dst.b, fx.b], writes=[dst.b])
        P.op("act", lambda e: e.activation(out=dst.t[:], in_=dst.t[:], func=AF.Sin), reads=[dst.b], writes=[dst.b])


def emit_mla_p1(P, R, x3, g3_d, lg_d, freq_d, PT_d, pos_loc, win, L_):
    nxt = R.rot
    tks = []
    for tb_, d_ in ((R.g3, g3_d), (R.lg, lg_d), (R.freq, freq_d), (R.PT, PT_d)):
        P.dma("sp", tb_.t[:], d_, tb_.slot, writes=[tb_.b])
    for tt in range(T // TT):
        t0 = tt * TT
        emit_norm(P, R, x3, t0, R.g3, nxt)
        emit_rope_tables(P, R, pos_loc[0:1, t0:t0 + TT])
        tks.append(P.dma("sp", L_["rt"][0:64, t0:t0 + TT], R.cos.t[:], R.rtst, reads=[R.cos.b]))
        tks.append(P.dma("sp", L_["rt"][64:128, t0:t0 + TT], R.sin.t[:], R.rtst, reads=[R.sin.b]))

        def evac(oc, tb, bank):
            cl = slice(tb * 512, (tb + 1) * 512)
            if oc > 8:
                return
            P.op("act", lambda e: e.activation(out=R.lat_t[:, oc, cl], in_=bank.t[:], func=AF.Copy),
                 reads=[bank.b], writes=[R.lat[oc]])
        emit_proj_fm(P, R, R.xn_t, R.xn, KD, win, 5, TT, R.ps[4:8], evac)
        for grp, dst_t, dst_b, g0, key in ((0, R.cqn_t, R.cqn, 0, "cq"), (1, R.ckvn_t, R.ckvn, 4, "ckv")):
            for tb in range(TT // 512):
                cl = slice(tb * 512, (tb + 1) * 512)
                ssb = R.ps[2 + tb]
                for j in range(4):
                    qq = nxt("qq", R.qq)
                    P.op("act", lambda e: e.activation(out=qq.t[:], in_=R.lat_t[:, grp * 4 + j, cl], func=AF.Square),
                         reads=[R.lat[grp * 4 + j]], writes=[qq.b])
                    P.op("pe", lambda e: e.matmul(ssb.t[:], lhsT=R.ones.t[:], rhs=qq.t[:], start=(j == 0),
                                                  stop=(j == 3)), reads=[qq.b, R.ones.b], writes=[ssb.b])
                qr = nxt("qr", R.qr)
                P.op("act", lambda e: e.activation(out=qr.t[:], in_=ssb.t[:], func=AF.Sqrt, scale=1.0 / 512,
                                                   bias=R.epsb.t[:]), reads=[ssb.b, R.epsb.b], writes=[qr.b])
                P.op("dve", lambda e: e.reciprocal(out=qr.t[:], in_=qr.t[:]), reads=[qr.b], writes=[qr.b])
                for j in range(4):
                    P.op("dve", lambda e: e.scalar_tensor_tensor(
                        out=dst_t[:, j, cl], in0=R.lat_t[:, grp * 4 + j, cl], scalar=R.lg.t[:, g0 + j:g0 + j + 1],
                        in1=qr.t[:], op0=ALU.mult, op1=ALU.mult),
                        reads=[R.lat[grp * 4 + j], R.lg.b, qr.b], writes=[dst_b[j]])
            tks.append(P.dma("sp", L_[key][:, t0:t0 + TT].rearrange("(j p) t -> p j t", p=128), dst_t[:],
                             R.latst[grp], reads=dst_b))
        for tb in range(TT // 512):
            cl = slice(tb * 512, (tb + 1) * 512)
            emit_headnorm_sb(P, R, R.lat_t[0:64, 8, cl], R.lat[8], 64, 11, 1.0, True, cl,
                             L_["kr"][:, t0 + tb * 512:t0 + (tb + 1) * 512], tks)
    return tks


def emit_mla_p2(P, R, lg_d, freq_d, PT_d, g_rt, g_cq, g_ckv, wuq, wkn, wv2, Q_):
    nxt = R.rot
    tks = []
    pid_p = P.nc.gpsimd.partition_id()
    for tb_, d_ in ((R.lg, lg_d), (R.freq, freq_d), (R.PT, PT_d)):
        P.dma("sp", tb_.t[:], d_, tb_.slot, writes=[tb_.b])
    P.op("dve", lambda e: e.memset(R.PT128.t[:], 0.0), writes=[R.PT128.b])
    P.dma("sp", R.PT128.t[0:64, 0:64], PT_d, R.PT128.slot, writes=[R.PT128.b])
    P.dma("pool", R.wv2.t[:], wv2[bass.ds(pid_p, 1)].rearrange("a p k n -> (a p) k n"), R.wv2.slot, writes=[R.wv2.b])
    wuq_l = [wuq[bass.ds(pid_p * HL + i, 1)].rearrange("a p k n -> (a p) k n") for i in range(HL)]
    wkn_l = [wkn[bass.ds(pid_p, 1)].rearrange("a p k n -> (a p) k n")]
    for gt in range(NGT):
        src, c0 = gt // (T // TT), (gt % (T // TT)) * TT
        g0 = gt * TT
        P.dma("sp", R.cqn_t[:], g_cq[src, :, c0:c0 + TT].rearrange("(j p) t -> p j t", p=128), R.latst[0],
              writes=R.cqn)
        P.dma("sp", R.ckvn_t[:], g_ckv[src, :, c0:c0 + TT].rearrange("(j p) t -> p j t", p=128), R.latst[1],
              writes=R.ckvn)
        P.dma("sp", R.cos.t[:], g_rt[src, 0:64, c0:c0 + TT], R.rtst, writes=[R.cos.b])
        P.dma("sp", R.sin.t[:], g_rt[src, 64:128, c0:c0 + TT], R.rtst, writes=[R.sin.b])

        def evac_q(oc, tb, bank):
            hl, isr = oc // 2, oc % 2
            cl = slice(tb * 512, (tb + 1) * 512)
            gc = slice(g0 + tb * 512, g0 + (tb + 1) * 512)
            if not isr:
                emit_headnorm(P, R, bank, 128, 8, ASCALE, False, cl, Q_["qn"][hl, :, gc], tks)
            else:
                emit_headnorm(P, R, bank, 64, 9, ASCALE, True, cl, Q_["qr"][hl, :, gc], tks)
        emit_proj_fm(P, R, R.cqn_t, R.cqn, 4, wuq_l, HL, TT, R.ps[4:8], evac_q)

        def evac_k(oc, tb, bank):
            cl = slice(tb * 512, (tb + 1) * 512)
            gc = slice(g0 + tb * 512, g0 + (tb + 1) * 512)
            emit_headnorm(P, R, bank, 128, 10, 1.0, False, cl, Q_["kn"][oc, :, gc], tks)
        emit_proj_fm(P, R, R.ckvn_t, R.ckvn, 4, wkn_l, 1, TT, R.ps[4:8], evac_k)
        for ts in range(TT // 128):
            bank = nxt("pbank", R.ps[4:8])

            def f(e):
                for k in range(4):
                    e.matmul(bank.t[:, 0:HL * 128], lhsT=R.ckvn_t[:, k, ts * 128:(ts + 1) * 128], rhs=R.wv2.t[:, k, :],
                             start=(k == 0), stop=(k == 3))
            P.op("pe", f, reads=[R.wv2.b] + R.ckvn, writes=[bank.b])
            st = nxt("stg", R.stg)
            P.op("act", lambda e: e.activation(out=st.t[:, 0:HL * 128], in_=bank.t[:, 0:HL * 128], func=AF.Copy),
                 reads=[bank.b], writes=[st.b])
            rows = slice(g0 + ts * 128, g0 + (ts + 1) * 128)
            tks.append(P.dma("sp", Q_["V"][:, rows, :].rearrange("h t v -> t h v"),
                             st.t[:, 0:HL * 128].rearrange("t (h v) -> t h v", h=HL), st.slot, reads=[st.b]))
    return tks


def alloc_mla_p2(P, R):
    R.lg = TB(P.sbuf("lg", [128, 12], F32), "lg", P.slot("sl_lg"))
    R.freq = TB(P.sbuf("freq", [64, 1], F32), "freq", P.slot("sl_freq"))
    R.PT = TB(P.sbuf("PT", [64, 64], F32), "PT", P.slot("sl_PT"))
    R.wgu = [TB(P.sbuf(f"wgu{i}", [128, KD, 256], BF16), f"wgu{i}", P.slot(f"sl_wgu{i}")) for i in range(2)]
    R.wv2 = TB(P.sbuf("wv2", [128, 4, HL * 128], BF16), "wv2", P.slot("sl_wv2"))
    alloc_mla_shared(P, R)


def alloc_mla_shared(P, R):
    R.cqn_t = P.sbuf("cqn", [128, 4, TT], BF16)
    R.cqn = [Buf(f"cqn{j}") for j in range(4)]
    R.ckvn_t = P.sbuf("ckvn", [128, 4, TT], BF16)
    R.ckvn = [Buf(f"ckvn{j}") for j in range(4)]
    R.latst = [P.slot("sl_latst0"), P.slot("sl_latst1")]
    R.posi = TB(P.sbuf("posi", [64, TT], I32), "posi", P.slot("sl_posi"))
    R.ang = TB(P.sbuf("ang", [64, TT], F32), "ang")
    R.kk = TB(P.sbuf("kk", [64, TT], F32), "kk")
    R.kki = TB(P.sbuf("kki", [64, TT], I32), "kki")
    R.fx = TB(P.sbuf("fx", [64, TT], F32), "fx")
    R.cos = TB(P.sbuf("cos", [64, TT], F32), "cos")
    R.sin = TB(P.sbuf("sin", [64, TT], F32), "sin")
    R.qf = [TB(P.sbuf(f"qf{i}", [128, 512], F32), f"qf{i}") for i in range(2)]
    R.qq = [TB(P.sbuf(f"qq{i}", [128, 512], BF16), f"qq{i}") for i in range(2)]
    R.qr = [TB(P.sbuf(f"qr{i}", [128, 512], F32), f"qr{i}") for i in range(2)]
    R.qn2 = [TB(P.sbuf(f"qn2{i}", [128, 512], F32), f"qn2{i}") for i in range(2)]
    R.qt2 = [TB(P.sbuf(f"qt2{i}", [128, 512], F32), f"qt2{i}") for i in range(2)]
    R.stg = [TB(P.sbuf(f"astg{i}", [128, 512], BF16), f"astg{i}", P.slot(f"sl_astg{i}")) for i in range(4)]
    R.epsq = TB(P.sbuf("epsq", [128, 1], F32), "epsq")
    P.op("dve", lambda e: e.memset(R.epsq.t[:], EPS / (ASCALE * ASCALE)), writes=[R.epsq.b])
    R.rtst = P.slot("sl_rtst")
    R.PT128 = TB(P.sbuf("PT128", [128, 128], F32), "PT128", P.slot("sl_PT128"))


def alloc_mla_p1(P, R):
    R.g3 = TB(P.sbuf("g3", [128, KD], F32), "g3", P.slot("sl_g3"))
    R.lg = TB(P.sbuf("lg", [128, 12], F32), "lg", P.slot("sl_lg"))
    R.freq = TB(P.sbuf("freq", [64, 1], F32), "freq", P.slot("sl_freq"))
    R.PT = TB(P.sbuf("PT", [64, 64], F32), "PT", P.slot("sl_PT"))
    R.lat_t = P.sbuf("lat", [128, 9, TT], F32)
    R.lat = [Buf(f"lat{j}") for j in range(9)]
    alloc_mla_shared(P, R)


def alloc_mla_attn2(P, R):
    R.Kn = [TB(P.sbuf(f"Kn{i}", [128, S], BF16), f"Kn{i}", P.slot(f"sl_Kn{i}")) for i in range(HL)]
    R.Vh = [TB(P.sbuf(f"Vh{i}", [128, S // 128, 128], BF16), f"Vh{i}", P.slot(f"sl_Vh{i}")) for i in range(HL)]
    R.Kr = TB(P.sbuf("Kr", [128, S], BF16), "Kr", P.slot("sl_Kr"))
    R.qnb = [TB(P.sbuf(f"qnb{i}", [128, 512], BF16), f"qnb{i}", P.slot(f"sl_qnb{i}")) for i in range(2)]
    R.qrb = [TB(P.sbuf(f"qrb{i}", [128, 512], BF16), f"qrb{i}", P.slot(f"sl_qrb{i}")) for i in range(2)]
    P.op("dve", lambda e: e.memset(R.Kr.t[64:128, :], 0.0), writes=[R.Kr.b])
    for q_ in R.qrb:
        P.op("dve", lambda e: e.memset(q_.t[64:128, :], 0.0), writes=[q_.b])
    R.m01 = TB(P.sbuf("m01", [128, 4, 512], BF16), "m01", P.slot("sl_m01"))
    R.PTt = [TB(P.sbuf(f"PTt{i}", [128, 512], BF16), f"PTt{i}") for i in range(8)]
    R.rd = TB(P.sbuf("rd", [128, 512], F32), "rd")
    R.ost = [TB(P.sbuf(f"ost{i}", [128, 512], BF16), f"ost{i}", P.slot(f"sl_ost{i}")) for i in range(2)]
    R.padd = [TB(P.sbuf(f"padd{i}", [128, 512], BF16), f"padd{i}") for i in range(6)]
    R.oloc_b = [Buf(f"oloc{i}") for i in range(HL)]


def emit_mla_attn2(P, R, Q_, g_kr, m01_d, o_loc, after_head=None):
    nxt = R.rot
    ps = R.ps
    tks = []
    P.dma("sp", R.m01.t[:], m01_d, R.m01.slot, writes=[R.m01.b])
    P.dma("sp", R.Kr.t[0:64, :].rearrange("r (c t) -> r c t", c=NCORES), g_kr.rearrange("c r t -> r c t"), R.Kr.slot,
          writes=[R.Kr.b])
    for hl in range(HL):
        P.dma("sp", R.Kn[hl].t[:], Q_["kn"][hl], R.Kn[hl].slot, writes=[R.Kn[hl].b])
        P.dma("sp", R.Vh[hl].t[:], Q_["V"][hl].rearrange("(i p) v -> p i v", p=128), R.Vh[hl].slot,
              writes=[R.Vh[hl].b])
    it = 0
    for hl in range(HL):
        Kn, Vh = R.Kn[hl], R.Vh[hl]
        for qb in range(S // 512):
            qs_ = slice(qb * 512, (qb + 1) * 512)
            qn, qr = nxt("qnb", R.qnb), nxt("qrb", R.qrb)
            P.dma("sp", qn.t[:], Q_["qn"][hl, :, qs_], qn.slot, writes=[qn.b])
            P.dma("sp", qr.t[0:64, :], Q_["qr"][hl, :, qs_], qr.slot, writes=[qr.b])
            n = 4 * qb + 4
            outp, denp = ps[6], ps[7]
            it += 1
            LOOK = 5
            sbanks = {}

            def issue_s(kt):
                ks = slice(kt * 128, (kt + 1) * 128)
                sb = nxt("sbank6", ps[0:6])
                sbanks[kt] = sb

                def f(e):
                    e.matmul(sb.t[:], lhsT=Kn.t[:, ks], rhs=qn.t[:], start=True, stop=False)
                    e.matmul(sb.t[:], lhsT=R.Kr.t[:, ks], rhs=qr.t[:], start=False, stop=True)
                P.op("pe", f, reads=[Kn.b, R.Kr.b, qn.b, qr.b], writes=[sb.b])
            for kt in range(min(LOOK, n)):
                issue_s(kt)
            pts = {}
            pas = {}

            def issue_pv(j):
                pj = pts[j]
                P.op("pe", lambda e: e.matmul(outp.t[:], lhsT=Vh.t[:, j, :], rhs=pj.t[:], start=(j == 0),
                                              stop=(j == n - 1)), reads=[Vh.b, pj.b], writes=[outp.b])

            def issue_den(j):
                pa = pas.pop(j)
                P.op("pe", lambda e: e.matmul(denp.t[:], lhsT=R.ones.t[:], rhs=pa.t[:], start=(j == 3),
                                              stop=(j == n - 1)), reads=[pa.b, R.ones.b], writes=[denp.b])
            DP, DD = 1, 3
            for kt in range(n + DD):
                if kt < n:
                    sb = sbanks.pop(kt)
                    pt = nxt("PTt", R.PTt)
                    pts[kt] = pt
                    P.op("act", lambda e: e.activation(out=pt.t[:], in_=sb.t[:], func=AF.Exp), reads=[sb.b],
                         writes=[pt.b])
                    m = kt - 4 * qb
                    if m >= 0:
                        P.op("dve", lambda e: e.tensor_tensor(out=pt.t[:], in0=pt.t[:], in1=R.m01.t[:, m, :],
                                                              op=ALU.mult), reads=[pt.b, R.m01.b], writes=[pt.b])
                    if kt % 2 == 1:
                        pa = nxt("padd", R.padd)
                        P.op("dve", lambda e: e.tensor_tensor(out=pa.t[:], in0=pts[kt - 1].t[:], in1=pt.t[:],
                                                              op=ALU.add), reads=[pts[kt - 1].b, pt.b], writes=[pa.b])
                        if kt % 4 == 1:
                            pa_prev = pa
                        else:
                            P.op("dve", lambda e: e.tensor_tensor(out=pa.t[:], in0=pa.t[:], in1=pa_prev.t[:],
                                                                  op=ALU.add), reads=[pa.b, pa_prev.b], writes=[pa.b])
                            pas[kt] = pa
                    if kt + LOOK < n:
                        issue_s(kt + LOOK)
                if 0 <= kt - DP < n:
                    issue_pv(kt - DP)
                if 0 <= kt - DD < n and (kt - DD) % 4 == 3:
                    issue_den(kt - DD)
            P.op("dve", lambda e: e.reciprocal(out=R.rd.t[:], in_=denp.t[:]), reads=[denp.b], writes=[R.rd.b])
            ost = nxt("ost", R.ost)
            P.op("dve", lambda e: e.tensor_tensor(out=ost.t[:], in0=outp.t[:], in1=R.rd.t[:], op=ALU.mult),
                 reads=[outp.b, R.rd.b], writes=[ost.b])
            dest, lc = qb // (T // 512), (qb % (T // 512)) * 512
            tks.append(P.dma("sp", o_loc[hl][dest, :, lc:lc + 512], ost.t[:], ost.slot, reads=[ost.b],
                             writes=[R.oloc_b[hl]]))
        if after_head is not None:
            after_head(hl)
    return tks


def emit_proj_resid_v2(P, R, load_src, wtiles, xin, xout, scale=1.0):
    nxt = R.rot
    tks = []
    for tt in range(T // TT):
        t0 = tt * TT
        for a, ap in enumerate(load_src(t0)):
            P.dma("sp", R.xn_t[:, a:a + 1, :], ap, R.xr[0].slot, writes=R.xn[a:a + 1])

        def evac(oc, tb, bank):
            cols = slice(t0 + tb * 512, t0 + (tb + 1) * 512)
            xr = nxt("xr", R.xr)
            P.dma("sp", xr.t[:], xin[oc * 128:(oc + 1) * 128, cols], xr.slot, writes=[xr.b])
            ob = nxt("osb", R.osb)
            P.op("dve", lambda e: e.scalar_tensor_tensor(out=ob.t[:], in0=bank.t[:], scalar=scale, in1=xr.t[:],
                                                         op0=ALU.mult, op1=ALU.add),
                 reads=[bank.b, xr.b], writes=[ob.b])
            tks.append(P.dma("sp", xout[oc * 128:(oc + 1) * 128, cols], ob.t[:], ob.slot, reads=[ob.b]))
        emit_proj_fm(P, R, R.xn_t, R.xn, KD, wtiles, 8, TT, R.ps[0:8], evac)
    return tks
```

```python
import contextlib
import numpy as np
DBG = 99
import concourse.bass as bass
import concourse.mybir as mybir
from concourse.bass_utils import run_bass_kernel_spmd

F32 = mybir.dt.float32
BF16 = mybir.dt.bfloat16
I32 = mybir.dt.int32
AF = mybir.ActivationFunctionType
ALU = mybir.AluOpType
AX = mybir.AxisListType

NCORES = 8
D = 2048
S = 16384
T = S // NCORES
KD = D // 128
DFF = 5632
NH = DFF // 128
TT = 1024
EPS = 1e-6


class Buf:
    __slots__ = ("w", "r", "name")

    def __init__(self, name=""):
        self.w = None
        self.r = {}
        self.name = name


class Slot:
    def __init__(self, sem):
        self.sem = sem
        self.expected = 0


class _Rec:
    def __init__(self):
        self.calls = []

    def __getattr__(self, name):
        def f(*a, **k):
            self.calls.append((name, a, k))
            return self
        return f


class Prog:
    ENGS = ("pe", "act", "dve", "pool", "sp")

    def __init__(self, nc, es):
        self.nc = nc
        self.es = es
        self.ops = {e: [] for e in self.ENGS}
        self.count = {e: 0 for e in self.ENGS}
        self.waited = {e: {} for e in self.ENGS}
        self.es_base = es
        self.esem = {e: es.enter_context(nc.semaphore("s_" + e)) for e in self.ENGS}
        self.nsem = len(self.ENGS)
        self.slots = set()
        self.scopes = []
        self.free_slots = []
        self.scope_slots = []

    def sem(self, name):
        self.nsem += 1
        self.uid = getattr(self, "uid", 0) + 1
        return self.es_base.enter_context(self.nc.semaphore(f"{name}_{self.uid}"))

    def slot(self, name):
        if self.free_slots:
            sl = self.free_slots.pop()
        else:
            sl = Slot(self.sem(name))
        if self.scope_slots:
            self.scope_slots[-1].append(sl)
        return sl

    def sbuf(self, name, shape, dt):
        self.uid = getattr(self, "uid", 0) + 1
        return self.es.enter_context(self.nc.sbuf_tensor(f"sb{self.uid}_{name}", list(shape), dt))

    def psum(self, name, shape, dt):
        return self.es.enter_context(self.nc.psum_tensor(name, list(shape), dt))

    def _need(self, eng, waits, ticket):
        if ticket is None:
            return
        sem, val = ticket
        if eng == "pe" and sem is self.esem["pe"]:
            return
        w = self.waited[eng]
        if w.get(id(sem), 0) < val:
            w[id(sem)] = val
            waits.append((sem, val))

    def _deps(self, eng, reads, writes):
        waits = []
        for b in reads:
            self._need(eng, waits, b.w)
        for b in writes:
            self._need(eng, waits, b.w)
            for sem_id, (sem, val) in b.r.items():
                self._need(eng, waits, (sem, val))
        return waits

    def _mark(self, tk, reads, writes):
        sem, val = tk
        for b in reads:
            cur = b.r.get(id(sem))
            if cur is None or cur[1] < val:
                b.r[id(sem)] = (sem, val)
        for b in writes:
            b.w = tk
            b.r = {}

    def op(self, eng, fn, reads=(), writes=()):
        waits = self._deps(eng, reads, writes)
        self.count[eng] += 1
        tk = (self.esem[eng], self.count[eng])
        self._mark(tk, reads, writes)
        rec = _Rec()
        fn(rec)
        self.ops[eng].append((waits, rec.calls, (self.esem[eng], 1)))
        return tk

    def dma(self, eng, out, in_, slot, reads=(), writes=(), **kw):
        waits = self._deps(eng, reads, writes)
        slot.expected += 16
        tk = (slot.sem, slot.expected)
        self._mark(tk, reads, writes)
        self.ops[eng].append((waits, [("dma_start", (), dict(out=out, in_=in_, **kw))], (slot.sem, 16)))
        self.slots.add(slot)
        return tk

    def collective(self, kind, ins, outs, slot, reads=(), writes=(), inc=1):
        waits = self._deps("pool", reads, writes)
        slot.expected += inc
        tk = (slot.sem, slot.expected)
        self._mark(tk, reads, writes)
        call = ("collective_compute", (kind, ALU.bypass), dict(replica_groups=[list(range(NCORES))], ins=list(ins),
                                                               outs=list(outs)))
        self.ops["pool"].append((waits, [call], (slot.sem, inc)))
        self.slots.add(slot)
        return tk

    def wait(self, eng, tickets):
        waits = []
        for t in tickets:
            self._need(eng, waits, t)
        if waits:
            self.ops[eng].append((waits, None, None))

    def barrier(self):
        tks = [(self.esem[e], self.count[e]) for e in self.ENGS if self.count[e] > 0]
        tks += [(sl.sem, sl.expected) for sl in self.slots if sl.expected > 0]
        for e in self.ENGS:
            self.wait(e, tks)

    def push(self):
        self.scope_slots.append([])
        self.scopes.append(self.es)
        self.es = contextlib.ExitStack()
        self.es.__enter__()

    def pop(self):
        self.barrier()
        self.es.__exit__(None, None, None)
        self.es = self.scopes.pop()
        self.free_slots.extend(self.scope_slots.pop())

    def emit(self):
        nc = self.nc

        def run(e, ops):
            for waits, fn, inc in ops:
                for sem, val in waits:
                    e.wait_ge(sem, val)
                if fn is not None:
                    ins = None
                    for name, a, k in fn:
                        ins = getattr(e, name)(*a, **k)
                    if inc is not None:
                        ins.then_inc(inc[0], inc[1])

        with nc.Block() as block:
            @block.tensor
            def _(e):
                run(e, self.ops["pe"])

            @block.scalar
            def _(e):
                run(e, self.ops["act"])

            @block.vector
            def _(e):
                run(e, self.ops["dve"])

            @block.gpsimd
            def _(e):
                run(e, self.ops["pool"])

            @block.sync
            def _(e):
                run(e, self.ops["sp"])


class TB:
    def __init__(self, t, name, slot=None):
        self.t = t
        self.b = Buf(name)
        self.slot = slot


class Res:
    pass


def alloc_common(P):
    R = Res()
    R.ps = [TB(P.psum(f"ps{i}", [128, 512], F32), f"ps{i}") for i in range(8)]
    R.ones = TB(P.sbuf("ones_bf", [128, 128], BF16), "ones")
    P.op("dve", lambda e: e.memset(R.ones.t[:], 1.0), writes=[R.ones.b])
    R.epsb = TB(P.sbuf("epsb", [128, 1], F32), "epsb")
    P.op("dve", lambda e: e.memset(R.epsb.t[:], EPS), writes=[R.epsb.b])
    R.onec = TB(P.sbuf("onec", [128, 1], F32), "onec")
    P.op("dve", lambda e: e.memset(R.onec.t[:], 1.0), writes=[R.onec.b])
    R.rot = Rot()
    return R


def alloc_ffn(P, R):
    alloc_normproj(P, R)
    alloc_ffn_rest(P, R)


def alloc_normproj(P, R):
    R.xc = [TB(P.sbuf(f"xc{i}", [128, TT], F32), f"xc{i}", P.slot(f"sl_xc{i}")) for i in range(3)]
    R.sq = [TB(P.sbuf(f"sq{i}", [128, TT], BF16), f"sq{i}") for i in range(2)]
    R.rt = TB(P.sbuf("rtmp", [128, TT], F32), "rtmp")
    R.rstd = TB(P.sbuf("rstd", [128, TT], F32), "rstd")
    R.xn_t = P.sbuf("xn", [128, KD, TT], BF16)
    R.xn = [Buf(f"xn{k}") for k in range(KD)]
    R.wgu = [TB(P.sbuf(f"wgu{i}", [128, KD, 256], BF16), f"wgu{i}", P.slot(f"sl_wgu{i}")) for i in range(2)]
    R.silu = [TB(P.sbuf(f"silu{i}", [128, TT], F32), f"silu{i}") for i in range(2)]


def alloc_resid(P, R):
    R.xr = [TB(P.sbuf(f"xr{i}", [128, 512], F32), f"xr{i}", P.slot(f"sl_xr{i}")) for i in range(3)]
    R.osb = [TB(P.sbuf(f"osb{i}", [128, 512], F32), f"osb{i}", P.slot(f"sl_o{i}")) for i in range(3)]


def alloc_ffn_rest(P, R):
    R.g = TB(P.sbuf("ffn_g", [128, KD], F32), "ffn_g", P.slot("sl_g"))
    alloc_resid(P, R)
    R.act_t = P.sbuf("act", [128, NH, TT], BF16)
    R.act = [Buf(f"act{c}") for c in range(NH)]
    R.wd = [TB(P.sbuf(f"wd{i}", [128, NH // 2, 256], BF16), f"wd{i}", P.slot(f"sl_wd{i}")) for i in range(2)]


def emit_norm(P, R, xin, t0, g, nxt):
    ones = R.ones
    ss = [R.ps[0], R.ps[1]]
    for k in range(KD):
        xb = nxt("xc", R.xc)
        P.dma("sp", xb.t[:], xin[k * 128:(k + 1) * 128, t0:t0 + TT], xb.slot, writes=[xb.b])
        sq = nxt("sq", R.sq)
        P.op("act", lambda e: e.activation(out=sq.t[:], in_=xb.t[:], func=AF.Square), reads=[xb.b], writes=[sq.b])

        def f(e):
            for h in range(2):
                e.matmul(ss[h].t[:], lhsT=ones.t[:], rhs=sq.t[:, h * 512:(h + 1) * 512],
                         start=(k == 0), stop=(k == KD - 1))
        P.op("pe", f, reads=[sq.b, ones.b], writes=[ss[0].b, ss[1].b])
    for h in range(2):
        P.op("act", lambda e: e.activation(out=R.rt.t[:, h * 512:(h + 1) * 512], in_=ss[h].t[:],
                                           func=AF.Sqrt, scale=1.0 / D, bias=R.epsb.t[:]),
             reads=[ss[h].b, R.epsb.b], writes=[R.rt.b])
    P.op("dve", lambda e: e.reciprocal(out=R.rstd.t[:], in_=R.rt.t[:]), reads=[R.rt.b], writes=[R.rstd.b])
    for k in range(KD):
        xb = nxt("xc", R.xc)
        P.dma("sp", xb.t[:], xin[k * 128:(k + 1) * 128, t0:t0 + TT], xb.slot, writes=[xb.b])
        P.op("dve", lambda e: e.scalar_tensor_tensor(
            out=R.xn_t[:, k, :], in0=xb.t[:], scalar=g.t[:, k:k + 1], in1=R.rstd.t[:],
            op0=ALU.mult, op1=ALU.mult), reads=[xb.b, g.b, R.rstd.b], writes=[R.xn[k]])


class Rot:
    def __init__(self):
        self.cnt = {}

    def __call__(self, name, lst):
        i = self.cnt.get(name, 0)
        self.cnt[name] = i + 1
        return lst[i % len(lst)]


def emit_ffn(P, R, xin, xout, g_dram, wgu, wd, ntok):
    ones = R.ones
    out_tk = []
    P.dma("sp", R.g.t[:], g_dram, R.g.slot, writes=[R.g.b])
    wd_v = wd.rearrange("(c p) n -> p c n", p=128)
    nxt = R.rot

    for tt in range(ntok // TT):
        t0 = tt * TT
        emit_norm(P, R, xin, t0, R.g, nxt)
        for c in range(NH):
            w = nxt("wgu", R.wgu)
            P.dma("pool", w.t[:], wgu[c], w.slot, writes=[w.b])
            base = (c % 2) * 4
            gp = [R.ps[base + 0], R.ps[base + 1]]
            up = [R.ps[base + 2], R.ps[base + 3]]
            for which, banks in ((0, gp), (1, up)):
                def f(e, w=w, banks=banks, which=which):
                    for k in range(KD):
                        for h in range(2):
                            ins = e.matmul(banks[h].t[:], lhsT=w.t[:, k, which * 128:(which + 1) * 128],
                                           rhs=R.xn_t[:, k, h * 512:(h + 1) * 512],
                                           start=(k == 0), stop=(k == KD - 1))
                    return ins
                P.op("pe", f, reads=[w.b] + R.xn, writes=[banks[0].b, banks[1].b])
            sl = nxt("silu", R.silu)
            for h in range(2):
                P.op("act", (lambda sl, h, gp: lambda e: e.activation(
                    out=sl.t[:, h * 512:(h + 1) * 512], in_=gp[h].t[:], func=AF.Silu))(sl, h, gp),
                    reads=[gp[h].b], writes=[sl.b])
            for h in range(2):
                P.op("dve", (lambda sl, h, up, c: lambda e: e.tensor_tensor(
                    out=R.act_t[:, c, h * 512:(h + 1) * 512], in0=sl.t[:, h * 512:(h + 1) * 512],
                    in1=up[h].t[:], op=ALU.mult))(sl, h, up, c),
                    reads=[sl.b, up[h].b], writes=[R.act[c]])
        HC = NH // 2
        for dg in range(D // 256):
            base = (dg % 2) * 4
            for hc in range(2):
                w = nxt("wd", R.wd)
                P.dma("pool", w.t[:], wd_v[:, hc * HC:(hc + 1) * HC, dg * 256:(dg + 1) * 256], w.slot, writes=[w.b])

                def f(e, w=w, hc=hc, base=base):
                    for cc in range(HC):
                        c = hc * HC + cc
                        for dd in range(2):
                            for h in range(2):
                                ins = e.matmul(R.ps[base + dd * 2 + h].t[:], lhsT=w.t[:, cc, dd * 128:(dd + 1) * 128],
                                               rhs=R.act_t[:, c, h * 512:(h + 1) * 512],
                                               start=(c == 0), stop=(c == NH - 1))
                    return ins
                P.op("pe", f, reads=[w.b] + R.act[hc * HC:(hc + 1) * HC],
                     writes=[R.ps[base + i].b for i in range(4)])
            for dd in range(2):
                for h in range(2):
                    row = (dg * 2 + dd) * 128
                    xr = nxt("xr", R.xr)
                    P.dma("sp", xr.t[:], xin[row:row + 128, t0 + h * 512:t0 + (h + 1) * 512], xr.slot, writes=[xr.b])
                    ob = nxt("osb", R.osb)
                    pb = R.ps[base + dd * 2 + h]
                    P.op("dve", (lambda ob, pb, xr: lambda e: e.scalar_tensor_tensor(
                        out=ob.t[:], in0=pb.t[:], scalar=0.5, in1=xr.t[:], op0=ALU.mult, op1=ALU.add))(ob, pb, xr),
                        reads=[pb.b, xr.b], writes=[ob.b])
                    tk = P.dma("sp", xout[row:row + 128, t0 + h * 512:t0 + (h + 1) * 512], ob.t[:], ob.slot,
                               reads=[ob.b])
                    out_tk.append(tk)
    return out_tk


def build_ffn_prog(ntok=T):
    nc = bass.Bass("TRN2", target_bir_lowering=False)
    xin = nc.dram_tensor("xin", [D, ntok], F32, kind="ExternalInput").ap()
    g = nc.dram_tensor("g", [128, KD], F32, kind="ExternalInput").ap()
    wgu = nc.dram_tensor("wgu", [NH, 128, KD, 256], F32, kind="ExternalInput").ap()
    wd = nc.dram_tensor("wd", [DFF, D], F32, kind="ExternalInput").ap()
    xout = nc.dram_tensor("xout", [D, ntok], F32, kind="ExternalOutput").ap()
    with contextlib.ExitStack() as es:
        P = Prog(nc, es)
        R = alloc_common(P)
        alloc_ffn(P, R)
        tks = emit_ffn(P, R, xin, xout, g, wgu, wd, ntok)
        P.wait("sp", tks)
        P.emit()
    return nc


def lay_gain(g):
    return np.ascontiguousarray(g.reshape(KD, 128).T)


def lay_wgu(w):
    gate = w[:, :DFF].reshape(KD, 128, NH, 128)
    up = w[:, DFF:].reshape(KD, 128, NH, 128)
    cat = np.concatenate([gate, up], axis=3)
    return np.ascontiguousarray(cat.transpose(2, 1, 0, 3))


def emit_proj_fm(P, R, rhs_t, rhs_bufs, nk, wtiles, ntile, ntok, banks, evac, tok0=0):
    nxt = R.rot
    for ti in range(ntile):
        w = nxt("wgu", R.wgu)
        P.dma("pool", w.t[:, 0:nk, :], wtiles[ti], w.slot, writes=[w.b])
        for j in range(2):
            oc = ti * 2 + j
            for tb in range(ntok // 512):
                bank = nxt("pbank", banks)

                def f(e):
                    for k in range(nk):
                        e.matmul(bank.t[:], lhsT=w.t[:, k, j * 128:(j + 1) * 128],
                                 rhs=rhs_t[:, k, tok0 + tb * 512:tok0 + (tb + 1) * 512],
                                 start=(k == 0), stop=(k == nk - 1))
                P.op("pe", f, reads=[w.b] + list(rhs_bufs), writes=[bank.b])
                evac(oc, tb, bank)


MH = 4
DQK = 256
DV = 512
NT128 = T // 128


def alloc_mlstm_proj(P, R):
    R.g2 = TB(P.sbuf("g2", [128, KD], F32), "g2", P.slot("sl_g2"))
    R.hng = TB(P.sbuf("hng", [128, KD], F32), "hng", P.slot("sl_hng"))
    R.gb = TB(P.sbuf("gb", [4, 2], F32), "gb", P.slot("sl_gb"))
    R.ngb = TB(P.sbuf("ngb", [4, 2], F32), "ngb")
    R.wg = TB(P.sbuf("wg", [128, KD, 8], BF16), "wg", P.slot("sl_wg"))
    R.wtm = [TB(P.sbuf(f"wtm{i}", [128, KD, 512], BF16), f"wtm{i}", P.slot(f"sl_wtm{i}")) for i in range(2)]
    R.stg = [TB(P.sbuf(f"stg{i}", [128, 512], BF16), f"stg{i}", P.slot(f"sl_stg{i}")) for i in range(4)]
    R.rows = {n: TB(P.sbuf("row_" + n, [4, T], F32), "row_" + n, P.slot("sl_row_" + n)) for n in ("ig", "sp")}
    R.onesrow = TB(P.sbuf("onesrow", [4, T], F32), "onesrow")
    P.op("dve", lambda e: e.memset(R.onesrow.t[:], 1.0), writes=[R.onesrow.b])


def emit_mlstm_proj(P, R, x1, g2_d, wfm, wtm, wg_d, gb_d, hng_d, S_):
    nxt = R.rot
    P.dma("sp", R.g2.t[:], g2_d, R.g2.slot, writes=[R.g2.b])
    P.dma("sp", R.hng.t[:], hng_d, R.hng.slot, writes=[R.hng.b])
    P.dma("sp", R.gb.t[:], gb_d, R.gb.slot, writes=[R.gb.b])
    P.dma("pool", R.wg.t[:], wg_d, R.wg.slot, writes=[R.wg.b])
    P.op("dve", lambda e: e.tensor_scalar(out=R.ngb.t[:], in0=R.gb.t[:], scalar1=-1.0, scalar2=None, op0=ALU.mult),
         reads=[R.gb.b], writes=[R.ngb.b])
    tks = []
    for tt in range(T // TT):
        t0 = tt * TT
        emit_norm(P, R, x1, t0, R.g2, nxt)

        def evac(oc, tb, bank):
            st = nxt("stg", R.stg)
            cols = slice(t0 + tb * 512, t0 + (tb + 1) * 512)
            if oc < 8:
                P.op("act", lambda e: e.activation(out=st.t[:], in_=bank.t[:], func=AF.Copy, scale=DQK ** -0.5),
                     reads=[bank.b], writes=[st.b])
                dst = S_["qT"][oc * 128:(oc + 1) * 128, cols]
            elif oc < 16:
                P.op("act", lambda e: e.activation(out=st.t[:], in_=bank.t[:], func=AF.Copy),
                     reads=[bank.b], writes=[st.b])
                dst = S_["kT"][(oc - 8) * 128:(oc - 7) * 128, cols]
            else:
                j = oc - 16
                sl = nxt("silu", R.silu)
                P.op("act", lambda e: e.activation(out=sl.t[:, 0:512], in_=bank.t[:], func=AF.Sigmoid),
                     reads=[bank.b], writes=[sl.b])
                P.op("dve", lambda e: e.tensor_scalar(out=st.t[:], in0=sl.t[:, 0:512], scalar1=R.hng.t[:, j:j + 1],
                                                      scalar2=None, op0=ALU.mult),
                     reads=[sl.b, R.hng.b], writes=[st.b])
                dst = S_["sg"][j * 128:(j + 1) * 128, cols]
            tks.append(P.dma("sp", dst, st.t[:], st.slot, reads=[st.b]))
        emit_proj_fm(P, R, R.xn_t, R.xn, KD, wfm, 16, TT, R.ps[2:8], evac)

        for tb in range(TT // 512):
            cols = slice(t0 + tb * 512, t0 + (tb + 1) * 512)
            for which in range(2):
                bank = nxt("pbank", R.ps[2:8])

                def f(e):
                    for k in range(KD):
                        e.matmul(bank.t[0:4, :], lhsT=R.wg.t[:, k, which * 4:(which + 1) * 4],
                                 rhs=R.xn_t[:, k, tb * 512:(tb + 1) * 512], start=(k == 0), stop=(k == KD - 1))
                P.op("pe", f, reads=[R.wg.b] + R.xn, writes=[bank.b])
                if which == 0:
                    P.op("dve", lambda e: e.tensor_scalar(out=R.rows["ig"].t[:, cols], in0=bank.t[0:4, :],
                                                          scalar1=R.gb.t[:, 0:1], scalar2=None, op0=ALU.add),
                         reads=[bank.b, R.gb.b], writes=[R.rows["ig"].b])
                else:
                    P.op("act", lambda e: e.activation(out=R.rows["sp"].t[:, cols], in_=bank.t[0:4, :], func=AF.Exp,
                                                       scale=-1.0, bias=R.ngb.t[:, 1:2]),
                         reads=[bank.b, R.ngb.b], writes=[R.rows["sp"].b])
                    P.op("act", lambda e: e.activation(out=R.rows["sp"].t[:, cols], in_=R.rows["sp"].t[:, cols],
                                                       func=AF.Ln, bias=R.onec.t[0:4, :]),
                         reads=[R.rows["sp"].b, R.onec.b], writes=[R.rows["sp"].b])
        for grp in range(6):
            w = nxt("wtm", R.wtm)
            P.dma("pool", w.t[:], wtm[grp], w.slot, writes=[w.b])
            for ts in range(TT // 128):
                bank = nxt("pbank", R.ps[2:8])

                def f(e):
                    for k in range(KD):
                        e.matmul(bank.t[:], lhsT=R.xn_t[:, k, ts * 128:(ts + 1) * 128], rhs=w.t[:, k, :],
                                 start=(k == 0), stop=(k == KD - 1))
                P.op("pe", f, reads=[w.b] + R.xn, writes=[bank.b])
                st = nxt("stg", R.stg)
                P.op("act", lambda e: e.activation(out=st.t[:], in_=bank.t[:], func=AF.Copy),
                     reads=[bank.b], writes=[st.b])
                rows = slice(t0 + ts * 128, t0 + (ts + 1) * 128)
                if grp < 2:
                    dst = S_["ktok"][rows, grp * 512:(grp + 1) * 512]
                else:
                    dst = S_["vtok"][rows, (grp - 2) * 512:(grp - 1) * 512]
                tks.append(P.dma("sp", dst, st.t[:], st.slot, reads=[st.b]))
    ig, sp = R.rows["ig"], R.rows["sp"]
    P.op("dve", lambda e: e.tensor_tensor_scan(out=sp.t[:], data0=R.onesrow.t[0:4, :], data1=sp.t[:], initial=0.0,
                                               op0=ALU.mult, op1=ALU.add),
         reads=[sp.b, R.onesrow.b], writes=[sp.b])
    P.op("dve", lambda e: e.tensor_tensor(out=ig.t[:], in0=ig.t[:], in1=sp.t[:], op=ALU.add),
         reads=[ig.b, sp.b], writes=[ig.b])
    tks.append(P.dma("sp", S_["arow"], ig.t[:], ig.slot, reads=[ig.b]))
    tks.append(P.dma("sp", S_["csp"], sp.t[:], sp.slot, reads=[sp.b]))
    return tks


def alloc_mlstm_rec(P, R, phase):
    R.mr = {n: TB(P.sbuf("mr_" + n, [4, T], F32), "mr_" + n, P.slot("sl_mr_" + n))
            for n in (("a", "csp", "M", "wk") if phase == "A" else ("a", "csp", "M", "wk", "ai"))}
    R.Mi = TB(P.sbuf("Mi", [4, 1], F32), "Mi")
    R.nMend = TB(P.sbuf("nMend", [4, NT128], F32), "nMend")
    R.Mprev = TB(P.sbuf("Mprev", [4, NT128], F32), "Mprev")
    R.decay = TB(P.sbuf("decay", [4, NT128], F32), "decay")
    R.cols = TB(P.sbuf("cols", [128, NT128, 8], F32), "cols")
    R.decr = TB(P.sbuf("decr", [128, MH, NT128], F32), "decr")
    R.sel = TB(P.sbuf("sel", [4, MH, 128], F32), "sel", P.slot("sl_sel"))
    R.id4 = TB(P.sbuf("id4", [4, 4], F32), "id4", P.slot("sl_id4"))
    R.Cf = [TB(P.sbuf(f"Cf{h}", [128, 2, 513], F32), f"Cf{h}", P.slot(f"sl_Cf{h}")) for h in range(MH)]
    R.Cb = [TB(P.sbuf(f"Cb{h}", [128, 2, 513], BF16), f"Cb{h}") for h in range(MH)]
    R.ktok = [TB(P.sbuf(f"ktok{i}", [128, MH, DQK], BF16), f"ktok{i}", P.slot(f"sl_ktok{i}")) for i in range(2)]
    R.vtok = [TB(P.sbuf(f"vtok{i}", [128, MH, DV], BF16), f"vtok{i}", P.slot(f"sl_vtok{i}")) for i in range(2)]
    R.kw = [TB(P.sbuf(f"kw{i}", [128, MH, DQK], BF16), f"kw{i}") for i in range(2)]
    R.onecb = TB(P.sbuf("onecb", [128, 1], BF16), "onecb")
    P.op("dve", lambda e: e.memset(R.onecb.t[:], 1.0), writes=[R.onecb.b])
    R.sc2 = TB(P.sbuf("sc2", [4, 2], F32), "sc2", P.slot("sl_sc2"))
    R.psC = [R.ps[4], R.ps[5]]
    R.psn = TB(R.ps[6].t, "psn")
    R.denp = TB(R.ps[6].t, "denp")
    R.onesrow = TB(P.sbuf("onesrow8", [4, 8], F32), "onesrow8")
    P.op("dve", lambda e: e.memset(R.onesrow.t[:], 1.0), writes=[R.onesrow.b])
    if phase == "B":
        R.maskneg = TB(P.sbuf("maskneg", [128, 128], BF16), "maskneg", P.slot("sl_mneg"))
        R.ident = TB(P.sbuf("ident", [128, 128], BF16), "ident", P.slot("sl_ident"))
        R.qt = [TB(P.sbuf(f"qt{i}", [128, 2 * MH, 128], BF16), f"qt{i}", P.slot(f"sl_qt{i}")) for i in range(2)]
        R.kt = [TB(P.sbuf(f"kt{i}", [128, 2 * MH, 128], BF16), f"kt{i}", P.slot(f"sl_kt{i}")) for i in range(2)]
        R.qs = [TB(P.sbuf(f"qs{i}", [128, 2 * MH, 128], BF16), f"qs{i}") for i in range(2)]
        R.sgt = [TB(P.sbuf(f"sgt{i}", [128, KD, 128], BF16), f"sgt{i}", P.slot(f"sl_sgt{i}")) for i in range(2)]
        R.Dm = [TB(P.sbuf(f"Dm{i}", [128, 128], F32), f"Dm{i}") for i in range(2)]
        R.wT = [TB(P.sbuf(f"wT{i}", [128, 128], BF16), f"wT{i}") for i in range(2)]
        R.hn = [TB(P.sbuf(f"hn{i}", [128, MH * DV], BF16), f"hn{i}") for i in range(2)]
        R.junk = TB(P.sbuf("junk", [128, DV], F32), "junk")
        R.gts = [TB(P.sbuf(f"gts{i}", [128, KD, 128], BF16), f"gts{i}", P.slot(f"sl_gts{i}")) for i in range(2)]
        R.sm = {n: TB(P.sbuf("sm_" + n, [128, 4], F32), "sm_" + n) for n in ("ss", "da", "r", "t1", "t2", "fac")}
        R.cmb = {n: TB(P.sbuf("cmb_" + n, [4, 8], F32), "cmb_" + n) for n in
                 ("Ms", "Cs", "mask", "Pi", "val", "t1", "t2", "wgt")}
        R.cmb_scal = TB(P.sbuf("cmb_scal", [4, 8, 2], F32), "cmb_scal", P.slot("sl_cmbs"))
        R.cmb_mask = P.slot("sl_cmbm")
        R.cmb1 = {n: TB(P.sbuf("cmb1_" + n, [4, 1], F32), "cmb1_" + n) for n in ("G", "Pc")}
        R.wrep = TB(P.sbuf("wrep", [128, MH, 8], F32), "wrep")
        R.stin = [TB(P.sbuf(f"stin{i}", [128, 2, 513], F32), f"stin{i}", P.slot(f"sl_stin{i}")) for i in range(3)]


def emit_mlstm_rec(P, R, S_, C_, phase, st_out=None, scal_out=None, allst=None, allscal=None, mask_d=None,
                   gT_d=None):
    nxt = R.rot
    mr = R.mr
    ps = R.ps
    tks = []
    a, csp, M, wk = mr["a"], mr["csp"], mr["M"], mr["wk"]
    P.dma("sp", a.t[:], S_["arow"], a.slot, writes=[a.b])
    P.dma("sp", csp.t[:], S_["csp"], csp.slot, writes=[csp.b])
    P.dma("sp", R.sel.t[:], C_["sel"], R.sel.slot, writes=[R.sel.b])
    P.dma("sp", R.id4.t[:], C_["id4"], R.id4.slot, writes=[R.id4.b])
    if phase == "A":
        P.op("dve", lambda e: e.memset(R.Mi.t[:], -1e30), writes=[R.Mi.b])
        for h in range(MH):
            P.op("dve", lambda e: e.memset(R.Cf[h].t[:], 0.0), writes=[R.Cf[h].b])
    else:
        P.dma("sp", R.maskneg.t[:], C_["maskneg"], R.maskneg.slot, writes=[R.maskneg.b])
        P.dma("sp", R.ident.t[:], C_["ident"], R.ident.slot, writes=[R.ident.b])
        c = R.cmb
        P.dma("sp", R.cmb_scal.t[:], allscal, R.cmb_scal.slot, writes=[R.cmb_scal.b])
        P.dma("sp", c["mask"].t[:], mask_d, R.cmb_mask, writes=[c["mask"].b])
        P.op("dve", lambda e: e.tensor_copy(out=c["Ms"].t[:], in_=R.cmb_scal.t[:, :, 0]),
             reads=[R.cmb_scal.b], writes=[c["Ms"].b])
        P.op("dve", lambda e: e.tensor_copy(out=c["Cs"].t[:], in_=R.cmb_scal.t[:, :, 1]),
             reads=[R.cmb_scal.b], writes=[c["Cs"].b])
        P.op("dve", lambda e: e.tensor_tensor_scan(out=c["Pi"].t[:], data0=R.onesrow.t[0:4, 0:8], data1=c["Cs"].t[:],
                                                   initial=0.0, op0=ALU.mult, op1=ALU.add),
             reads=[c["Cs"].b, R.onesrow.b], writes=[c["Pi"].b])
        P.op("dve", lambda e: e.tensor_tensor(out=c["Pi"].t[:], in0=c["Pi"].t[:], in1=c["Cs"].t[:], op=ALU.subtract),
             reads=[c["Pi"].b, c["Cs"].b], writes=[c["Pi"].b])
        P.op("dve", lambda e: e.tensor_tensor(out=c["val"].t[:], in0=c["Ms"].t[:], in1=c["Pi"].t[:], op=ALU.add),
             reads=[c["Ms"].b, c["Pi"].b], writes=[c["val"].b])
        P.op("dve", lambda e: e.tensor_tensor(out=c["t1"].t[:], in0=c["val"].t[:], in1=c["mask"].t[:], op=ALU.mult),
             reads=[c["val"].b, c["mask"].b], writes=[c["t1"].b])
        P.op("dve", lambda e: e.tensor_scalar(out=c["t2"].t[:], in0=c["mask"].t[:], scalar1=1e30, scalar2=-1e30,
                                              op0=ALU.mult, op1=ALU.add), reads=[c["mask"].b], writes=[c["t2"].b])
        P.op("dve", lambda e: e.tensor_tensor(out=c["t1"].t[:], in0=c["t1"].t[:], in1=c["t2"].t[:], op=ALU.add),
             reads=[c["t1"].b, c["t2"].b], writes=[c["t1"].b])
        G, Pc = R.cmb1["G"], R.cmb1["Pc"]
        P.op("dve", lambda e: e.tensor_reduce(out=G.t[:], in_=c["t1"].t[:], axis=AX.X, op=ALU.max),
             reads=[c["t1"].b], writes=[G.b])
        P.op("dve", lambda e: e.tensor_scalar(out=G.t[:], in0=G.t[:], scalar1=0.0, scalar2=None, op0=ALU.max),
             reads=[G.b], writes=[G.b])
        P.op("dve", lambda e: e.tensor_tensor(out=c["t2"].t[:], in0=c["Cs"].t[:], in1=c["mask"].t[:], op=ALU.mult),
             reads=[c["Cs"].b, c["mask"].b], writes=[c["t2"].b])
        P.op("dve", lambda e: e.tensor_reduce(out=Pc.t[:], in_=c["t2"].t[:], axis=AX.X, op=ALU.add),
             reads=[c["t2"].b], writes=[Pc.b])
        P.op("dve", lambda e: e.tensor_tensor(out=R.Mi.t[:], in0=G.t[:], in1=Pc.t[:], op=ALU.subtract),
             reads=[G.b, Pc.b], writes=[R.Mi.b])
        P.op("dve", lambda e: e.tensor_scalar(out=c["wgt"].t[:], in0=c["val"].t[:], scalar1=G.t[:, 0:1], scalar2=0.0,
                                              op0=ALU.subtract, op1=ALU.min), reads=[c["val"].b, G.b],
             writes=[c["wgt"].b])
        P.op("act", lambda e: e.activation(out=c["wgt"].t[:], in_=c["wgt"].t[:], func=AF.Exp),
             reads=[c["wgt"].b], writes=[c["wgt"].b])
        P.op("dve", lambda e: e.tensor_tensor(out=c["wgt"].t[:], in0=c["wgt"].t[:], in1=c["mask"].t[:], op=ALU.mult),
             reads=[c["wgt"].b, c["mask"].b], writes=[c["wgt"].b])

        def f(e):
            for h in range(MH):
                e.matmul(ps[7].t[:, h * 8:(h + 1) * 8], lhsT=R.sel.t[:, h, :], rhs=c["wgt"].t[:], start=True, stop=True)
        P.op("pe", f, reads=[R.sel.b, c["wgt"].b], writes=[ps[7].b])
        P.op("dve", lambda e: e.tensor_copy(out=R.wrep.t[:], in_=ps[7].t[:, 0:32]), reads=[ps[7].b], writes=[R.wrep.b])
        for h in range(MH):
            for cc in range(NCORES):
                sb = nxt("stin", R.stin)
                P.dma("sp", sb.t[:], allst[cc, h], sb.slot, writes=[sb.b])
                if cc == 0:
                    P.op("dve", lambda e: e.tensor_scalar(out=R.Cf[h].t[:], in0=sb.t[:], scalar1=R.wrep.t[:, h, 0:1],
                                                          scalar2=None, op0=ALU.mult),
                         reads=[sb.b, R.wrep.b], writes=[R.Cf[h].b])
                else:
                    for half in range(2):
                        P.op("dve", lambda e: e.scalar_tensor_tensor(
                            out=R.Cf[h].t[:, half, :], in0=sb.t[:, half, :], scalar=R.wrep.t[:, h, cc:cc + 1],
                            in1=R.Cf[h].t[:, half, :], op0=ALU.mult, op1=ALU.add),
                            reads=[sb.b, R.wrep.b, R.Cf[h].b], writes=[R.Cf[h].b])
    for h in range(MH):
        P.op("act", lambda e: e.activation(out=R.Cb[h].t[:], in_=R.Cf[h].t[:], func=AF.Copy),
             reads=[R.Cf[h].b], writes=[R.Cb[h].b])
    P.op("dve", lambda e: e.tensor_tensor_scan(out=M.t[:], data0=a.t[:], data1=a.t[:], initial=R.Mi.t[:, 0:1],
                                               op0=ALU.max, op1=ALU.max), reads=[a.b, R.Mi.b], writes=[M.b])
    if phase == "B":
        P.op("dve", lambda e: e.tensor_tensor(out=csp.t[:], in0=csp.t[:], in1=M.t[:], op=ALU.subtract),
             reads=[csp.b, M.b], writes=[csp.b])
        P.op("act", lambda e: e.activation(out=csp.t[:], in_=csp.t[:], func=AF.Exp), reads=[csp.b], writes=[csp.b])
    else:
        P.op("dve", lambda e: e.tensor_copy(out=R.sc2.t[:, 1:2], in_=csp.t[:, T - 1:T]), reads=[csp.b],
             writes=[R.sc2.b])
    P.op("dve", lambda e: e.tensor_scalar(out=M.t[:], in0=M.t[:], scalar1=-1.0, scalar2=None, op0=ALU.mult),
         reads=[M.b], writes=[M.b])
    nM = M
    Mv = nM.t[:].rearrange("p (c l) -> p c l", l=128)
    P.op("dve", lambda e: e.tensor_copy(out=R.nMend.t[:], in_=Mv[:, :, 127]), reads=[nM.b], writes=[R.nMend.b])
    P.op("dve", lambda e: e.tensor_copy(out=R.Mprev.t[:, 0:1], in_=R.Mi.t[:]), reads=[R.Mi.b], writes=[R.Mprev.b])
    P.op("dve", lambda e: e.tensor_scalar(out=R.Mprev.t[:, 1:NT128], in0=R.nMend.t[:, 0:NT128 - 1], scalar1=-1.0,
                                          scalar2=None, op0=ALU.mult), reads=[R.nMend.b], writes=[R.Mprev.b])
    P.op("dve", lambda e: e.tensor_tensor(out=R.decay.t[:], in0=R.Mprev.t[:], in1=R.nMend.t[:], op=ALU.add),
         reads=[R.Mprev.b, R.nMend.b], writes=[R.decay.b])
    P.op("act", lambda e: e.activation(out=R.decay.t[:], in_=R.decay.t[:], func=AF.Exp),
         reads=[R.decay.b], writes=[R.decay.b])
    if phase == "A":
        P.op("dve", lambda e: e.tensor_scalar(out=R.sc2.t[:, 0:1], in0=R.nMend.t[:, NT128 - 1:NT128], scalar1=-1.0,
                                              scalar2=None, op0=ALU.mult), reads=[R.nMend.b], writes=[R.sc2.b])
        tks.append(P.dma("sp", scal_out, R.sc2.t[:], R.sc2.slot, reads=[R.sc2.b]))
    for i in range(NT128):
        cs = slice(i * 128, (i + 1) * 128)
        P.op("act", lambda e: e.activation(out=wk.t[:, cs], in_=a.t[:, cs], func=AF.Exp, bias=R.nMend.t[:, i:i + 1]),
             reads=[a.b, R.nMend.b], writes=[wk.b])
        if phase == "B":
            P.op("act", lambda e: e.activation(out=mr["ai"].t[:, cs], in_=nM.t[:, cs], func=AF.Exp,
                                               bias=R.Mprev.t[:, i:i + 1]),
                 reads=[nM.b, R.Mprev.b], writes=[mr["ai"].b])

    def f(e):
        for i in range(NT128):
            cs = slice(i * 128, (i + 1) * 128)
            e.matmul(ps[7].t[:, i * 8:i * 8 + 4], lhsT=wk.t[:, cs], rhs=R.id4.t[:], start=True, stop=True)
            if phase == "B":
                e.matmul(ps[7].t[:, i * 8 + 4:i * 8 + 8], lhsT=csp.t[:, cs], rhs=R.id4.t[:], start=True, stop=True)
        for h in range(MH):
            e.matmul(ps[7].t[:, 256 + h * NT128:256 + (h + 1) * NT128], lhsT=R.sel.t[:, h, :], rhs=R.decay.t[:],
                     start=True, stop=True)
    P.op("pe", f, reads=[wk.b, csp.b, R.id4.b, R.sel.b, R.decay.b], writes=[ps[7].b])
    if phase == "A":
        P.op("dve", lambda e: e.memset(R.cols.t[:], 0.0), writes=[R.cols.b])
        P.op("dve", lambda e: e.tensor_copy(out=R.cols.t[:, :, 0:4],
                                            in_=ps[7].t[:, 0:NT128 * 8].rearrange("p (i c) -> p i c", c=8)[:, :, 0:4]),
             reads=[ps[7].b], writes=[R.cols.b])
    else:
        P.op("dve", lambda e: e.tensor_copy(out=R.cols.t[:],
                                            in_=ps[7].t[:, 0:NT128 * 8].rearrange("p (i c) -> p i c", c=8)),
             reads=[ps[7].b], writes=[R.cols.b])
    P.op("dve", lambda e: e.tensor_copy(out=R.decr.t[:],
                                        in_=ps[7].t[:, 256:256 + MH * NT128].rearrange("p (h i) -> p h i", h=MH)),
         reads=[ps[7].b], writes=[R.decr.b])

    for i in range(NT128 if DBG >= 2 else 0):
        cs = slice(i * 128, (i + 1) * 128)
        kt_ = nxt("ktok", R.ktok)
        vt_ = nxt("vtok", R.vtok)
        P.dma("sp", kt_.t[:], S_["ktok"][cs, :].rearrange("t (h d) -> t h d", h=MH), kt_.slot, writes=[kt_.b])
        P.dma("sp", vt_.t[:], S_["vtok"][cs, :].rearrange("t (h d) -> t h d", h=MH), vt_.slot, writes=[vt_.b])
        last = (i == NT128 - 1)
        need_update = (phase == "A") or not last
        kw = nxt("kw", R.kw)
        if need_update:
            for h in range(MH):
                P.op("pool", lambda e: e.tensor_scalar(out=kw.t[:, h, :], in0=kt_.t[:, h, :],
                                                       scalar1=R.cols.t[:, i, h:h + 1], scalar2=1.0,
                                                       op0=ALU.mult, op1=ALU.mult),
                     reads=[kt_.b, R.cols.b], writes=[kw.b])
        if phase == "B":
            qt, kT, qs, sgt = nxt("qt", R.qt), nxt("kt", R.kt), nxt("qs", R.qs), nxt("sgt", R.sgt)
            P.dma("sp", qt.t[:], S_["qT"][:, cs].rearrange("(a p) t -> p a t", p=128), qt.slot, writes=[qt.b])
            P.dma("sp", kT.t[:], S_["kT"][:, cs].rearrange("(a p) t -> p a t", p=128), kT.slot, writes=[kT.b])
            P.dma("sp", sgt.t[:], S_["sg"][:, cs].rearrange("(a p) t -> p a t", p=128), sgt.slot, writes=[sgt.b])
            arep = ps[7]

            def f(e):
                for h in range(MH):
                    e.matmul(arep.t[:, h * 128:(h + 1) * 128], lhsT=R.sel.t[:, h, :], rhs=mr["ai"].t[:, cs],
                             start=True, stop=True)
            P.op("pe", f, reads=[R.sel.b, mr["ai"].b], writes=[arep.b])
            for half in range(2):
                P.op("dve", lambda e: e.tensor_tensor(
                    out=qs.t[:].rearrange("p (h two) t -> p two h t", two=2)[:, half],
                    in0=qt.t[:].rearrange("p (h two) t -> p two h t", two=2)[:, half],
                    in1=arep.t[:].rearrange("p (h t) -> p h t", h=MH), op=ALU.mult),
                    reads=[qt.b, arep.b], writes=[qs.b])
            hn = nxt("hn", R.hn)
            denp = R.denp
            for h in range(MH if DBG >= 21 else 0):
                small = ps[h % 2]
                nump = ps[2 + h % 2]
                Dm, wT = nxt("Dm", R.Dm), nxt("wT", R.wT)

                def f(e):
                    for half in range(2):
                        e.matmul(small.t[:, 0:128], lhsT=kT.t[:, 2 * h + half, :], rhs=qt.t[:, 2 * h + half, :],
                                 start=(half == 0), stop=(half == 1))
                    e.matmul(small.t[:, 128:256], lhsT=a.t[:, cs], rhs=R.sel.t[:, h, :], start=True, stop=False)
                    e.matmul(small.t[:, 128:256], lhsT=R.sel.t[:, h, :], rhs=nM.t[:, cs], start=False, stop=False)
                    e.matmul(small.t[:, 128:256], lhsT=R.ident.t[:], rhs=R.maskneg.t[:], start=False, stop=True)
                P.op("pe", f, reads=[kT.b, qt.b, a.b, nM.b, R.sel.b, R.ident.b, R.maskneg.b], writes=[small.b])
                P.op("act", lambda e: e.activation(out=Dm.t[:], in_=small.t[:, 128:256], func=AF.Exp),
                     reads=[small.b], writes=[Dm.b])
                P.op("dve", lambda e: e.tensor_tensor(out=wT.t[:], in0=small.t[:, 0:128], in1=Dm.t[:], op=ALU.mult),
                     reads=[small.b, Dm.b], writes=[wT.b])

                if DBG < 22:
                    continue

                def f(e):
                    e.matmul(nump.t[:], lhsT=wT.t[:], rhs=vt_.t[:, h, :], start=True, stop=False)
                    for half in range(2):
                        e.matmul(nump.t[:], lhsT=qs.t[:, 2 * h + half, :], rhs=R.Cb[h].t[:, half, 0:512],
                                 start=False, stop=(half == 1))
                    e.matmul(denp.t[:, h:h + 1], lhsT=wT.t[:], rhs=R.onecb.t[:], start=True, stop=False)
                    for half in range(2):
                        e.matmul(denp.t[:, h:h + 1], lhsT=qs.t[:, 2 * h + half, :], rhs=R.Cb[h].t[:, half, 512:513],
                                 start=False, stop=(half == 1))
                P.op("pe", f, reads=[wT.b, vt_.b, qs.b, R.Cb[h].b, R.onecb.b], writes=[nump.b, denp.b])
                if DBG < 23:
                    continue
                P.op("act", lambda e: e.activation(out=R.junk.t[:], in_=nump.t[:], func=AF.Square),
                     reads=[nump.b], writes=[R.junk.b])
                P.op("dve", lambda e: e.reduce_sum(out=R.sm["ss"].t[:, h:h + 1], in_=R.junk.t[:], axis=AX.X),
                     reads=[R.junk.b], writes=[R.sm["ss"].b])
                P.op("dve", lambda e: e.tensor_copy(out=hn.t[:, h * DV:(h + 1) * DV], in_=nump.t[:]),
                     reads=[nump.b], writes=[hn.b])
                if need_update:
                    emit_state_update(P, R, h, i, kw, vt_)
            if DBG < 24:
                continue
            sm = R.sm
            P.op("dve", lambda e: e.tensor_scalar(out=sm["t1"].t[:], in0=denp.t[:, 0:4], scalar1=-1.0, scalar2=None,
                                                  op0=ALU.mult), reads=[denp.b], writes=[sm["t1"].b])
            P.op("dve", lambda e: e.tensor_tensor(out=sm["da"].t[:], in0=denp.t[:, 0:4], in1=sm["t1"].t[:],
                                                  op=ALU.max), reads=[denp.b, sm["t1"].b], writes=[sm["da"].b])
            P.op("dve", lambda e: e.tensor_tensor(out=sm["da"].t[:], in0=sm["da"].t[:], in1=R.cols.t[:, i, 4:8],
                                                  op=ALU.max), reads=[sm["da"].b, R.cols.b], writes=[sm["da"].b])
            P.op("dve", lambda e: e.reciprocal(out=sm["r"].t[:], in_=sm["da"].t[:]), reads=[sm["da"].b],
                 writes=[sm["r"].b])
            P.op("dve", lambda e: e.tensor_tensor(out=sm["t1"].t[:], in0=sm["ss"].t[:], in1=sm["r"].t[:], op=ALU.mult),
                 reads=[sm["ss"].b, sm["r"].b], writes=[sm["t1"].b])
            P.op("dve", lambda e: e.tensor_tensor(out=sm["t1"].t[:], in0=sm["t1"].t[:], in1=sm["r"].t[:], op=ALU.mult),
                 reads=[sm["t1"].b, sm["r"].b], writes=[sm["t1"].b])
            P.op("act", lambda e: e.activation(out=sm["t2"].t[:], in_=sm["t1"].t[:], func=AF.Sqrt, scale=1.0 / DV,
                                               bias=R.epsb.t[:]), reads=[sm["t1"].b, R.epsb.b], writes=[sm["t2"].b])
            P.op("dve", lambda e: e.reciprocal(out=sm["t2"].t[:], in_=sm["t2"].t[:]), reads=[sm["t2"].b],
                 writes=[sm["t2"].b])
            P.op("dve", lambda e: e.tensor_tensor(out=sm["fac"].t[:], in0=sm["t2"].t[:], in1=sm["r"].t[:], op=ALU.mult),
                 reads=[sm["t2"].b, sm["r"].b], writes=[sm["fac"].b])
            for h in range(MH):
                P.op("pool", lambda e: e.tensor_scalar(out=hn.t[:, h * DV:(h + 1) * DV], in0=hn.t[:, h * DV:(h + 1) * DV],
                                                       scalar1=sm["fac"].t[:, h:h + 1], scalar2=1.0,
                                                       op0=ALU.mult, op1=ALU.mult),
                     reads=[hn.b, sm["fac"].b], writes=[hn.b])
            gts = nxt("gts", R.gts)
            for grp in range(2 if DBG >= 3 else 0):
                tp = ps[7]
                tpv = tp.t[:].bitcast(BF16).rearrange("p (j t) -> p j t", t=128)

                def f(e):
                    for jj in range(8):
                        j = grp * 8 + jj
                        e.transpose(tpv[:, jj, :], hn.t[:, j * 128:(j + 1) * 128], R.ident.t[:])
                P.op("pe", f, reads=[hn.b, R.ident.b], writes=[tp.b])
                P.op("dve", lambda e: e.tensor_tensor(out=gts.t[:, grp * 8:(grp + 1) * 8, :], in0=tpv,
                                                      in1=sgt.t[:, grp * 8:(grp + 1) * 8, :], op=ALU.mult),
                     reads=[tp.b, sgt.b], writes=[gts.b])
            tks.append(P.dma("sp", gT_d[:, cs].rearrange("(a p) t -> p a t", p=128), gts.t[:], gts.slot,
                             reads=[gts.b]))
        else:
            for h in range(MH):
                emit_state_update(P, R, h, i, kw, vt_)
    if phase == "A":
        for h in range(MH):
            tks.append(P.dma("sp", st_out[h], R.Cf[h].t[:], R.Cf[h].slot, reads=[R.Cf[h].b]))
    return tks


def emit_state_update(P, R, h, i, kw, vt_):
    def f(e):
        for half in range(2):
            e.matmul(R.psC[half].t[:], lhsT=kw.t[:, h, half * 128:(half + 1) * 128], rhs=vt_.t[:, h, :],
                     start=True, stop=True)
            e.matmul(R.psn.t[:, 8 + half:9 + half], lhsT=kw.t[:, h, half * 128:(half + 1) * 128], rhs=R.onecb.t[:],
                     start=True, stop=True)
    P.op("pe", f, reads=[kw.b, vt_.b, R.onecb.b], writes=[R.psC[0].b, R.psC[1].b, R.psn.b])
    dec = R.decr.t[:, h, i:i + 1]
    for half in range(2):
        P.op("dve", lambda e: e.scalar_tensor_tensor(out=R.Cf[h].t[:, half, 0:512], in0=R.Cf[h].t[:, half, 0:512],
                                                     scalar=dec, in1=R.psC[half].t[:], op0=ALU.mult, op1=ALU.add),
             reads=[R.Cf[h].b, R.decr.b, R.psC[half].b], writes=[R.Cf[h].b])
    P.op("dve", lambda e: e.scalar_tensor_tensor(out=R.Cf[h].t[:, :, 512], in0=R.Cf[h].t[:, :, 512], scalar=dec,
                                                 in1=R.psn.t[:, 8:10], op0=ALU.mult, op1=ALU.add),
         reads=[R.Cf[h].b, R.decr.b, R.psn.b], writes=[R.Cf[h].b])
    P.op("act", lambda e: e.activation(out=R.Cb[h].t[:], in_=R.Cf[h].t[:], func=AF.Copy),
         reads=[R.Cf[h].b], writes=[R.Cb[h].b])


def emit_proj_resid(P, R, src_d, wtiles, xin, xout, scale=1.0):
    nxt = R.rot
    tks = []
    for tt in range(T // TT):
        t0 = tt * TT
        P.dma("sp", R.xn_t[:], src_d[:, t0:t0 + TT].rearrange("(a p) t -> p a t", p=128), R.xr[0].slot,
              writes=R.xn)

        def evac(oc, tb, bank):
            cols = slice(t0 + tb * 512, t0 + (tb + 1) * 512)
            xr = nxt("xr", R.xr)
            P.dma("sp", xr.t[:], xin[oc * 128:(oc + 1) * 128, cols], xr.slot, writes=[xr.b])
            ob = nxt("osb", R.osb)
            P.op("dve", lambda e: e.scalar_tensor_tensor(out=ob.t[:], in0=bank.t[:], scalar=scale, in1=xr.t[:],
                                                         op0=ALU.mult, op1=ALU.add),
                 reads=[bank.b, xr.b], writes=[ob.b])
            tks.append(P.dma("sp", xout[oc * 128:(oc + 1) * 128, cols], ob.t[:], ob.slot, reads=[ob.b]))
        emit_proj_fm(P, R, R.xn_t, R.xn, KD, wtiles, 8, TT, R.ps[0:8], evac)
    return tks


def dram(nc, name, shape, dt, kind):
    return nc.dram_tensor(name, list(shape), dt, kind=kind).ap()


def mlstm_scratch(nc, kind):
    return {
        "qT": dram(nc, "m_qT", [MH * DQK, T], BF16, kind), "kT": dram(nc, "m_kT", [MH * DQK, T], BF16, kind),
        "sg": dram(nc, "m_sg", [D, T], BF16, kind), "ktok": dram(nc, "m_ktok", [T, MH * DQK], BF16, kind),
        "vtok": dram(nc, "m_vtok", [T, D], BF16, kind), "arow": dram(nc, "m_arow", [4, T], F32, kind),
        "csp": dram(nc, "m_csp", [4, T], F32, kind),
    }


def mlstm_consts(nc):
    return {"sel": dram(nc, "c_sel", [4, MH, 128], F32, "ExternalInput"),
            "id4": dram(nc, "c_id4", [4, 4], F32, "ExternalInput"),
            "maskneg": dram(nc, "c_maskneg", [128, 128], BF16, "ExternalInput"),
            "ident": dram(nc, "c_ident", [128, 128], BF16, "ExternalInput")}


def host_consts():
    import ml_dtypes
    sel = np.zeros((4, MH, 128), np.float32)
    for h in range(MH):
        sel[h, h, :] = 1.0
    s_ = np.arange(128)[:, None]
    t_ = np.arange(128)[None, :]
    maskneg = np.where(s_ <= t_, 0.0, -30000.0).astype(ml_dtypes.bfloat16)
    return {"c_sel": sel, "c_id4": np.eye(4, dtype=np.float32), "c_maskneg": maskneg,
            "c_ident": np.eye(128, dtype=np.float32).astype(ml_dtypes.bfloat16)}


def build_mlstm_A():
    nc = bass.Bass("TRN2", target_bir_lowering=False)
    x1 = dram(nc, "x1", [D, T], F32, "ExternalInput")
    g2 = dram(nc, "g2", [128, KD], F32, "ExternalInput")
    wfm = dram(nc, "m_wfm", [16, 128, KD, 256], F32, "ExternalInput")
    wtm = dram(nc, "m_wtm", [6, 128, KD, 512], F32, "ExternalInput")
    wg = dram(nc, "m_wg", [128, KD, 8], F32, "ExternalInput")
    gb = dram(nc, "m_gb", [4, 2], F32, "ExternalInput")
    hng = dram(nc, "m_hng", [128, KD], F32, "ExternalInput")
    S_ = mlstm_scratch(nc, "ExternalOutput")
    C_ = mlstm_consts(nc)
    st_out = dram(nc, "st_out", [MH, 128, 2, 513], F32, "ExternalOutput")
    scal_out = dram(nc, "scal_out", [4, 2], F32, "ExternalOutput")
    with contextlib.ExitStack() as es:
        P = Prog(nc, es)
        R = alloc_common(P)
        P.push()
        alloc_normproj(P, R)
        alloc_mlstm_proj(P, R)
        tks = emit_mlstm_proj(P, R, x1, g2, wfm, wtm, wg, gb, hng, S_)
        P.wait("sp", tks)
        P.pop()
        P.push()
        alloc_mlstm_rec(P, R, "A")
        tks = emit_mlstm_rec(P, R, S_, C_, "A", st_out=st_out, scal_out=scal_out)
        P.wait("sp", tks)
        P.pop()
        P.emit()
    return nc


def build_mlstm_B():
    nc = bass.Bass("TRN2", target_bir_lowering=False)
    x1 = dram(nc, "x1", [D, T], F32, "ExternalInput")
    wo = dram(nc, "m_wo", [8, 128, KD, 256], F32, "ExternalInput")
    S_ = mlstm_scratch(nc, "ExternalInput")
    C_ = mlstm_consts(nc)
    allst = dram(nc, "allst", [NCORES, MH, 128, 2, 513], F32, "ExternalInput")
    allscal = dram(nc, "allscal", [4, NCORES, 2], F32, "ExternalInput")
    mask = dram(nc, "cmask", [4, NCORES], F32, "ExternalInput")
    gT = dram(nc, "m_gT", [D, T], BF16, "Internal")
    x2 = dram(nc, "x2", [D, T], F32, "ExternalOutput")
    with contextlib.ExitStack() as es:
        P = Prog(nc, es)
        R = alloc_common(P)
        P.push()
        alloc_mlstm_rec(P, R, "B")
        tks = emit_mlstm_rec(P, R, S_, C_, "B", allst=allst, allscal=allscal, mask_d=mask, gT_d=gT)
        P.wait("sp", tks)
        P.pop()
        P.push()
        alloc_normproj(P, R)
        alloc_resid(P, R)
        tks = emit_proj_resid(P, R, gT, wo, x1, x2)
        P.wait("sp", tks)
        P.pop()
        P.emit()
    return nc


def lay_tiles(w, ncol):
    K_, N_ = w.shape
    return np.ascontiguousarray(w.reshape(K_ // 128, 128, N_ // ncol, ncol).transpose(2, 1, 0, 3))


def lay_mlstm(w_in, gate_bias, head_norm, w_out):
    q, k, v, o = w_in[:, 0:1024], w_in[:, 1024:2048], w_in[:, 2048:4096], w_in[:, 4096:6144]
    wfm = lay_tiles(np.concatenate([q, k, o], axis=1), 256)
    wtm = lay_tiles(np.concatenate([k, v], axis=1), 512)
    wg = np.ascontiguousarray(w_in[:, 6144:6152].reshape(KD, 128, 8).transpose(1, 0, 2))
    gb = np.ascontiguousarray(gate_bias.T)
    return {"m_wfm": wfm, "m_wtm": wtm, "m_wg": wg, "m_gb": gb, "m_hng": lay_gain(head_norm),
            "m_wo": lay_tiles(w_out, 256)}


AH = 16
TWO_PI = 6.283185307179586
CW1 = 6.28125
CW2 = float(np.float32(TWO_PI - CW1))
CW3 = float(TWO_PI - CW1 - float(np.float32(TWO_PI - CW1)))
ASCALE = 192 ** -0.5


def alloc_mla_proj(P, R):
    R.g3 = TB(P.sbuf("g3", [128, KD], F32), "g3", P.slot("sl_g3"))
    R.lg = TB(P.sbuf("lg", [128, 12], F32), "lg", P.slot("sl_lg"))
    R.freq = TB(P.sbuf("freq", [64, 1], F32), "freq", P.slot("sl_freq"))
    R.PT = TB(P.sbuf("PT", [64, 64], F32), "PT", P.slot("sl_PT"))
    R.lat_t = P.sbuf("lat", [128, 9, TT], F32)
    R.lat = [Buf(f"lat{j}") for j in range(9)]
    R.cqn_t = P.sbuf("cqn", [128, 4, TT], BF16)
    R.cqn = [Buf(f"cqn{j}") for j in range(4)]
    R.ckvn_t = P.sbuf("ckvn", [128, 4, TT], BF16)
    R.ckvn = [Buf(f"ckvn{j}") for j in range(4)]
    R.posi = TB(P.sbuf("posi", [64, TT], I32), "posi", P.slot("sl_posi"))
    R.ang = TB(P.sbuf("ang", [64, TT], F32), "ang")
    R.kk = TB(P.sbuf("kk", [64, TT], F32), "kk")
    R.kki = TB(P.sbuf("kki", [64, TT], I32), "kki")
    R.fx = TB(P.sbuf("fx", [64, TT], F32), "fx")
    R.cos = TB(P.sbuf("cos", [64, TT], F32), "cos")
    R.sin = TB(P.sbuf("sin", [64, TT], F32), "sin")
    R.qf = [TB(P.sbuf(f"qf{i}", [128, 512], F32), f"qf{i}") for i in range(2)]
    R.qq = [TB(P.sbuf(f"qq{i}", [128, 512], BF16), f"qq{i}") for i in range(2)]
    R.qr = [TB(P.sbuf(f"qr{i}", [128, 512], F32), f"qr{i}") for i in range(2)]
    R.qn2 = [TB(P.sbuf(f"qn2{i}", [128, 512], F32), f"qn2{i}") for i in range(2)]
    R.stg = [TB(P.sbuf(f"astg{i}", [128, 512], BF16), f"astg{i}", P.slot(f"sl_astg{i}")) for i in range(4)]
    R.wv = [TB(P.sbuf(f"wv{i}", [128, 4, 512], BF16), f"wv{i}", P.slot(f"sl_wv{i}")) for i in range(2)]


def emit_headnorm(P, R, bank, npart, gcol, scale, rope, cols_local, dst, tks):
    nxt = R.rot
    qq, qr, qn2 = nxt("qq", R.qq), nxt("qr", R.qr), nxt("qn2", R.qn2)
    ssb = nxt("ssb", R.ps[0:2])
    pp = slice(0, npart)
    epsb = R.epsb if scale == 1.0 else R.epsq
    P.op("act", lambda e: e.activation(out=qq.t[pp, :], in_=bank.t[pp, :], func=AF.Square), reads=[bank.b],
         writes=[qq.b])
    P.op("pe", lambda e: e.matmul(ssb.t[:], lhsT=R.ones.t[pp, :], rhs=qq.t[pp, :], start=True, stop=True),
         reads=[qq.b, R.ones.b], writes=[ssb.b])
    P.op("act", lambda e: e.activation(out=qr.t[:], in_=ssb.t[:], func=AF.Ln, scale=1.0 / (npart * scale * scale),
                                       bias=epsb.t[:]), reads=[ssb.b, epsb.b], writes=[qr.b])
    P.op("act", lambda e: e.activation(out=qr.t[:], in_=qr.t[:], func=AF.Exp, scale=-0.5), reads=[qr.b],
         writes=[qr.b])
    st = nxt("stg", R.stg)
    if not rope:
        P.op("dve", lambda e: e.scalar_tensor_tensor(out=st.t[pp, :], in0=bank.t[pp, :],
                                                     scalar=R.lg.t[pp, gcol:gcol + 1], in1=qr.t[pp, :],
                                                     op0=ALU.mult, op1=ALU.mult),
             reads=[bank.b, R.lg.b, qr.b], writes=[st.b])
    else:
        qf, qt2 = nxt("qf", R.qf), nxt("qt2", R.qt2)
        P.op("dve", lambda e: e.scalar_tensor_tensor(out=qn2.t[pp, :], in0=bank.t[pp, :],
                                                     scalar=R.lg.t[pp, gcol:gcol + 1], in1=qr.t[pp, :],
                                                     op0=ALU.mult, op1=ALU.mult),
             reads=[bank.b, R.lg.b, qr.b], writes=[qn2.b])
        rb = nxt("ssb", R.ps[0:2])
        P.op("pe", lambda e: e.matmul(rb.t[pp, :], lhsT=R.PT.t[:], rhs=qn2.t[pp, :], start=True, stop=True),
             reads=[qn2.b, R.PT.b], writes=[rb.b])
        P.op("pool", lambda e: e.tensor_tensor(out=qf.t[pp, :], in0=qn2.t[pp, :], in1=R.cos.t[:, cols_local],
                                               op=ALU.mult), reads=[qn2.b, R.cos.b], writes=[qf.b])
        P.op("dve", lambda e: e.tensor_tensor(out=qt2.t[pp, :], in0=rb.t[pp, :], in1=R.sin.t[:, cols_local],
                                              op=ALU.mult), reads=[rb.b, R.sin.b], writes=[qt2.b])
        P.op("dve", lambda e: e.tensor_tensor(out=st.t[pp, :], in0=qt2.t[pp, :], in1=qf.t[pp, :], op=ALU.add),
             reads=[qt2.b, qf.b], writes=[st.b])
    tks.append(P.dma("sp", dst, st.t[pp, :], st.slot, reads=[st.b]))


def emit_mla_proj(P, R, x3, g3_d, lg_d, freq_d, PT_d, pos_d, win, wuq, wkn, wv, A_):
    nxt = R.rot
    tks = []
    for tb_, d_ in ((R.g3, g3_d), (R.lg, lg_d), (R.freq, freq_d), (R.PT, PT_d)):
        P.dma("sp", tb_.t[:], d_, tb_.slot, writes=[tb_.b])
    for tt in range(T // TT):
        t0 = tt * TT
        emit_norm(P, R, x3, t0, R.g3, nxt)
        P.dma("sp", R.posi.t[:], pos_d[0:1, t0:t0 + TT].partition_broadcast(64), R.posi.slot, writes=[R.posi.b])
        ang, kk, kki, fx = R.ang, R.kk, R.kki, R.fx
        P.op("dve", lambda e: e.tensor_copy(out=ang.t[:], in_=R.posi.t[:]), reads=[R.posi.b], writes=[ang.b])
        P.op("dve", lambda e: e.tensor_scalar(out=ang.t[:], in0=ang.t[:], scalar1=R.freq.t[:, 0:1], scalar2=None,
                                              op0=ALU.mult), reads=[ang.b, R.freq.b], writes=[ang.b])
        for which, shift in (("sin", 0.0), ("cos", np.pi / 2)):
            dst = R.sin if which == "sin" else R.cos
            P.op("dve", lambda e: e.tensor_scalar(out=kk.t[:], in0=ang.t[:], scalar1=1.0 / TWO_PI, scalar2=0.5,
                                                  op0=ALU.mult, op1=ALU.add), reads=[ang.b], writes=[kk.b])
            P.op("dve", lambda e: e.tensor_copy(out=kki.t[:], in_=kk.t[:]), reads=[kk.b], writes=[kki.b])
            P.op("dve", lambda e: e.tensor_copy(out=kk.t[:], in_=kki.t[:]), reads=[kki.b], writes=[kk.b])
            P.op("dve", lambda e: e.scalar_tensor_tensor(out=dst.t[:], in0=kk.t[:], scalar=-CW1, in1=ang.t[:],
                                                         op0=ALU.mult, op1=ALU.add), reads=[kk.b, ang.b],
                 writes=[dst.b])
            P.op("dve", lambda e: e.scalar_tensor_tensor(out=dst.t[:], in0=kk.t[:], scalar=-CW2, in1=dst.t[:],
                                                         op0=ALU.mult, op1=ALU.add), reads=[kk.b, dst.b],
                 writes=[dst.b])
            P.op("dve", lambda e: e.scalar_tensor_tensor(out=dst.t[:], in0=kk.t[:], scalar=-CW3, in1=dst.t[:],
                                                         op0=ALU.mult, op1=ALU.add), reads=[kk.b, dst.b],
                 writes=[dst.b])
            if shift:
                P.op("dve", lambda e: e.tensor_scalar(out=dst.t[:], in0=dst.t[:], scalar1=shift, scalar2=None,
                                                      op0=ALU.add), reads=[dst.b], writes=[dst.b])
            for cmp_, sgn in ((ALU.is_gt, -TWO_PI), (ALU.is_lt, TWO_PI)):
                thr = np.pi if sgn < 0 else -np.pi
                P.op("dve", lambda e: e.tensor_scalar(out=fx.t[:], in0=dst.t[:], scalar1=thr, scalar2=sgn,
                                                      op0=cmp_, op1=ALU.mult), reads=[dst.b], writes=[fx.b])
                P.op("dve", lambda e: e.tensor_tensor(out=dst.t[:], in0=dst.t[:], in1=fx.t[:], op=ALU.add),
                     reads=[dst.b, fx.b], writes=[dst.b])
            P.op("act", lambda e: e.activation(out=dst.t[:], in_=dst.t[:], func=AF.Sin), reads=[dst.b], writes=[dst.b])

        ssq = [R.ps[2], R.ps[3]]

        def evac(oc, tb, bank):
            cl = slice(tb * 512, (tb + 1) * 512)
            if oc > 8:
                return
            P.op("act", lambda e: e.activation(out=R.lat_t[:, oc, cl], in_=bank.t[:], func=AF.Copy),
                 reads=[bank.b], writes=[R.lat[oc]])
        emit_proj_fm(P, R, R.xn_t, R.xn, KD, win, 5, TT, R.ps[4:8], evac)
        for grp, dst_t, dst_b, g0 in ((0, R.cqn_t, R.cqn, 0), (1, R.ckvn_t, R.ckvn, 4)):
            for tb in range(TT // 512):
                cl = slice(tb * 512, (tb + 1) * 512)
                ssb = R.ps[2 + tb]
                for j in range(4):
                    qq = nxt("qq", R.qq)
                    P.op("act", lambda e: e.activation(out=qq.t[:], in_=R.lat_t[:, grp * 4 + j, cl], func=AF.Square),
                         reads=[R.lat[grp * 4 + j]], writes=[qq.b])
                    P.op("pe", lambda e: e.matmul(ssb.t[:], lhsT=R.ones.t[:], rhs=qq.t[:], start=(j == 0),
                                                  stop=(j == 3)), reads=[qq.b, R.ones.b], writes=[ssb.b])
                qr = nxt("qr", R.qr)
                P.op("act", lambda e: e.activation(out=qr.t[:], in_=ssb.t[:], func=AF.Sqrt, scale=1.0 / 512,
                                                   bias=R.epsb.t[:]), reads=[ssb.b, R.epsb.b], writes=[qr.b])
                P.op("dve", lambda e: e.reciprocal(out=qr.t[:], in_=qr.t[:]), reads=[qr.b], writes=[qr.b])
                for j in range(4):
                    P.op("dve", lambda e: e.scalar_tensor_tensor(
                        out=dst_t[:, j, cl], in0=R.lat_t[:, grp * 4 + j, cl], scalar=R.lg.t[:, g0 + j:g0 + j + 1],
                        in1=qr.t[:], op0=ALU.mult, op1=ALU.mult),
                        reads=[R.lat[grp * 4 + j], R.lg.b, qr.b], writes=[dst_b[j]])
        for tb in range(TT // 512):
            cl = slice(tb * 512, (tb + 1) * 512)
            emit_headnorm_sb(P, R, R.lat_t[0:64, 8, cl], R.lat[8], 64, 11, 1.0, True, cl,
                             A_["krT"][:, t0 + tb * 512:t0 + (tb + 1) * 512], tks)

        def evac_q(oc, tb, bank):
            h, isr = oc // 2, oc % 2
            cl = slice(tb * 512, (tb + 1) * 512)
            gc = slice(t0 + tb * 512, t0 + (tb + 1) * 512)
            if not isr:
                emit_headnorm(P, R, bank, 128, 8, ASCALE, False, cl, A_["qnT"][h, :, gc], tks)
            else:
                emit_headnorm(P, R, bank, 64, 9, ASCALE, True, cl, A_["qrT"][h, :, gc], tks)
        emit_proj_fm(P, R, R.cqn_t, R.cqn, 4, wuq, 16, TT, R.ps[4:8], evac_q)

        def evac_k(oc, tb, bank):
            cl = slice(tb * 512, (tb + 1) * 512)
            gc = slice(t0 + tb * 512, t0 + (tb + 1) * 512)
            emit_headnorm(P, R, bank, 128, 10, 1.0, False, cl, A_["knT"][oc, :, gc], tks)
        emit_proj_fm(P, R, R.ckvn_t, R.ckvn, 4, wkn, 8, TT, R.ps[4:8], evac_k)
        for grp in range(4):
            w = nxt("wv", R.wv)
            P.dma("pool", w.t[:], wv[grp], w.slot, writes=[w.b])
            for ts in range(TT // 128):
                bank = nxt("pbank", R.ps[4:8])

                def f(e):
                    for k in range(4):
                        e.matmul(bank.t[:], lhsT=R.ckvn_t[:, k, ts * 128:(ts + 1) * 128], rhs=w.t[:, k, :],
                                 start=(k == 0), stop=(k == 3))
                P.op("pe", f, reads=[w.b] + R.ckvn, writes=[bank.b])
                st = nxt("stg", R.stg)
                P.op("act", lambda e: e.activation(out=st.t[:], in_=bank.t[:], func=AF.Copy), reads=[bank.b],
                     writes=[st.b])
                tks.append(P.dma("sp", A_["V"][t0 + ts * 128:t0 + (ts + 1) * 128, grp * 512:(grp + 1) * 512], st.t[:],
                                 st.slot, reads=[st.b]))
    return tks


def emit_headnorm_sb(P, R, src_ap, src_buf, npart, gcol, scale, rope, cols_local, dst, tks):
    nxt = R.rot
    qf, qq, qr, qn2 = nxt("qf", R.qf), nxt("qq", R.qq), nxt("qr", R.qr), nxt("qn2", R.qn2)
    ssb = nxt("ssb", R.ps[0:2])
    pp = slice(0, npart)
    P.op("act", lambda e: e.activation(out=qq.t[pp, :], in_=src_ap, func=AF.Square), reads=[src_buf], writes=[qq.b])
    P.op("pe", lambda e: e.matmul(ssb.t[:], lhsT=R.ones.t[pp, :], rhs=qq.t[pp, :], start=True, stop=True),
         reads=[qq.b, R.ones.b], writes=[ssb.b])
    P.op("act", lambda e: e.activation(out=qr.t[:], in_=ssb.t[:], func=AF.Sqrt, scale=1.0 / npart, bias=R.epsb.t[:]),
         reads=[ssb.b, R.epsb.b], writes=[qr.b])
    P.op("dve", lambda e: e.reciprocal(out=qr.t[:], in_=qr.t[:]), reads=[qr.b], writes=[qr.b])
    st = nxt("stg", R.stg)
    P.op("dve", lambda e: e.tensor_scalar(out=qn2.t[pp, :], in0=src_ap, scalar1=R.lg.t[pp, gcol:gcol + 1],
                                          scalar2=scale, op0=ALU.mult, op1=ALU.mult),
         reads=[src_buf, R.lg.b], writes=[qn2.b])
    P.op("dve", lambda e: e.tensor_tensor(out=qn2.t[pp, :], in0=qn2.t[pp, :], in1=qr.t[pp, :], op=ALU.mult),
         reads=[qn2.b, qr.b], writes=[qn2.b])
    rb = nxt("ssb", R.ps[0:2])
    P.op("pe", lambda e: e.matmul(rb.t[pp, :], lhsT=R.PT.t[:], rhs=qn2.t[pp, :], start=True, stop=True),
         reads=[qn2.b, R.PT.b], writes=[rb.b])
    P.op("dve", lambda e: e.tensor_tensor(out=qf.t[pp, :], in0=rb.t[pp, :], in1=R.sin.t[:, cols_local], op=ALU.mult),
         reads=[rb.b, R.sin.b], writes=[qf.b])
    P.op("dve", lambda e: e.tensor_tensor(out=qn2.t[pp, :], in0=qn2.t[pp, :], in1=R.cos.t[:, cols_local], op=ALU.mult),
         reads=[qn2.b, R.cos.b], writes=[qn2.b])
    P.op("dve", lambda e: e.tensor_tensor(out=st.t[pp, :], in0=qn2.t[pp, :], in1=qf.t[pp, :], op=ALU.add),
         reads=[qn2.b, qf.b], writes=[st.b])
    tks.append(P.dma("sp", dst, st.t[pp, :], st.slot, reads=[st.b]))


def alloc_mla_attn(P, R):
    R.KnA = [TB(P.sbuf(f"KnA{i}", [128, 4, T], BF16), f"KnA{i}", P.slot(f"sl_KnA{i}")) for i in range(2)]
    R.VA = [TB(P.sbuf(f"VA{i}", [128, 4 * NT128, 128], BF16), f"VA{i}", P.slot(f"sl_VA{i}")) for i in range(2)]
    R.KrA = TB(P.sbuf("KrA", [64, NCORES, T], BF16), "KrA", P.slot("sl_KrA"))
    R.KrO = TB(P.sbuf("KrO", [64, T], BF16), "KrO", P.slot("sl_KrO"))
    R.KnO = TB(P.sbuf("KnO", [128, T], BF16), "KnO", P.slot("sl_KnO"))
    R.VO = TB(P.sbuf("VO", [128, NT128, 128], BF16), "VO", P.slot("sl_VO"))
    R.qn = TB(P.sbuf("qn", [128, T], BF16), "qn", P.slot("sl_qn"))
    R.qrr = TB(P.sbuf("qrr", [64, T], BF16), "qrr", P.slot("sl_qrr"))
    R.m01 = TB(P.sbuf("m01", [128, 4, 512], BF16), "m01", P.slot("sl_m01"))
    R.biasm = TB(P.sbuf("biasm", [128, NCORES], F32), "biasm", P.slot("sl_biasm"))
    R.PTt = [TB(P.sbuf(f"PTt{i}", [128, 512], BF16), f"PTt{i}") for i in range(4)]
    R.rd = TB(P.sbuf("rd", [128, 512], F32), "rd")
    R.ost = [TB(P.sbuf(f"ost{i}", [128, 512], BF16), f"ost{i}", P.slot(f"sl_ost{i}")) for i in range(2)]


def emit_mla_attn(P, R, A_, G_, m01_d, biasm_d, oT_d):
    nxt = R.rot
    ps = R.ps
    tks = []
    P.dma("sp", R.m01.t[:], m01_d, R.m01.slot, writes=[R.m01.b])
    P.dma("sp", R.biasm.t[:], biasm_d, R.biasm.slot, writes=[R.biasm.b])
    P.dma("sp", R.KrA.t[:], G_["krT"].rearrange("c r t -> r c t"), R.KrA.slot, writes=[R.KrA.b])
    P.dma("sp", R.KrO.t[:], A_["krT"], R.KrO.slot, writes=[R.KrO.b])
    it = 0
    for h in range(AH):
        for half in range(2):
            P.dma("sp", R.KnA[half].t[:], G_["knT"][half * 4:(half + 1) * 4, h].rearrange("c d t -> d c t"),
                  R.KnA[half].slot, writes=[R.KnA[half].b])
            P.dma("sp", R.VA[half].t[:],
                  G_["V"][half * 4:(half + 1) * 4, :, h * 128:(h + 1) * 128].rearrange("c (i p) v -> p (c i) v", p=128),
                  R.VA[half].slot, writes=[R.VA[half].b])
        P.dma("sp", R.KnO.t[:], A_["knT"][h], R.KnO.slot, writes=[R.KnO.b])
        P.dma("sp", R.VO.t[:], A_["V"][:, h * 128:(h + 1) * 128].rearrange("(i p) v -> p i v", p=128), R.VO.slot,
              writes=[R.VO.b])
        P.dma("sp", R.qn.t[:], A_["qnT"][h], R.qn.slot, writes=[R.qn.b])
        P.dma("sp", R.qrr.t[:], A_["qrT"][h], R.qrr.slot, writes=[R.qrr.b])
        for qb in range(T // 512):
            qs_ = slice(qb * 512, (qb + 1) * 512)
            tiles = []
            for j in range(NCORES):
                for i in range(NT128):
                    ks = slice(i * 128, (i + 1) * 128)
                    tiles.append(dict(kn=R.KnA[j // 4].t[:, j % 4, ks], knb=R.KnA[j // 4].b,
                                      kr=R.KrA.t[:, j, ks], krb=R.KrA.b,
                                      v=R.VA[j // 4].t[:, (j % 4) * NT128 + i, :], vb=R.VA[j // 4].b,
                                      bias=R.biasm.t[:, j:j + 1], m=None))
            for i in range(4 * (qb + 1)):
                ks = slice(i * 128, (i + 1) * 128)
                m = i - 4 * qb
                tiles.append(dict(kn=R.KnO.t[:, ks], knb=R.KnO.b, kr=R.KrO.t[:, ks], krb=R.KrO.b,
                                  v=R.VO.t[:, i, :], vb=R.VO.b, bias=None, m=(m if m >= 0 else None)))
            outp, denp = ps[4 + it % 2], ps[6 + it % 2]
            it += 1
            n = len(tiles)
            LOOK = 2
            sbanks = {}

            def issue_s(idx):
                tl = tiles[idx]
                sb = nxt("sbank", ps[0:4])
                sbanks[idx] = sb

                def f(e):
                    e.matmul(sb.t[:], lhsT=tl["kn"], rhs=R.qn.t[:, qs_], start=True, stop=False)
                    e.matmul(sb.t[:], lhsT=tl["kr"], rhs=R.qrr.t[:, qs_], start=False, stop=True)
                P.op("pe", f, reads=[tl["knb"], tl["krb"], R.qn.b, R.qrr.b], writes=[sb.b])
            for idx in range(min(LOOK, n)):
                issue_s(idx)
            for idx in range(n):
                tl = tiles[idx]
                sb = sbanks.pop(idx)
                pt = nxt("PTt", R.PTt)
                if tl["bias"] is not None:
                    P.op("act", lambda e: e.activation(out=pt.t[:], in_=sb.t[:], func=AF.Exp, bias=tl["bias"]),
                         reads=[sb.b, R.biasm.b], writes=[pt.b])
                else:
                    P.op("act", lambda e: e.activation(out=pt.t[:], in_=sb.t[:], func=AF.Exp),
                         reads=[sb.b], writes=[pt.b])
                if tl["m"] is not None:
                    P.op("dve", lambda e: e.tensor_tensor(out=pt.t[:], in0=pt.t[:], in1=R.m01.t[:, tl["m"], :],
                                                          op=ALU.mult), reads=[pt.b, R.m01.b], writes=[pt.b])
                if idx + LOOK < n:
                    issue_s(idx + LOOK)

                def f(e):
                    e.matmul(outp.t[:], lhsT=tl["v"], rhs=pt.t[:], start=(idx == 0), stop=(idx == n - 1))
                    e.matmul(denp.t[:], lhsT=R.ones.t[:], rhs=pt.t[:], start=(idx == 0), stop=(idx == n - 1))
                P.op("pe", f, reads=[tl["vb"], pt.b, R.ones.b], writes=[outp.b, denp.b])
            P.op("dve", lambda e: e.reciprocal(out=R.rd.t[:], in_=denp.t[:]), reads=[denp.b], writes=[R.rd.b])
            ost = nxt("ost", R.ost)
            P.op("dve", lambda e: e.tensor_tensor(out=ost.t[:], in0=outp.t[:], in1=R.rd.t[:], op=ALU.mult),
                 reads=[outp.b, R.rd.b], writes=[ost.b])
            tks.append(P.dma("sp", oT_d[h * 128:(h + 1) * 128, qs_], ost.t[:], ost.slot, reads=[ost.b]))
    return tks


def mla_scratch(nc, kind):
    return {"qnT": dram(nc, "a_qnT", [AH, 128, T], BF16, kind), "qrT": dram(nc, "a_qrT", [AH, 64, T], BF16, kind),
            "knT": dram(nc, "a_knT", [AH, 128, T], BF16, kind), "krT": dram(nc, "a_krT", [64, T], BF16, kind),
            "V": dram(nc, "a_V", [T, AH * 128], BF16, kind)}


def build_mla_A():
    nc = bass.Bass("TRN2", target_bir_lowering=False)
    x3 = dram(nc, "x3", [D, T], F32, "ExternalInput")
    g3 = dram(nc, "g3", [128, KD], F32, "ExternalInput")
    lg = dram(nc, "a_lg", [128, 12], F32, "ExternalInput")
    freq = dram(nc, "c_freq", [64, 1], F32, "ExternalInput")
    PT = dram(nc, "c_PT", [64, 64], F32, "ExternalInput")
    pos = dram(nc, "pos", [1, T], I32, "ExternalInput")
    win = dram(nc, "a_win", [5, 128, KD, 256], F32, "ExternalInput")
    wuq = dram(nc, "a_wuq", [16, 128, 4, 256], F32, "ExternalInput")
    wkn = dram(nc, "a_wkn", [8, 128, 4, 256], F32, "ExternalInput")
    wv = dram(nc, "a_wv", [4, 128, 4, 512], F32, "ExternalInput")
    A_ = mla_scratch(nc, "ExternalOutput")
    with contextlib.ExitStack() as es:
        P = Prog(nc, es)
        R = alloc_common(P)
        P.push()
        alloc_normproj(P, R)
        alloc_mla_proj(P, R)
        tks = emit_mla_proj(P, R, x3, g3, lg, freq, PT, pos, win, wuq, wkn, wv, A_)
        P.wait("sp", tks)
        P.pop()
        P.emit()
    return nc


def build_mla_B():
    nc = bass.Bass("TRN2", target_bir_lowering=False)
    x3 = dram(nc, "x3", [D, T], F32, "ExternalInput")
    wo = dram(nc, "a_wo", [8, 128, KD, 256], F32, "ExternalInput")
    A_ = mla_scratch(nc, "ExternalInput")
    G_ = {"knT": dram(nc, "g_knT", [NCORES, AH, 128, T], BF16, "ExternalInput"),
          "krT": dram(nc, "g_krT", [NCORES, 64, T], BF16, "ExternalInput"),
          "V": dram(nc, "g_V", [NCORES, T, AH * 128], BF16, "ExternalInput")}
    m01 = dram(nc, "c_m01", [128, 4, 512], BF16, "ExternalInput")
    biasm = dram(nc, "biasm", [128, NCORES], F32, "ExternalInput")
    oT = dram(nc, "a_oT", [D, T], BF16, "Internal")
    x4 = dram(nc, "x4", [D, T], F32, "ExternalOutput")
    with contextlib.ExitStack() as es:
        P = Prog(nc, es)
        R = alloc_common(P)
        P.push()
        alloc_mla_attn(P, R)
        tks = emit_mla_attn(P, R, A_, G_, m01, biasm, oT)
        P.wait("sp", tks)
        P.pop()
        P.push()
        alloc_normproj(P, R)
        alloc_resid(P, R)
        tks = emit_proj_resid(P, R, oT, wo, x3, x4)
        P.wait("sp", tks)
        P.pop()
        P.emit()
    return nc


def lay_mla(w_in, q_norm, kv_norm, w_uq, w_ukv, qk_norm, w_out):
    win = np.zeros((D, 1280), np.float32)
    win[:, :1088] = w_in
    uq = w_uq.reshape(512, AH, 192)
    wuq = np.zeros((512, AH, 256), np.float32)
    wuq[:, :, 0:192] = uq
    ukv = w_ukv.reshape(512, AH, 256)
    wkn = np.ascontiguousarray(ukv[:, :, 0:128]).reshape(512, AH * 128)
    wv = np.ascontiguousarray(ukv[:, :, 128:256]).reshape(512, AH * 128)
    lg = np.zeros((128, 12), np.float32)
    lg[:, 0:4] = q_norm.reshape(4, 128).T
    lg[:, 4:8] = kv_norm.reshape(4, 128).T
    lg[:, 8] = qk_norm[0, :128]
    lg[:64, 9] = qk_norm[0, 128:]
    lg[:, 10] = qk_norm[1, :128]
    lg[:64, 11] = qk_norm[1, 128:]
    wv2 = np.ascontiguousarray(wv.reshape(4, 128, NCORES, HL * 128).transpose(2, 1, 0, 3))
    return {"a_wv2": wv2, "a_win": lay_tiles(win, 256), "a_wuq": lay_tiles(wuq.reshape(512, AH * 256), 256),
            "a_wkn": lay_tiles(wkn, 256), "a_wv": lay_tiles(wv, 512), "a_lg": lg, "a_wo": lay_tiles(w_out, 256)}


def host_consts_mla():
    import ml_dtypes
    freqs = (10000.0 ** (-np.arange(0, 64, 2, dtype=np.float32) / 64)).astype(np.float32)
    freq2 = np.concatenate([freqs, freqs]).reshape(64, 1).astype(np.float32)
    PT = np.zeros((64, 64), np.float32)
    for i in range(32):
        PT[i + 32, i] = -1.0
        PT[i, i + 32] = 1.0
    k_ = np.arange(128)[:, None, None]
    m_ = np.arange(4)[None, :, None]
    q_ = np.arange(512)[None, None, :]
    m01 = ((m_ * 128 + k_) <= q_).astype(np.float32).astype(ml_dtypes.bfloat16)
    return {"c_freq": freq2, "c_PT": PT, "c_m01": m01}


_PROGS = {}


def _prog(name, builder):
    if name not in _PROGS:
        _PROGS[name] = builder()
    return _PROGS[name]


def _launch(name, builder, in_maps):
    res = run_bass_kernel_spmd(_prog(name, builder), in_maps, core_ids=list(range(NCORES)))
    return res.results


def _run_ffn(xT, gain, wgu, wd):
    gl, wl = lay_gain(gain), lay_wgu(wgu)
    wdc = np.ascontiguousarray(wd)
    res = _launch("ffn", lambda: build_ffn_prog(T), [{"xin": xT[c], "g": gl, "wgu": wl, "wd": wdc} for c in range(NCORES)])
    return [np.ascontiguousarray(res[c]["xout"]) for c in range(NCORES)]


def _run_mlstm(xT, g2, w_in, gate_bias, head_norm, w_out):
    L = lay_mlstm(w_in, gate_bias, head_norm, w_out)
    C = host_consts()
    g2l = lay_gain(g2)
    resA = _launch("mlA", build_mlstm_A, [dict(x1=xT[c], g2=g2l, m_wfm=L["m_wfm"], m_wtm=L["m_wtm"], m_wg=L["m_wg"],
                                               m_gb=L["m_gb"], m_hng=L["m_hng"], **C) for c in range(NCORES)])
    allst = np.stack([resA[c]["st_out"] for c in range(NCORES)])
    allscal = np.ascontiguousarray(np.stack([resA[c]["scal_out"] for c in range(NCORES)], axis=1))
    ins = []
    for c in range(NCORES):
        m = np.zeros((4, NCORES), np.float32)
        m[:, :c] = 1.0
        d = dict(x1=xT[c], m_wo=L["m_wo"], allst=allst, allscal=allscal, cmask=m, **C)
        for k in ("m_qT", "m_kT", "m_sg", "m_ktok", "m_vtok", "m_arow", "m_csp"):
            d[k] = resA[c][k]
        ins.append(d)
    resB = _launch("mlB", build_mlstm_B, ins)
    return [np.ascontiguousarray(resB[c]["x2"]) for c in range(NCORES)]


def _run_mla(xT, pos, g3, w_in, q_norm, kv_norm, w_uq, w_ukv, qk_norm, w_out):
    L = lay_mla(w_in, q_norm, kv_norm, w_uq, w_ukv, qk_norm, w_out)
    C = host_consts_mla()
    g3l = lay_gain(g3)
    resA = _launch("mlaA", build_mla_A, [dict(x3=xT[c], g3=g3l, a_lg=L["a_lg"], c_freq=C["c_freq"], c_PT=C["c_PT"],
                                              pos=np.ascontiguousarray(pos[:, c * T:(c + 1) * T]), a_win=L["a_win"],
                                              a_wuq=L["a_wuq"], a_wkn=L["a_wkn"], a_wv=L["a_wv"])
                                         for c in range(NCORES)])
    gk = np.stack([resA[c]["a_knT"] for c in range(NCORES)])
    gr = np.stack([resA[c]["a_krT"] for c in range(NCORES)])
    gv = np.stack([resA[c]["a_V"] for c in range(NCORES)])
    ins = []
    for c in range(NCORES):
        bm = np.full((128, NCORES), -30000.0, np.float32)
        bm[:, :c] = 0.0
        d = dict(x3=xT[c], a_wo=L["a_wo"], g_knT=gk, g_krT=gr, g_V=gv, c_m01=C["c_m01"], biasm=bm)
        for k in ("a_qnT", "a_qrT", "a_knT", "a_krT", "a_V"):
            d[k] = resA[c][k]
        ins.append(d)
    resB = _launch("mlaB", build_mla_B, ins)
    return [np.ascontiguousarray(resB[c]["x4"]) for c in range(NCORES)]


def kernel_unfused(x, positions, ffn1_norm, ffn1_w_gate_up, ffn1_w_down, mix_norm, ffn2_norm, ffn2_w_gate_up,
           ffn2_w_down, mlstm_w_in, mlstm_gate_bias, mlstm_head_norm, mlstm_w_out, mla_w_in, mla_q_norm,
           mla_kv_norm, mla_w_uq, mla_w_ukv, mla_qk_norm, mla_w_out):
    f = lambda a: np.asarray(a, dtype=np.float32)
    x = f(x)[0]
    pos = np.asarray(positions).astype(np.int32)
    xT = [np.ascontiguousarray(x[c * T:(c + 1) * T].T) for c in range(NCORES)]
    xT = _run_ffn(xT, f(ffn1_norm)[0], f(ffn1_w_gate_up)[0], f(ffn1_w_down)[0])
    xT = _run_mlstm(xT, f(mix_norm)[0], f(mlstm_w_in)[0], f(mlstm_gate_bias)[0], f(mlstm_head_norm)[0], f(mlstm_w_out)[0])
    xT = _run_ffn(xT, f(ffn2_norm)[0], f(ffn2_w_gate_up)[0], f(ffn2_w_down)[0])
    xT = _run_ffn(xT, f(ffn1_norm)[1], f(ffn1_w_gate_up)[1], f(ffn1_w_down)[1])
    xT = _run_mla(xT, pos, f(mix_norm)[1], f(mla_w_in)[0], f(mla_q_norm)[0], f(mla_kv_norm)[0], f(mla_w_uq)[0],
                  f(mla_w_ukv)[0], f(mla_qk_norm)[0], f(mla_w_out)[0])
    xT = _run_ffn(xT, f(ffn2_norm)[1], f(ffn2_w_gate_up)[1], f(ffn2_w_down)[1])
    out = np.concatenate([s.T for s in xT], axis=0)[None]
    return np.ascontiguousarray(out.astype(np.float32))


def build_fused():
    nc = bass.Bass("TRN2", target_bir_lowering=False)
    EI = "ExternalInput"
    xin = dram(nc, "xin", [D, T], F32, EI)
    pos = dram(nc, "pos", [1, T], I32, EI)
    ffn = []
    for i in range(4):
        ffn.append((dram(nc, f"f{i}_g", [128, KD], F32, EI), dram(nc, f"f{i}_wgu", [NH, 128, KD, 256], F32, EI),
                    dram(nc, f"f{i}_wd", [DFF, D], F32, EI)))
    g2 = dram(nc, "g2", [128, KD], F32, EI)
    m_wfm = dram(nc, "m_wfm", [16, 128, KD, 256], F32, EI)
    m_wtm = dram(nc, "m_wtm", [6, 128, KD, 512], F32, EI)
    m_wg = dram(nc, "m_wg", [128, KD, 8], F32, EI)
    m_gb = dram(nc, "m_gb", [4, 2], F32, EI)
    m_hng = dram(nc, "m_hng", [128, KD], F32, EI)
    m_wo = dram(nc, "m_wo", [8, 128, KD, 256], F32, EI)
    C_ = mlstm_consts(nc)
    cmask = dram(nc, "cmask", [4, NCORES], F32, EI)
    g3 = dram(nc, "g3", [128, KD], F32, EI)
    a_lg = dram(nc, "a_lg", [128, 12], F32, EI)
    c_freq = dram(nc, "c_freq", [64, 1], F32, EI)
    c_PT = dram(nc, "c_PT", [64, 64], F32, EI)
    a_win = dram(nc, "a_win", [5, 128, KD, 256], F32, EI)
    a_wuq = dram(nc, "a_wuq", [16, 128, 4, 256], F32, EI)
    a_wkn = dram(nc, "a_wkn", [8, 128, 4, 256], F32, EI)
    a_wv = dram(nc, "a_wv", [4, 128, 4, 512], F32, EI)
    a_wo = dram(nc, "a_wo", [8, 128, KD, 256], F32, EI)
    c_m01 = dram(nc, "c_m01", [128, 4, 512], BF16, EI)
    biasm = dram(nc, "biasm", [128, NCORES], F32, EI)
    xout = dram(nc, "xout", [D, T], F32, "ExternalOutput")
    IN = "Internal"
    xa = dram(nc, "xa", [D, T], F32, IN)
    xb = dram(nc, "xb", [D, T], F32, IN)
    S_ = mlstm_scratch(nc, IN)
    st_loc = dram(nc, "st_loc", [MH * 128, 2 * 513], F32, IN)
    scal_loc = dram(nc, "scal_loc", [4, 2], F32, IN)
    st_all = nc.dram_tensor("st_all", [NCORES * MH * 128, 2 * 513], F32, kind=IN, addr_space="Local").ap()
    scal_all = nc.dram_tensor("scal_all", [NCORES * 4, 2], F32, kind=IN, addr_space="Local").ap()
    gT = dram(nc, "m_gT", [D, T], BF16, IN)
    A2 = {"qnT": dram(nc, "a_qnT", [AH * 128, T], BF16, IN), "qrT": dram(nc, "a_qrT", [AH * 64, T], BF16, IN),
          "knT": dram(nc, "a_knT", [AH * 128, T], BF16, IN), "krT": dram(nc, "a_krT", [64, T], BF16, IN),
          "V": dram(nc, "a_V", [T, AH * 128], BF16, IN)}
    A_ = {"qnT": A2["qnT"].rearrange("(h d) t -> h d t", h=AH), "qrT": A2["qrT"].rearrange("(h d) t -> h d t", h=AH),
          "knT": A2["knT"].rearrange("(h d) t -> h d t", h=AH), "krT": A2["krT"], "V": A2["V"]}
    g_kn = nc.dram_tensor("g_knT", [NCORES * AH * 128, T], BF16, kind=IN, addr_space="Local").ap()
    g_kr = nc.dram_tensor("g_krT", [NCORES * 64, T], BF16, kind=IN, addr_space="Local").ap()
    g_v = nc.dram_tensor("g_V", [NCORES * T, AH * 128], BF16, kind=IN, addr_space="Local").ap()
    G_ = {"knT": g_kn.rearrange("(c h d) t -> c h d t", c=NCORES, h=AH),
          "krT": g_kr.rearrange("(c r) t -> c r t", c=NCORES),
          "V": g_v.rearrange("(c t) v -> c t v", c=NCORES)}
    oT = dram(nc, "a_oT", [D, T], BF16, IN)
    pos_full = dram(nc, "pos_full", [1, S], I32, EI)
    a_wv2 = dram(nc, "a_wv2", [NCORES, 128, 4, HL * 128], F32, EI)
    L_ = {"cq": dram(nc, "l_cq", [512, T], BF16, IN), "ckv": dram(nc, "l_ckv", [512, T], BF16, IN), "kr": A2["krT"],
          "rt": dram(nc, "l_rt", [128, T], F32, IN)}
    g_rt = nc.dram_tensor("g_rt", [NCORES * 128, T], F32, kind=IN, addr_space="Local").ap()
    g_cq2 = nc.dram_tensor("g_cq", [NCORES * 512, T], BF16, kind=IN, addr_space="Local").ap()
    g_ckv2 = nc.dram_tensor("g_ckv", [NCORES * 512, T], BF16, kind=IN, addr_space="Local").ap()
    Q_ = {"qn": dram(nc, "q_qn", [HL, 128, S], BF16, IN), "qr": dram(nc, "q_qr", [HL, 64, S], BF16, IN),
          "kn": dram(nc, "q_kn", [HL, 128, S], BF16, IN), "V": dram(nc, "q_V", [HL, S, 128], BF16, IN)}
    o_loc = [dram(nc, f"o_loc{h}", [NCORES * 128, T], BF16, IN) for h in range(HL)]
    g_o = [nc.dram_tensor(f"g_o{h}", [NCORES * NCORES * 128, T], BF16, kind=IN, addr_space="Local").ap()
           for h in range(HL)]
    with contextlib.ExitStack() as es:
        P = Prog(nc, es)
        R = alloc_common(P)

        def ffn_stage(i, src, dst):
            P.push()
            alloc_ffn(P, R)
            tks = emit_ffn(P, R, src, dst, ffn[i][0], ffn[i][1], ffn[i][2], T)
            P.wait("sp", tks)
            P.pop()
            return tks

        ffn_stage(0, xin, xa)
        P.push()
        alloc_normproj(P, R)
        alloc_mlstm_proj(P, R)
        P.wait("sp", emit_mlstm_proj(P, R, xa, g2, m_wfm, m_wtm, m_wg, m_gb, m_hng, S_))
        P.pop()
        P.push()
        alloc_mlstm_rec(P, R, "A")
        P.wait("sp", emit_mlstm_rec(P, R, S_, C_, "A",
                                    st_out=st_loc.rearrange("(h p) (a v) -> h p a v", h=MH, a=2), scal_out=scal_loc))
        P.pop()
        cc = P.slot("sl_cc")
        P.collective("AllGather", [st_loc], [st_all], cc)
        P.collective("AllGather", [scal_loc], [scal_all], cc)
        P.barrier()
        P.push()
        alloc_mlstm_rec(P, R, "B")
        P.wait("sp", emit_mlstm_rec(P, R, S_, C_, "B",
                                    allst=st_all.rearrange("(c h p) (a v) -> c h p a v", c=NCORES, h=MH, a=2),
                                    allscal=scal_all.rearrange("(c h) s -> h c s", c=NCORES), mask_d=cmask, gT_d=gT))
        P.pop()
        P.push()
        alloc_normproj(P, R)
        alloc_resid(P, R)
        P.wait("sp", emit_proj_resid(P, R, gT, m_wo, xa, xb))
        P.pop()
        ffn_stage(1, xb, xa)
        ffn_stage(2, xa, xb)
        P.push()
        alloc_normproj(P, R)
        alloc_mla_p1(P, R)
        P.wait("sp", emit_mla_p1(P, R, xb, g3, a_lg, c_freq, c_PT, pos, a_win, L_))
        P.pop()
        P.collective("AllGather", [L_["cq"]], [g_cq2], cc)
        P.collective("AllGather", [L_["ckv"]], [g_ckv2], cc)
        P.collective("AllGather", [L_["kr"]], [g_kr], cc)
        P.collective("AllGather", [L_["rt"]], [g_rt], cc)
        P.barrier()
        P.push()
        alloc_mla_p2(P, R)
        P.wait("sp", emit_mla_p2(P, R, a_lg, c_freq, c_PT, g_rt.rearrange("(c f) t -> c f t", c=NCORES),
                                 g_cq2.rearrange("(c f) t -> c f t", c=NCORES),
                                 g_ckv2.rearrange("(c f) t -> c f t", c=NCORES), a_wuq, a_wkn, a_wv2, Q_))
        P.pop()
        P.push()
        alloc_mla_attn2(P, R)
        gob = [Buf("go0"), Buf("go1")]

        def after_head(hl):
            P.collective("AllGather", [o_loc[hl]], [g_o[hl]], cc, reads=[R.oloc_b[hl]], writes=[gob[hl]])
        P.wait("sp", emit_mla_attn2(P, R, Q_, G_["krT"], c_m01,
                                    [o.rearrange("(c f) t -> c f t", c=NCORES) for o in o_loc], after_head))
        P.pop()
        P.push()
        alloc_normproj(P, R)
        alloc_resid(P, R)
        pid_s = nc.sync.partition_id()
        g_o5 = [g.rearrange("(s d f) t -> s d f t", s=NCORES, d=NCORES) for g in g_o]

        def load_o(t0):
            return [g_o5[hl][a, bass.ds(pid_s, 1), :, t0:t0 + TT].rearrange("d p t -> p d t")
                    for a in range(NCORES) for hl in range(HL)]
        P.wait("sp", emit_proj_resid_v2(P, R, load_o, a_wo, xb, xa))
        P.pop()
        tks = ffn_stage(3, xa, xout)
        P.wait("sp", tks)
        P.emit()
    return nc


def kernel(x, positions, ffn1_norm, ffn1_w_gate_up, ffn1_w_down, mix_norm, ffn2_norm, ffn2_w_gate_up,
                 ffn2_w_down, mlstm_w_in, mlstm_gate_bias, mlstm_head_norm, mlstm_w_out, mla_w_in, mla_q_norm,
                 mla_kv_norm, mla_w_uq, mla_w_ukv, mla_qk_norm, mla_w_out):
    f = lambda a: np.asarray(a, dtype=np.float32)
    x = f(x)[0]
    pos = np.asarray(positions).astype(np.int32)
    common = {}
    order = [(ffn1_norm, ffn1_w_gate_up, ffn1_w_down, 0), (ffn2_norm, ffn2_w_gate_up, ffn2_w_down, 0),
             (ffn1_norm, ffn1_w_gate_up, ffn1_w_down, 1), (ffn2_norm, ffn2_w_gate_up, ffn2_w_down, 1)]
    for i, (g, wgu, wd, l) in enumerate(order):
        common[f"f{i}_g"] = lay_gain(f(g)[l])
        common[f"f{i}_wgu"] = lay_wgu(f(wgu)[l])
        common[f"f{i}_wd"] = np.ascontiguousarray(f(wd)[l])
    common["g2"] = lay_gain(f(mix_norm)[0])
    common.update({k: v for k, v in lay_mlstm(f(mlstm_w_in)[0], f(mlstm_gate_bias)[0], f(mlstm_head_norm)[0],
                                              f(mlstm_w_out)[0]).items()})
    common.update(host_consts())
    common["g3"] = lay_gain(f(mix_norm)[1])
    common.update(lay_mla(f(mla_w_in)[0], f(mla_q_norm)[0], f(mla_kv_norm)[0], f(mla_w_uq)[0], f(mla_w_ukv)[0],
                          f(mla_qk_norm)[0], f(mla_w_out)[0]))
    common.update(host_consts_mla())
    in_maps = []
    for c in range(NCORES):
        d = dict(common)
        d["xin"] = np.ascontiguousarray(x[c * T:(c + 1) * T].T)
        d["pos"] = np.ascontiguousarray(pos[:, c * T:(c + 1) * T])
        d["pos_full"] = np.ascontiguousarray(pos)
        m = np.zeros((4, NCORES), np.float32)
        m[:, :c] = 1.0
        d["cmask"] = m
        bm = np.full((128, NCORES), -30000.0, np.float32)
        bm[:, :c] = 0.0
        d["biasm"] = bm
        in_maps.append(d)
    res = run_bass_kernel_spmd(_prog("fused", build_fused), in_maps, core_ids=list(range(NCORES)))
    out = np.concatenate([res.results[c]["xout"].T for c in range(NCORES)], axis=0)[None]
    return np.ascontiguousarray(out.astype(np.float32))


HL = AH // NCORES
NGT = S // TT


def emit_rope_tables(P, R, pos_ap):
    P.dma("sp", R.posi.t[:], pos_ap.partition_broadcast(64), R.posi.slot, writes=[R.posi.b])
    ang, kk, kki, fx = R.ang, R.kk, R.kki, R.fx
    P.op("dve", lambda e: e.tensor_copy(out=ang.t[:], in_=R.posi.t[:]), reads=[R.posi.b], writes=[ang.b])
    P.op("dve", lambda e: e.tensor_scalar(out=ang.t[:], in0=ang.t[:], scalar1=R.freq.t[:, 0:1], scalar2=None,
                                          op0=ALU.mult), reads=[ang.b, R.freq.b], writes=[ang.b])
    for which, shift in (("sin", 0.0), ("cos", np.pi / 2)):
        dst = R.sin if which == "sin" else R.cos
        P.op("dve", lambda e: e.tensor_scalar(out=kk.t[:], in0=ang.t[:], scalar1=1.0 / TWO_PI, scalar2=0.5,
                                              op0=ALU.mult, op1=ALU.add), reads=[ang.b], writes=[kk.b])
        P.op("dve", lambda e: e.tensor_copy(out=kki.t[:], in_=kk.t[:]), reads=[kk.b], writes=[kki.b])
        P.op("dve", lambda e: e.tensor_copy(out=kk.t[:], in_=kki.t[:]), reads=[kki.b], writes=[kk.b])
        for cw, src in ((CW1, ang), (CW2, dst), (CW3, dst)):
            P.op("dve", lambda e: e.scalar_tensor_tensor(out=dst.t[:], in0=kk.t[:], scalar=-cw, in1=src.t[:],
                                                         op0=ALU.mult, op1=ALU.add), reads=[kk.b, src.b],
                 writes=[dst.b])
        if shift:
            P.op("dve", lambda e: e.tensor_scalar(out=dst.t[:], in0=dst.t[:], scalar1=shift, scalar2=None,
                                                  op0=ALU.add), reads=[dst.b], writes=[dst.b])
        for cmp_, sgn in ((ALU.is_gt, -TWO_PI), (ALU.is_lt, TWO_PI)):
            thr = np.pi if sgn < 0 else -np.pi
            P.op("dve", lambda e: e.tensor_scalar(out=fx.t[:], in0=dst.t[:], scalar1=thr, scalar2=sgn,
                                                  op0=cmp_, op1=ALU.mult), reads=[dst.b], writes=[fx.b])
            P.op("dve", lambda e: e.tensor_tensor(out=dst.t[:], in0=dst.t[:], in1=fx.t[:], op=ALU.add),
                 reads=[dst.b, fx.b], writes=[dst.b])
        P.op("act", lambda e: e.activation(out=dst.t[:], in_=dst.t[:], func=AF.Sin), reads=[dst.b], writes=[dst.b])


def emit_mla_p1(P, R, x3, g3_d, lg_d, freq_d, PT_d, pos_loc, win, L_):
    nxt = R.rot
    tks = []
    for tb_, d_ in ((R.g3, g3_d), (R.lg, lg_d), (R.freq, freq_d), (R.PT, PT_d)):
        P.dma("sp", tb_.t[:], d_, tb_.slot, writes=[tb_.b])
    for tt in range(T // TT):
        t0 = tt * TT
        emit_norm(P, R, x3, t0, R.g3, nxt)
        emit_rope_tables(P, R, pos_loc[0:1, t0:t0 + TT])
        tks.append(P.dma("sp", L_["rt"][0:64, t0:t0 + TT], R.cos.t[:], R.rtst, reads=[R.cos.b]))
        tks.append(P.dma("sp", L_["rt"][64:128, t0:t0 + TT], R.sin.t[:], R.rtst, reads=[R.sin.b]))

        def evac(oc, tb, bank):
            cl = slice(tb * 512, (tb + 1) * 512)
            if oc > 8:
                return
            P.op("act", lambda e: e.activation(out=R.lat_t[:, oc, cl], in_=bank.t[:], func=AF.Copy),
                 reads=[bank.b], writes=[R.lat[oc]])
        emit_proj_fm(P, R, R.xn_t, R.xn, KD, win, 5, TT, R.ps[4:8], evac)
        for grp, dst_t, dst_b, g0, key in ((0, R.cqn_t, R.cqn, 0, "cq"), (1, R.ckvn_t, R.ckvn, 4, "ckv")):
            for tb in range(TT // 512):
                cl = slice(tb * 512, (tb + 1) * 512)
                ssb = R.ps[2 + tb]
                for j in range(4):
                    qq = nxt("qq", R.qq)
                    P.op("act", lambda e: e.activation(out=qq.t[:], in_=R.lat_t[:, grp * 4 + j, cl], func=AF.Square),
                         reads=[R.lat[grp * 4 + j]], writes=[qq.b])
                    P.op("pe", lambda e: e.matmul(ssb.t[:], lhsT=R.ones.t[:], rhs=qq.t[:], start=(j == 0),
                                                  stop=(j == 3)), reads=[qq.b, R.ones.b], writes=[ssb.b])
                qr = nxt("qr", R.qr)
                P.op("act", lambda e: e.activation(out=qr.t[:], in_=ssb.t[:], func=AF.Sqrt, scale=1.0 / 512,
                                                   bias=R.epsb.t[:]), reads=[ssb.b, R.epsb.b], writes=[qr.b])
                P.op("dve", lambda e: e.reciprocal(out=qr.t[:], in_=qr.t[:]), reads=[qr.b], writes=[qr.b])
                for j in range(4):
                    P.op("dve", lambda e: e.scalar_tensor_tensor(
                        out=dst_t[:, j, cl], in0=R.lat_t[:, grp * 4 + j, cl], scalar=R.lg.t[:, g0 + j:g0 + j + 1],
                        in1=qr.t[:], op0=ALU.mult, op1=ALU.mult),
                        reads=[R.lat[grp * 4 + j], R.lg.b, qr.b], writes=[dst_b[j]])
            tks.append(P.dma("sp", L_[key][:, t0:t0 + TT].rearrange("(j p) t -> p j t", p=128), dst_t[:],
                             R.latst[grp], reads=dst_b))
        for tb in range(TT // 512):
            cl = slice(tb * 512, (tb + 1) * 512)
            emit_headnorm_sb(P, R, R.lat_t[0:64, 8, cl], R.lat[8], 64, 11, 1.0, True, cl,
                             L_["kr"][:, t0 + tb * 512:t0 + (tb + 1) * 512], tks)
    return tks


def emit_mla_p2(P, R, lg_d, freq_d, PT_d, g_rt, g_cq, g_ckv, wuq, wkn, wv2, Q_):
    nxt = R.rot
    tks = []
    pid_p = P.nc.gpsimd.partition_id()
    for tb_, d_ in ((R.lg, lg_d), (R.freq, freq_d), (R.PT, PT_d)):
        P.dma("sp", tb_.t[:], d_, tb_.slot, writes=[tb_.b])
    P.dma("pool", R.wv2.t[:], wv2[bass.ds(pid_p, 1)].rearrange("a p k n -> (a p) k n"), R.wv2.slot, writes=[R.wv2.b])
    wuq_l = [wuq[bass.ds(pid_p * HL + i, 1)].rearrange("a p k n -> (a p) k n") for i in range(HL)]
    wkn_l = [wkn[bass.ds(pid_p, 1)].rearrange("a p k n -> (a p) k n")]
    for gt in range(NGT):
        src, c0 = gt // (T // TT), (gt % (T // TT)) * TT
        g0 = gt * TT
        P.dma("sp", R.cqn_t[:], g_cq[src, :, c0:c0 + TT].rearrange("(j p) t -> p j t", p=128), R.latst[0],
              writes=R.cqn)
        P.dma("sp", R.ckvn_t[:], g_ckv[src, :, c0:c0 + TT].rearrange("(j p) t -> p j t", p=128), R.latst[1],
              writes=R.ckvn)
        P.dma("sp", R.cos.t[:], g_rt[src, 0:64, c0:c0 + TT], R.rtst, writes=[R.cos.b])
        P.dma("sp", R.sin.t[:], g_rt[src, 64:128, c0:c0 + TT], R.rtst, writes=[R.sin.b])

        def evac_q(oc, tb, bank):
            hl, isr = oc // 2, oc % 2
            cl = slice(tb * 512, (tb + 1) * 512)
            gc = slice(g0 + tb * 512, g0 + (tb + 1) * 512)
            if not isr:
                emit_headnorm(P, R, bank, 128, 8, ASCALE, False, cl, Q_["qn"][hl, :, gc], tks)
            else:
                emit_headnorm(P, R, bank, 64, 9, ASCALE, True, cl, Q_["qr"][hl, :, gc], tks)
        emit_proj_fm(P, R, R.cqn_t, R.cqn, 4, wuq_l, HL, TT, R.ps[4:8], evac_q)

        def evac_k(oc, tb, bank):
            cl = slice(tb * 512, (tb + 1) * 512)
            gc = slice(g0 + tb * 512, g0 + (tb + 1) * 512)
            emit_headnorm(P, R, bank, 128, 10, 1.0, False, cl, Q_["kn"][oc, :, gc], tks)
        emit_proj_fm(P, R, R.ckvn_t, R.ckvn, 4, wkn_l, 1, TT, R.ps[4:8], evac_k)
        for ts in range(TT // 128):
            bank = nxt("pbank", R.ps[4:8])

            def f(e):
                for k in range(4):
                    e.matmul(bank.t[:, 0:HL * 128], lhsT=R.ckvn_t[:, k, ts * 128:(ts + 1) * 128], rhs=R.wv2.t[:, k, :],
                             start=(k == 0), stop=(k == 3))
            P.op("pe", f, reads=[R.wv2.b] + R.ckvn, writes=[bank.b])
            st = nxt("stg", R.stg)
            P.op("act", lambda e: e.activation(out=st.t[:, 0:HL * 128], in_=bank.t[:, 0:HL * 128], func=AF.Copy),
                 reads=[bank.b], writes=[st.b])
            rows = slice(g0 + ts * 128, g0 + (ts + 1) * 128)
            tks.append(P.dma("sp", Q_["V"][:, rows, :].rearrange("h t v -> t h v"),
                             st.t[:, 0:HL * 128].rearrange("t (h v) -> t h v", h=HL), st.slot, reads=[st.b]))
    return tks


def alloc_mla_p2(P, R):
    R.lg = TB(P.sbuf("lg", [128, 12], F32), "lg", P.slot("sl_lg"))
    R.freq = TB(P.sbuf("freq", [64, 1], F32), "freq", P.slot("sl_freq"))
    R.PT = TB(P.sbuf("PT", [64, 64], F32), "PT", P.slot("sl_PT"))
    R.wgu = [TB(P.sbuf(f"wgu{i}", [128, KD, 256], BF16), f"wgu{i}", P.slot(f"sl_wgu{i}")) for i in range(2)]
    R.wv2 = TB(P.sbuf("wv2", [128, 4, HL * 128], BF16), "wv2", P.slot("sl_wv2"))
    alloc_mla_shared(P, R)


def alloc_mla_shared(P, R):
    R.cqn_t = P.sbuf("cqn", [128, 4, TT], BF16)
    R.cqn = [Buf(f"cqn{j}") for j in range(4)]
    R.ckvn_t = P.sbuf("ckvn", [128, 4, TT], BF16)
    R.ckvn = [Buf(f"ckvn{j}") for j in range(4)]
    R.latst = [P.slot("sl_latst0"), P.slot("sl_latst1")]
    R.posi = TB(P.sbuf("posi", [64, TT], I32), "posi", P.slot("sl_posi"))
    R.ang = TB(P.sbuf("ang", [64, TT], F32), "ang")
    R.kk = TB(P.sbuf("kk", [64, TT], F32), "kk")
    R.kki = TB(P.sbuf("kki", [64, TT], I32), "kki")
    R.fx = TB(P.sbuf("fx", [64, TT], F32), "fx")
    R.cos = TB(P.sbuf("cos", [64, TT], F32), "cos")
    R.sin = TB(P.sbuf("sin", [64, TT], F32), "sin")
    R.qf = [TB(P.sbuf(f"qf{i}", [128, 512], F32), f"qf{i}") for i in range(2)]
    R.qq = [TB(P.sbuf(f"qq{i}", [128, 512], BF16), f"qq{i}") for i in range(2)]
    R.qr = [TB(P.sbuf(f"qr{i}", [128, 512], F32), f"qr{i}") for i in range(2)]
    R.qn2 = [TB(P.sbuf(f"qn2{i}", [128, 512], F32), f"qn2{i}") for i in range(2)]
    R.qt2 = [TB(P.sbuf(f"qt2{i}", [128, 512], F32), f"qt2{i}") for i in range(2)]
    R.stg = [TB(P.sbuf(f"astg{i}", [128, 512], BF16), f"astg{i}", P.slot(f"sl_astg{i}")) for i in range(4)]
    R.epsq = TB(P.sbuf("epsq", [128, 1], F32), "epsq")
    P.op("dve", lambda e: e.memset(R.epsq.t[:], EPS / (ASCALE * ASCALE)), writes=[R.epsq.b])
    R.rtst = P.slot("sl_rtst")


def alloc_mla_p1(P, R):
    R.g3 = TB(P.sbuf("g3", [128, KD], F32), "g3", P.slot("sl_g3"))
    R.lg = TB(P.sbuf("lg", [128, 12], F32), "lg", P.slot("sl_lg"))
    R.freq = TB(P.sbuf("freq", [64, 1], F32), "freq", P.slot("sl_freq"))
    R.PT = TB(P.sbuf("PT", [64, 64], F32), "PT", P.slot("sl_PT"))
    R.lat_t = P.sbuf("lat", [128, 9, TT], F32)
    R.lat = [Buf(f"lat{j}") for j in range(9)]
    alloc_mla_shared(P, R)


def alloc_mla_attn2(P, R):
    R.Kn = [TB(P.sbuf(f"Kn{i}", [128, S], BF16), f"Kn{i}", P.slot(f"sl_Kn{i}")) for i in range(HL)]
    R.Vh = [TB(P.sbuf(f"Vh{i}", [128, S // 128, 128], BF16), f"Vh{i}", P.slot(f"sl_Vh{i}")) for i in range(HL)]
    R.Kr = TB(P.sbuf("Kr", [128, S], BF16), "Kr", P.slot("sl_Kr"))
    R.qnb = [TB(P.sbuf(f"qnb{i}", [128, 512], BF16), f"qnb{i}", P.slot(f"sl_qnb{i}")) for i in range(2)]
    R.qrb = [TB(P.sbuf(f"qrb{i}", [128, 512], BF16), f"qrb{i}", P.slot(f"sl_qrb{i}")) for i in range(2)]
    P.op("dve", lambda e: e.memset(R.Kr.t[64:128, :], 0.0), writes=[R.Kr.b])
    for q_ in R.qrb:
        P.op("dve", lambda e: e.memset(q_.t[64:128, :], 0.0), writes=[q_.b])
    R.m01 = TB(P.sbuf("m01", [128, 4, 512], BF16), "m01", P.slot("sl_m01"))
    R.PTt = [TB(P.sbuf(f"PTt{i}", [128, 512], BF16), f"PTt{i}") for i in range(8)]
    R.rd = TB(P.sbuf("rd", [128, 512], F32), "rd")
    R.ost = [TB(P.sbuf(f"ost{i}", [128, 512], BF16), f"ost{i}", P.slot(f"sl_ost{i}")) for i in range(2)]
    R.padd = [TB(P.sbuf(f"padd{i}", [128, 512], BF16), f"padd{i}") for i in range(6)]
    R.oloc_b = [Buf(f"oloc{i}") for i in range(HL)]
    R.acc = [TB(P.sbuf(f"dacc{i}", [128, 512], F32), f"dacc{i}") for i in range(2)]
    R.onesf = TB(P.sbuf("onesf", [128, 128], F32), "onesf")
    P.op("dve", lambda e: e.memset(R.onesf.t[:], 1.0), writes=[R.onesf.b])


def emit_mla_attn2(P, R, Q_, g_kr, m01_d, o_loc, after_head=None):
    nxt = R.rot
    ps = R.ps
    tks = []
    P.dma("sp", R.m01.t[:], m01_d, R.m01.slot, writes=[R.m01.b])
    P.dma("sp", R.Kr.t[0:64, :].rearrange("r (c t) -> r c t", c=NCORES), g_kr.rearrange("c r t -> r c t"), R.Kr.slot,
          writes=[R.Kr.b])
    for hl in range(HL):
        P.dma("sp", R.Kn[hl].t[:], Q_["kn"][hl], R.Kn[hl].slot, writes=[R.Kn[hl].b])
        P.dma("sp", R.Vh[hl].t[:], Q_["V"][hl].rearrange("(i p) v -> p i v", p=128), R.Vh[hl].slot,
              writes=[R.Vh[hl].b])
    it = 0
    for hl in range(HL):
        Kn, Vh = R.Kn[hl], R.Vh[hl]
        for qb in range(S // 512):
            qs_ = slice(qb * 512, (qb + 1) * 512)
            qn, qr = nxt("qnb", R.qnb), nxt("qrb", R.qrb)
            P.dma("sp", qn.t[:], Q_["qn"][hl, :, qs_], qn.slot, writes=[qn.b])
            P.dma("sp", qr.t[0:64, :], Q_["qr"][hl, :, qs_], qr.slot, writes=[qr.b])
            n = 4 * qb + 4
            outp, denp = ps[6], ps[7]
            acc = R.acc[it % 2]
            it += 1
            LOOK = 5
            sbanks = {}

            def issue_s(kt):
                ks = slice(kt * 128, (kt + 1) * 128)
                sb = nxt("sbank6", ps[0:6])
                sbanks[kt] = sb

                def f(e):
                    e.matmul(sb.t[:], lhsT=Kn.t[:, ks], rhs=qn.t[:], start=True, stop=False)
                    e.matmul(sb.t[:], lhsT=R.Kr.t[:, ks], rhs=qr.t[:], start=False, stop=True)
                P.op("pe", f, reads=[Kn.b, R.Kr.b, qn.b, qr.b], writes=[sb.b])
            for kt in range(min(LOOK, n)):
                issue_s(kt)
            pts = {}
            pas = {}

            def issue_pv(j):
                pj = pts[j]
                P.op("pe", lambda e: e.matmul(outp.t[:], lhsT=Vh.t[:, j, :], rhs=pj.t[:], start=(j == 0),
                                              stop=(j == n - 1)), reads=[Vh.b, pj.b], writes=[outp.b])

            def issue_den(j):
                pa = pas.pop(j)
                P.op("pe", lambda e: e.matmul(denp.t[:], lhsT=R.ones.t[:], rhs=pa.t[:], start=(j == 3),
                                              stop=(j == n - 1)), reads=[pa.b, R.ones.b], writes=[denp.b])
            DP, DD = 1, 3
            for kt in range(n + DD):
                if kt < n:
                    sb = sbanks.pop(kt)
                    pt = nxt("PTt", R.PTt)
                    pts[kt] = pt
                    P.op("act", lambda e: e.activation(out=pt.t[:], in_=sb.t[:], func=AF.Exp), reads=[sb.b],
                         writes=[pt.b])
                    m = kt - 4 * qb
                    if m >= 0:
                        P.op("dve", lambda e: e.tensor_tensor(out=pt.t[:], in0=pt.t[:], in1=R.m01.t[:, m, :],
                                                              op=ALU.mult), reads=[pt.b, R.m01.b], writes=[pt.b])
                    if kt % 2 == 1:
                        pa = nxt("padd", R.padd)
                        P.op("dve", lambda e: e.tensor_tensor(out=pa.t[:], in0=pts[kt - 1].t[:], in1=pt.t[:],
                                                              op=ALU.add), reads=[pts[kt - 1].b, pt.b], writes=[pa.b])
                        if kt % 4 == 1:
                            pa_prev = pa
                        else:
                            P.op("dve", lambda e: e.tensor_tensor(out=pa.t[:], in0=pa.t[:], in1=pa_prev.t[:],
                                                                  op=ALU.add), reads=[pa.b, pa_prev.b], writes=[pa.b])
                            if kt == 3:
                                P.op("pool", lambda e: e.tensor_copy(out=acc.t[:], in_=pa.t[:]), reads=[pa.b],
                                     writes=[acc.b])
                            else:
                                P.op("pool", lambda e: e.tensor_tensor(out=acc.t[:], in0=acc.t[:], in1=pa.t[:],
                                                                       op=ALU.add), reads=[acc.b, pa.b], writes=[acc.b])
                    if kt + LOOK < n:
                        issue_s(kt + LOOK)
                if 0 <= kt - DP < n:
                    issue_pv(kt - DP)
            P.op("pe", lambda e: e.matmul(denp.t[:], lhsT=R.onesf.t[:], rhs=acc.t[:], start=True, stop=True),
                 reads=[R.onesf.b, acc.b], writes=[denp.b])
            P.op("dve", lambda e: e.reciprocal(out=R.rd.t[:], in_=denp.t[:]), reads=[denp.b], writes=[R.rd.b])
            ost = nxt("ost", R.ost)
            P.op("dve", lambda e: e.tensor_tensor(out=ost.t[:], in0=outp.t[:], in1=R.rd.t[:], op=ALU.mult),
                 reads=[outp.b, R.rd.b], writes=[ost.b])
            dest, lc = qb // (T // 512), (qb % (T // 512)) * 512
            tks.append(P.dma("sp", o_loc[hl][dest, :, lc:lc + 512], ost.t[:], ost.slot, reads=[ost.b],
                             writes=[R.oloc_b[hl]]))
        if after_head is not None:
            after_head(hl)
    return tks


def emit_proj_resid_v2(P, R, load_src, wtiles, xin, xout, scale=1.0):
    nxt = R.rot
    tks = []
    for tt in range(T // TT):
        t0 = tt * TT
        for a, ap in enumerate(load_src(t0)):
            P.dma("sp", R.xn_t[:, a:a + 1, :], ap, R.xr[0].slot, writes=R.xn[a:a + 1])

        def evac(oc, tb, bank):
            cols = slice(t0 + tb * 512, t0 + (tb + 1) * 512)
            xr = nxt("xr", R.xr)
            P.dma("sp", xr.t[:], xin[oc * 128:(oc + 1) * 128, cols], xr.slot, writes=[xr.b])
            ob = nxt("osb", R.osb)
            P.op("dve", lambda e: e.scalar_tensor_tensor(out=ob.t[:], in0=bank.t[:], scalar=scale, in1=xr.t[:],
                                                         op0=ALU.mult, op1=ALU.add),
                 reads=[bank.b, xr.b], writes=[ob.b])
            tks.append(P.dma("sp", xout[oc * 128:(oc + 1) * 128, cols], ob.t[:], ob.slot, reads=[ob.b]))
        emit_proj_fm(P, R, R.xn_t, R.xn, KD, wtiles, 8, TT, R.ps[0:8], evac)
    return tks
```
